# Optimizing a Trainium2 kernel written in Bass

```python
import math
import jax, jax.numpy as jnp
from jax import lax
import numpy as np

D_MODEL = 2048
BATCH = 4
SEQ = 2048
DEPTH = 1
DEC_BATCH = 32
DEC_SEQ = 16
PAST_LEN = 2048

CHUNK = 64
Q_BLOCK = 128
D_MIX = D_MODEL
D_POOL = D_MIX // 2
POOL_WINDOWS = (2, 4, 8, 16)
N_POOL_GROUPS = len(POOL_WINDOWS)
POOL_GROUP = D_POOL // N_POOL_GROUPS
POOL_HIST = max(POOL_WINDOWS) - 1
N_HEADS = 8
D_NOPE = 128
D_ROPE = 64
D_V = 128
D_MLA = N_HEADS * D_V
Q_RANK = D_MODEL // 4
KV_RANK = D_MODEL // 8
D_PLE = 256
ROPE_THETA = 10000.0
EPS = 1e-6
ATTN_SCALE = (D_NOPE + D_ROPE) ** -0.5
NEG_INF = -1e30
D_IN = 2 * D_POOL + Q_RANK + KV_RANK + D_ROPE + D_MLA
SPLIT_IDX = (D_POOL, 2 * D_POOL, 2 * D_POOL + Q_RANK, 2 * D_POOL + Q_RANK + KV_RANK,
             2 * D_POOL + Q_RANK + KV_RANK + D_ROPE)

kernel_name = 'hymba_pool_mla_streaming_step'


def rms_norm(x, g):
    xf = x.astype(jnp.float32)
    y = xf * lax.rsqrt(jnp.mean(xf * xf, axis=-1, keepdims=True) + EPS)
    return (y * g.astype(jnp.float32)).astype(x.dtype)


def rope_angles(pos):
    inv = ROPE_THETA ** (-(jnp.arange(0, D_ROPE, 2, dtype=jnp.float32) / D_ROPE))
    ang = pos.astype(jnp.float32)[:, None] * inv[None, :]
    return jnp.cos(ang), jnp.sin(ang)


def apply_rope(x, cos, sin):
    xf = x.astype(jnp.float32)
    x1, x2 = xf[..., :D_ROPE // 2], xf[..., D_ROPE // 2:]
    return jnp.concatenate([x1 * cos - x2 * sin, x2 * cos + x1 * sin], axis=-1).astype(x.dtype)


def multiscale_pool(u_hist, u, pos0, w_pool, pool_scale):
    B, T, C = u.shape
    ucat = jnp.concatenate([u_hist, u], axis=1).astype(jnp.float32)
    cs = jnp.concatenate([jnp.zeros((B, 1, C), jnp.float32), jnp.cumsum(ucat, axis=1)], axis=1)
    end = cs[:, POOL_HIST + 1:]
    pos = pos0 + jnp.arange(T)
    means = []
    for gi, w in enumerate(POOL_WINDOWS):
        sl = slice(gi * POOL_GROUP, (gi + 1) * POOL_GROUP)
        start = cs[:, POOL_HIST + 1 - w:POOL_HIST + 1 - w + T, sl]
        cnt = jnp.minimum(pos + 1, w).astype(jnp.float32)[None, :, None]
        means.append((end[..., sl] - start) / cnt)
    d = (jnp.concatenate(means, axis=-1) - u.astype(jnp.float32)).astype(u.dtype)
    d = d.reshape(B, T, N_POOL_GROUPS, POOL_GROUP)
    out = jnp.einsum('btgc,gcd->btgd', d, w_pool).reshape(B, T, C)
    return out * pool_scale


def chunk_causal_attention(q_nope, q_rope, k_nope, k_rope, v, q_pos, k_pos):
    B, T, H, _ = q_nope.shape
    blk = Q_BLOCK if T % Q_BLOCK == 0 else T
    nb = T // blk
    k_chunk = k_pos // CHUNK

    def to_blocks(a):
        return jnp.moveaxis(a.reshape((B, nb, blk) + a.shape[2:]), 1, 0)

    def one_block(args):
        qn, qr, qp = args
        s = (jnp.einsum('bqhd,bkhd->bhqk', qn, k_nope).astype(jnp.float32)
             + jnp.einsum('bqhr,bkr->bhqk', qr, k_rope).astype(jnp.float32)) * ATTN_SCALE
        mask = k_chunk[None, :] <= (qp // CHUNK)[:, None]
        s = jnp.where(mask[None, None], s, NEG_INF)
        pr = jax.nn.softmax(s, axis=-1).astype(v.dtype)
        return jnp.einsum('bhqk,bkhd->bqhd', pr, v)

    out = lax.map(one_block, (to_blocks(q_nope), to_blocks(q_rope), q_pos.reshape(nb, blk)))
    return jnp.moveaxis(out, 0, 1).reshape(B, T, H, D_V)


def hybrid_layer(x, p, pool_hist, ckv_hist, krope_hist, norm_g, w_in, q_norm_g, w_uq, kv_norm_g,
                 w_ukv, q_nope_g, q_rope_g, k_nope_g, k_rope_g, w_pool, pool_scale, w_out,
                 ple_norm_g, w_ple_gate, b_ple_gate, w_ple):
    B, T, _ = x.shape
    pos0 = ckv_hist.shape[1]
    xn = rms_norm(x, norm_g)
    u, g_pool, c_q, c_kv, k_r, g_mla = jnp.split(xn @ w_in, SPLIT_IDX, axis=-1)
    pool_out = multiscale_pool(pool_hist, u, pos0, w_pool, pool_scale) * jax.nn.silu(g_pool)
    new_pool = jnp.concatenate([pool_hist, u], axis=1)[:, -POOL_HIST:]
    q_pos = pos0 + jnp.arange(T)
    cos, sin = rope_angles(q_pos)
    q = (rms_norm(c_q, q_norm_g) @ w_uq).reshape(B, T, N_HEADS, D_NOPE + D_ROPE)
    q_nope = rms_norm(q[..., :D_NOPE], q_nope_g)
    q_rope = apply_rope(rms_norm(q[..., D_NOPE:], q_rope_g), cos[:, None], sin[:, None])
    ckv_new = rms_norm(c_kv, kv_norm_g)
    krope_new = apply_rope(rms_norm(k_r, k_rope_g), cos, sin)
    ckv_all = jnp.concatenate([ckv_hist, ckv_new], axis=1)
    krope_all = jnp.concatenate([krope_hist, krope_new], axis=1)
    S = ckv_all.shape[1]
    kv = (ckv_all @ w_ukv).reshape(B, S, N_HEADS, D_NOPE + D_V)
    k_nope = rms_norm(kv[..., :D_NOPE], k_nope_g)
    v = kv[..., D_NOPE:]
    attn = chunk_causal_attention(q_nope, q_rope, k_nope, krope_all, v, q_pos, jnp.arange(S))
    mla_out = attn.reshape(B, T, D_MLA) * jax.nn.silu(g_mla)
    h = x + jnp.concatenate([pool_out, mla_out], axis=-1) @ w_out
    gate = jax.nn.sigmoid(rms_norm(h, ple_norm_g) @ w_ple_gate + b_ple_gate)
    y = h + gate * (p @ w_ple)
    return y, ckv_new, krope_new, new_pool


def setup_inputs(seed: int = 0) -> dict:
    key = jax.random.key(seed)
    ks = jax.random.split(key, 32)
    f32 = jnp.float32

    def nrm(k, shape, scale=1.0):
        return jax.random.normal(k, shape, f32) * scale

    def gain(k, n):
        return 1.0 + 0.1 * jax.random.normal(k, (DEPTH, n), f32)

    return {
        'x_prompt': nrm(ks[0], (BATCH, SEQ, D_MODEL)),
        'x_sample': nrm(ks[1], (DEC_BATCH, DEC_SEQ, D_MODEL)),
        'cache_ckv': nrm(ks[2], (DEPTH, DEC_BATCH, PAST_LEN, KV_RANK)),
        'cache_krope': nrm(ks[3], (DEPTH, DEC_BATCH, PAST_LEN, D_ROPE)),
        'state_pool': nrm(ks[4], (DEPTH, DEC_BATCH, POOL_HIST, D_POOL)),
        'p_prompt': nrm(ks[5], (DEPTH, BATCH, SEQ, D_PLE)),
        'p_sample': nrm(ks[6], (DEPTH, DEC_BATCH, DEC_SEQ, D_PLE)),
        'norm_g': gain(ks[7], D_MODEL),
        'w_in': nrm(ks[8], (DEPTH, D_MODEL, D_IN), D_MODEL ** -0.5),
        'q_norm_g': gain(ks[9], Q_RANK),
        'w_uq': nrm(ks[10], (DEPTH, Q_RANK, N_HEADS * (D_NOPE + D_ROPE)), Q_RANK ** -0.5),
        'kv_norm_g': gain(ks[11], KV_RANK),
        'w_ukv': nrm(ks[12], (DEPTH, KV_RANK, N_HEADS * (D_NOPE + D_V)), KV_RANK ** -0.5),
        'q_nope_g': gain(ks[13], D_NOPE),
        'q_rope_g': gain(ks[14], D_ROPE),
        'k_nope_g': gain(ks[15], D_NOPE),
        'k_rope_g': gain(ks[16], D_ROPE),
        'w_pool': nrm(ks[17], (DEPTH, N_POOL_GROUPS, POOL_GROUP, POOL_GROUP), POOL_GROUP ** -0.5),
        'pool_scale': gain(ks[18], D_POOL),
        'w_out': nrm(ks[19], (DEPTH, D_MIX, D_MODEL), D_MIX ** -0.5),
        'ple_norm_g': gain(ks[20], D_MODEL),
        'w_ple_gate': nrm(ks[21], (DEPTH, D_MODEL, D_MODEL), D_MODEL ** -0.5),
        'b_ple_gate': nrm(ks[22], (DEPTH, D_MODEL), 0.01),
        'w_ple': nrm(ks[23], (DEPTH, D_PLE, D_MODEL), D_PLE ** -0.5),
    }


def reference(x_prompt, x_sample, cache_ckv, cache_krope, state_pool, p_prompt, p_sample,
              norm_g, w_in, q_norm_g, w_uq, kv_norm_g, w_ukv, q_nope_g, q_rope_g, k_nope_g,
              k_rope_g, w_pool, pool_scale, w_out, ple_norm_g, w_ple_gate, b_ple_gate, w_ple):
    yp, ys = x_prompt, x_sample
    B, dt = x_prompt.shape[0], x_prompt.dtype
    ckv_p, kr_p, pool_p, ckv_s, kr_s, pool_s = [], [], [], [], [], []
    for i in range(DEPTH):
        w = (norm_g[i], w_in[i], q_norm_g[i], w_uq[i], kv_norm_g[i], w_ukv[i], q_nope_g[i],
             q_rope_g[i], k_nope_g[i], k_rope_g[i], w_pool[i], pool_scale[i], w_out[i],
             ple_norm_g[i], w_ple_gate[i], b_ple_gate[i], w_ple[i])
        yp, c1, k1, s1 = hybrid_layer(
            yp, p_prompt[i], jnp.zeros((B, POOL_HIST, D_POOL), dt),
            jnp.zeros((B, 0, KV_RANK), dt), jnp.zeros((B, 0, D_ROPE), dt), *w)
        ys, c2, k2, s2 = hybrid_layer(ys, p_sample[i], state_pool[i], cache_ckv[i], cache_krope[i], *w)
        ckv_p.append(c1); kr_p.append(k1); pool_p.append(s1)
        ckv_s.append(c2); kr_s.append(k2); pool_s.append(s2)
    return (yp, ys, jnp.stack(ckv_p), jnp.stack(kr_p), jnp.stack(pool_p),
            jnp.stack(ckv_s), jnp.stack(kr_s), jnp.stack(pool_s))
```

```python
import numpy as np
from contextlib import ExitStack
import concourse.bass as bass
import concourse.mybir as mybir
from concourse.bass_utils import run_bass_kernel_spmd

F32 = mybir.dt.float32
BF16 = mybir.dt.bfloat16
AF = mybir.ActivationFunctionType
ALU = mybir.AluOpType
AX = mybir.AxisListType

D_MODEL = 2048
D_POOL = 1024
D_IN = 3904
Q_RANK = 512
KV_RANK = 256
D_ROPE = 64
D_NOPE = 128
N_HEADS = 8
D_PLE = 256
EPS = 1e-6
ATTN_SCALE = 192.0 ** -0.5
NEG = -30000.0
NOWN = 1024
NSMP = 64
NTOK = NOWN + NSMP
XCOLS = 16 + NTOK
UCOLS = 16 + NOWN + 128


class Instr:
    __slots__ = ("eng", "fn", "is_dma", "deps", "milestone", "val", "dsem", "dval", "uid", "cost", "attach", "tag", "waits")


class Sched:
    COMPUTE = ("pe", "act", "dve", "pool")
    QUEUES = ("sp", "act", "pool")

    def __init__(self, nc, es, ndma=8):
        self.nc = nc
        self.streams = {e: [] for e in ("pe", "act", "dve", "pool", "sp")}
        self.lastw = {}
        self.readers = {}
        self.sem = {e: es.enter_context(nc.semaphore("c_" + e)) for e in self.COMPUTE}
        self.dsems = {q: [es.enter_context(nc.semaphore("d_%s%d" % (q, i))) for i in range(ndma)]
                      for q in self.QUEUES}
        self.ndma = ndma
        self.dhist = {q: [] for q in self.QUEUES}
        self.uid = 0
        self.last = {e: None for e in self.streams}
        self.all_dma = []

    def add(self, eng, fn, reads=(), writes=(), dma=False, cost=100.0, attach=False):
        I = Instr()
        I.cost = cost
        I.attach = attach
        I.tag = getattr(self, "tag", None)
        I.eng = eng
        I.fn = fn
        I.is_dma = dma
        I.milestone = False
        I.val = 0
        I.uid = self.uid
        self.uid += 1
        deps = {}
        for b in reads:
            lw = self.lastw.get(b)
            if lw is not None:
                deps[lw.uid] = lw
        for b in writes:
            lw = self.lastw.get(b)
            if lw is not None:
                deps[lw.uid] = lw
            for r in self.readers.get(b, {}).values():
                deps[r.uid] = r
        if dma:
            h = self.dhist[eng]
            n = len(h)
            if n >= self.ndma:
                d = h[n - self.ndma]
                deps[d.uid] = d
            I.dsem = self.dsems[eng][n % self.ndma]
            I.dval = 16 * (n // self.ndma + 1)
            h.append(I)
            self.all_dma.append(I)
        if eng == "pe" and not dma:
            deps = {k: d for k, d in deps.items() if d.is_dma or d.eng != "pe"}
        I.deps = list(deps.values())
        for b in reads:
            rd = self.readers.setdefault(b, {})
            rd[("d", I.uid) if dma else eng] = I
        for b in writes:
            self.lastw[b] = I
            self.readers[b] = {}
        self.streams[eng].append(I)
        if fn is not None:
            self.last[eng] = I
        return I

    def barrier(self):
        prev = [self.last[e] for e in self.streams if self.last[e] is not None]
        prev_dma = list(self.all_dma)
        self.all_dma = []
        for e in self.streams:
            I = self.add(e, None, (), ())
            deps = {d.uid: d for d in prev if not d.is_dma}
            for d in prev_dma:
                deps[d.uid] = d
            I.deps = [d for d in deps.values() if d is not I]
        self.lastw = {}
        self.readers = {}

    def emit(self, block):
        for st in self.streams.values():
            for I in st:
                for d in I.deps:
                    if not d.is_dma:
                        d.milestone = True
        for e in self.COMPUTE:
            c = 0
            for I in self.streams[e]:
                if I.is_dma:
                    continue
                if I.milestone:
                    c += 1
                I.val = c
        sched = self
        know = {e: {} for e in self.streams}
        kdone = {}
        for I in sorted((I for st in self.streams.values() for I in st), key=lambda I: I.uid):
            ke = know[I.eng]
            need = {}
            for d in I.deps:
                if d.is_dma:
                    sm, v = d.dsem, d.dval
                else:
                    sm, v = self.sem[d.eng], d.val
                if ke.get(sm.name, 0) < v:
                    need[sm.name] = (sm, v)
                    for k2, v2 in kdone[d.uid].items():
                        if ke.get(k2, 0) < v2:
                            ke[k2] = v2
                    ke[sm.name] = v
            I.waits = list(need.values())
            if I.is_dma:
                kd = dict(ke)
                kd[I.dsem.name] = I.dval
                kdone[I.uid] = kd
            elif I.milestone:
                kd = dict(ke)
                kd[self.sem[I.eng].name] = I.val
                kdone[I.uid] = kd

        ms_list = {e: [I for I in self.streams[e] if (not I.is_dma) and I.milestone] for e in self.COMPUTE}
        semeng = {self.sem[e].name: e for e in self.COMPUTE}
        used = set()
        for st in self.streams.values():
            for I in st:
                for sm, v in I.waits:
                    if sm.name in semeng:
                        used.add(ms_list[semeng[sm.name]][v - 1].uid)
        for e in self.COMPUTE:
            c = 0
            for I in self.streams[e]:
                if I.is_dma:
                    continue
                I.milestone = I.uid in used
                if I.milestone:
                    c += 1
                I.val = c
        for st in self.streams.values():
            for I in st:
                new = []
                for sm, v in I.waits:
                    if sm.name in semeng:
                        X = ms_list[semeng[sm.name]][v - 1]
                        assert X.milestone and X.val >= 1 and X.uid < I.uid
                        new.append((sm, X.val))
                    else:
                        new.append((sm, v))
                I.waits = new

        def run(e, h):
            for I in sched.streams[e]:
                need = list(I.waits)
                carried = need.pop() if (need and I.attach and I.fn is not None) else None
                for s, v in need:
                    h.wait_ge(s, v)
                if I.fn is None:
                    if I.milestone:
                        h.nop().then_inc(sched.sem[e], 1)
                    continue
                ins = I.fn(h)
                if carried is not None:
                    ins._wait_ge(carried[0], carried[1])
                if I.is_dma:
                    ins.then_inc(I.dsem, 16)
                elif I.milestone:
                    ins.then_inc(sched.sem[e], 1)
            if e == "sp":
                for q in sched.QUEUES:
                    hq = sched.dhist[q]
                    for i in range(sched.ndma):
                        cnt = len([1 for n in range(len(hq)) if n % sched.ndma == i])
                        if cnt:
                            h.wait_ge(sched.dsems[q][i], 16 * cnt)

        @block.sync
        def _(h):
            run("sp", h)

        @block.scalar
        def _(h):
            run("act", h)

        @block.vector
        def _(h):
            run("dve", h)

        @block.gpsimd
        def _(h):
            run("pool", h)

        @block.tensor
        def _(h):
            run("pe", h)


PE_GHZ = 1.6


def _fsz(ap):
    n = 1
    for d in ap.shape[1:]:
        n *= d
    return float(n)


def _ecost(eng, ap):
    n = _fsz(ap)
    if eng == "pool":
        return 130.0 + 2.2 * n
    return 70.0 + 1.05 * n


def simulate(S, hop=350.0, selfhop=120.0):
    order = sorted((I for st in S.streams.values() for I in st), key=lambda I: I.uid)
    free = {e: 0.0 for e in S.streams}
    end = {}
    marks = []
    tags = {}
    for I in order:
        t = free[I.eng]
        for d in I.deps:
            de = end[d.uid]
            lat = selfhop if (d.eng == I.eng and not d.is_dma) else hop
            t = max(t, de + lat)
        if I.fn is None:
            free[I.eng] = t
            end[I.uid] = t
            if I.eng == "pe":
                marks.append(t)
            continue
        if I.is_dma:
            free[I.eng] = t + 60.0
            end[I.uid] = t + I.cost
        else:
            free[I.eng] = t + I.cost
            end[I.uid] = t + I.cost
        if I.tag is not None:
            a, b = tags.get(I.tag, (1e18, 0.0))
            tags[I.tag] = (min(a, t), max(b, end[I.uid]))
    simulate.tags = tags
    return marks, max(end.values())


def build_program(stage=99):
    nc = bass.Bass("TRN2", target_bir_lowering=False)

    def din(name, shape):
        return nc.dram_tensor(name, list(shape), F32, kind="ExternalInput").ap()

    def dout(name, shape):
        return nc.dram_tensor(name, list(shape), F32, kind="ExternalOutput").ap()

    xo = din("xo", [16 + NOWN, D_MODEL])
    xoth = din("xoth", [NOWN, D_MODEL])
    xsm = din("xsm", [NSMP, D_MODEL])
    cck = din("cck", [4, 2048, KV_RANK])
    ckr = din("ckr", [4, 2048, D_ROPE])
    spl = din("spl", [64, D_POOL])
    pp = din("pp", [NOWN, D_PLE])
    psm = din("psm", [NSMP, D_PLE])
    w_in = din("w_in", [D_MODEL, D_IN])
    w_uq = din("w_uq", [Q_RANK, 1536])
    w_ukv = din("w_ukv", [KV_RANK, 2048])
    w_pool = din("w_pool", [4, 256, 256])
    w_out = din("w_out", [D_MODEL, D_MODEL])
    w_gate = din("w_gate", [D_MODEL, D_MODEL])
    w_ple = din("w_ple", [D_PLE, D_MODEL])
    norm_g = din("norm_g", [D_MODEL])
    q_norm_g = din("q_norm_g", [Q_RANK])
    kv_norm_g = din("kv_norm_g", [KV_RANK])
    q_nope_g = din("q_nope_g", [D_NOPE])
    q_rope_g = din("q_rope_g", [D_ROPE])
    k_nope_g = din("k_nope_g", [D_NOPE])
    k_rope_g = din("k_rope_g", [D_ROPE])
    pool_scale = din("pool_scale", [D_POOL])
    ple_norm_g = din("ple_norm_g", [D_MODEL])
    b_gate = din("b_gate", [D_MODEL])
    cs = din("cs", [17, 128, 64])
    ob = din("ob", [128, 1])
    rc16 = din("rc16", [128, 64])

    y_o = dout("y_o", [NOWN, D_MODEL])
    y_s = dout("y_s", [NSMP, D_MODEL])
    ckv_o = dout("ckv_o", [NOWN, KV_RANK])
    kr_o = dout("kr_o", [NOWN, D_ROPE])
    pool_o = dout("pool_o", [16, D_POOL])
    ckv_s = dout("ckv_s", [NSMP, KV_RANK])
    kr_s = dout("kr_s", [NSMP, D_ROPE])
    pool_s = dout("pool_s", [NSMP, D_POOL])

    top = ExitStack()
    with top:
        S = Sched(nc, top)
        block = top.enter_context(nc.Block())

        def sb(es, name, shape, dt):
            return es.enter_context(nc.sbuf_tensor(name, list(shape), dt))

        def ps(es, name, shape, dt):
            return es.enter_context(nc.psum_tensor(name, list(shape), dt))

        def dma(q, out, in_, r, w, **kw):
            return S.add(q, lambda h: h.dma_start(out=out, in_=in_, **kw), r, w, dma=True,
                         cost=2000.0 + _fsz(out) * out.shape[0] * 4 / 150.0)

        def act(out, in_, func, r, w, **kw):
            return S.add("act", lambda h: h.activation(out=out, in_=in_, func=func, **kw), r, w,
                         cost=220.0 + _fsz(in_) * 0.95 + (100.0 if "accum_out" in kw else 0.0),
                         attach=("accum_out" not in kw))

        def mm(out, lhsT, rhs, start, stop, r, w):
            return S.add("pe", lambda h: h.matmul(out, lhsT=lhsT, rhs=rhs, start=start, stop=stop,
                                                  skip_group_check=True), r, w,
                         cost=max(64.0, _fsz(rhs)) / PE_GHZ + 3.0, attach=True)

        def tr(out, in_, ident, r, w):
            return S.add("pe", lambda h: h.transpose(out=out, in_=in_, identity=ident), r, w,
                         cost=max(64.0, in_.shape[0]) / PE_GHZ + 3.0, attach=True)

        def tt(eng, out, in0, in1, op, r, w):
            return S.add(eng, lambda h: h.tensor_tensor(out=out, in0=in0, in1=in1, op=op), r, w,
                         cost=_ecost(eng, out), attach=True)

        def tsc(eng, out, in0, s1, s2, op0, op1, r, w):
            if s2 is None:
                return S.add(eng, lambda h: h.tensor_scalar(out=out, in0=in0, scalar1=s1, scalar2=None,
                                                            op0=op0), r, w, cost=_ecost(eng, out), attach=True)
            return S.add(eng, lambda h: h.tensor_scalar(out=out, in0=in0, scalar1=s1, scalar2=s2,
                                                        op0=op0, op1=op1), r, w, cost=_ecost(eng, out), attach=True)

        def stt(eng, out, in0, scalar, in1, op0, op1, r, w):
            return S.add(eng, lambda h: h.scalar_tensor_tensor(out=out, in0=in0, scalar=scalar, in1=in1,
                                                               op0=op0, op1=op1), r, w, cost=_ecost(eng, out), attach=True)

        def cp(eng, out, in_, r, w):
            return S.add(eng, lambda h: h.tensor_copy(out=out, in_=in_), r, w, cost=_ecost(eng, out), attach=True)

        def red(eng, out, in_, r, w):
            return S.add(eng, lambda h: h.tensor_reduce(out=out, in_=in_, axis=AX.X, op=ALU.add), r, w,
                         cost=_ecost(eng, in_), attach=True)

        def recip(out, in_, r, w):
            return S.add("dve", lambda h: h.reciprocal(out=out, in_=in_), r, w, attach=True)

        def memset(eng, ap, val, w):
            return S.add(eng, lambda h: h.memset(ap, val), (), w, cost=_ecost(eng, ap))

        def bc(ap, shape, axis):
            return ap.unsqueeze(axis).to_broadcast(list(shape))

        IDB = sb(top, "IDB", [128, 128], BF16)
        IDF = sb(top, "IDF", [128, 128], F32)
        EPSB = sb(top, "EPSB", [128, 1], F32)
        ZEROB = sb(top, "ZEROB", [128, 1], F32)
        OB = sb(top, "OB", [128, 1], F32)
        RC16 = sb(top, "RC16", [128, 64], F32)
        CS = sb(top, "CS", [128, 17, 64], F32)
        GFM = sb(top, "GFM", [128, 16], F32)
        PGFM = sb(top, "PGFM", [128, 16], F32)
        PSFM = sb(top, "PSFM", [128, 8], F32)
        GKFM = sb(top, "GKFM", [128, 1], F32)
        GQ = sb(top, "GQ", [128, Q_RANK], F32)
        GKV = sb(top, "GKV", [128, KV_RANK], F32)
        GQN = sb(top, "GQN", [128, D_NOPE], F32)
        GQR = sb(top, "GQR", [128, D_ROPE], F32)
        GKR = sb(top, "GKR", [128, D_ROPE], F32)
        MIXRAW = sb(top, "MIXRAW", [128, 8 * NTOK], F32)
        MIX = MIXRAW[:, :].bitcast(BF16).rearrange("p (k t) -> p k t", k=16)
        STAT = sb(top, "STAT", [128, 18, 40], F32)

        memset("dve", IDF[:], 0.0, ["IDF"])
        S.add("pool", lambda h: h.affine_select(out=IDF[:], in_=IDF[:], pattern=[[-1, 128]],
                                                compare_op=ALU.not_equal, fill=1.0, base=0,
                                                channel_multiplier=1), ["IDF"], ["IDF"])
        cp("dve", IDB[:], IDF[:], ["IDF"], ["IDB"])
        memset("dve", EPSB[:], EPS, ["EPSB"])
        memset("dve", ZEROB[:], 0.0, ["ZEROB"])
        dma("sp", OB[:], ob[:, :], [], ["OB"])
        dma("sp", RC16[:], rc16[:, :], [], ["RC16"])
        dma("sp", CS[:], cs.rearrange("b p c -> p b c"), [], ["CS"])
        dma("sp", GFM[:], norm_g.rearrange("(k p) -> p k", p=128), [], ["GFM"], allow_slow_non_contiguous=True)
        dma("sp", PGFM[:], ple_norm_g.rearrange("(k p) -> p k", p=128), [], ["PGFM"],
            allow_slow_non_contiguous=True)
        dma("sp", PSFM[:], pool_scale.rearrange("(k p) -> p k", p=128), [], ["PSFM"],
            allow_slow_non_contiguous=True)
        dma("sp", GKFM[:], k_nope_g.rearrange("(p o) -> p o", o=1), [], ["GKFM"])
        dma("sp", GQ[:], q_norm_g.partition_broadcast(128), [], ["GQ"])
        dma("sp", GKV[:], kv_norm_g.partition_broadcast(128), [], ["GKV"])
        dma("sp", GQN[:], q_nope_g.partition_broadcast(128), [], ["GQN"])
        dma("sp", GQR[:], q_rope_g.partition_broadcast(128), [], ["GQR"])
        dma("sp", GKR[:], k_rope_g.partition_broadcast(128), [], ["GKR"])
        CONSTS = ["IDB", "IDF", "EPSB", "ZEROB", "OB", "RC16", "CS", "GFM", "PGFM", "PSFM", "GKFM", "GQ",
                  "GKV", "GQN", "GQR", "GKR"]
        S.barrier()

        def rstd_inplace(ap, T, n, key):
            act(ap, ap, AF.Ln, [key], [key], scale=1.0 / n, bias=EPSB[:T])
            act(ap, ap, AF.Exp, [key], [key], scale=-0.5)

        s1 = ExitStack()
        s1.__enter__()
        QTN = sb(s1, "QTN", [128, 8, NOWN], BF16)
        QTR = sb(s1, "QTR", [128, 8, NOWN], BF16)
        SQN = sb(s1, "SQN", [128, 4, 8, 16], BF16)
        SQR = sb(s1, "SQR", [128, 4, 8, 16], BF16)
        CKT = sb(s1, "CKT", [128, 2, 2048], BF16)
        KRT = sb(s1, "KRT", [128, 2048], BF16)
        SCKT = sb(s1, "SCKT", [128, 2, 64], BF16)
        SKRT = sb(s1, "SKRT", [128, 64], BF16)
        s2 = ExitStack()
        s2.__enter__()
        XN = sb(s2, "XN", [128, 16, XCOLS], BF16)

        def sweepA():
            with ExitStack() as es:
                WA = sb(es, "WA", [128, 16, 832], BF16)
                WUQ = sb(es, "WUQ", [128, 4, 1536], BF16)
                NXT = 4
                XT = [MIXRAW[:, 2048 * i:2048 * (i + 1)] for i in range(NXT)]
                memset("pool", QTR[64:128, :, :], 0.0, ["QTRz"])
                memset("pool", SQR[64:128, :, :, :], 0.0, ["SQRz"])
                XS = [sb(es, "XS%d" % i, [128, 2048], BF16) for i in range(2)]
                XNO = [sb(es, "XNO%d" % i, [128, 16, 128], BF16) for i in range(2)]
                JUNK = sb(es, "JUNK", [128, 512], BF16)
                CQN = sb(es, "CQN", [128, 512], BF16)
                CQT = sb(es, "CQT", [128, 4, 128], BF16)
                QRAW = sb(es, "QRAW", [128, 1536], F32)
                QSQ = sb(es, "QSQ", [128, 1536], F32)
                QF = QSQ[:, 0:1024].rearrange("p (h d) -> p h d", h=8)
                QNB = sb(es, "QNB", [128, 8, 128], BF16)
                QRF = QSQ[:, 1024:1536].rearrange("p (h d) -> p h d", h=8)
                RT = [sb(es, "RT%d" % i, [128, 8, 32], F32) for i in range(4)]
                RK_ = [sb(es, "RK_%d" % i, [128, 32], F32) for i in range(4)]
                QRB = sb(es, "QRB", [128, 8, 64], BF16)
                CKVF = [sb(es, "CKVF%d" % i, [128, 256], F32) for i in range(2)]
                CKVB = sb(es, "CKVB", [128, 256], BF16)
                KRF = sb(es, "KRF", [128, 64], F32)
                KRO = [sb(es, "KRO%d" % i, [128, 64], F32) for i in range(2)]
                KRB = sb(es, "KRB", [128, 128], BF16)
                TP = ps(es, "TP", [128, 2048], BF16)
                PA = [ps(es, "PA%d" % i, [128, 1024], F32) for i in range(2)]
                PQ = ps(es, "PQ", [128, 512], F32)
                TQ = ps(es, "TQ", [128, 1024], BF16)

                for k in range(16):
                    dma("pool", WA[:, k, :], w_in[128 * k:128 * (k + 1), 2048:2880], [], [("WA", k)])
                dma("pool", WUQ[:], w_uq.rearrange("(k p) n -> p k n", p=128), [], ["WUQ"])

                blocks = []
                blocks.append((16, xo[0:16, :], "halo", 0))
                for j in range(8):
                    blocks.append((128, xo[16 + 128 * j:16 + 128 * (j + 1), :], "own", 16 + 128 * j))
                    blocks.append((128, xoth[128 * j:128 * (j + 1), :], "oth", None))
                blocks.append((64, xsm[:, :], "smp", 16 + NOWN))
                NB = len(blocks)
                info = {}
                jo = 0
                jt = 0
                for i, (T, src, kind, xc) in enumerate(blocks):
                    d = {"T": T, "kind": kind, "xc": xc}
                    if kind == "own":
                        d["csi"] = jo
                        d["mcol"] = 128 * jo
                        d["kc"] = 1024 + 128 * jo
                        jo += 1
                    elif kind == "smp":
                        d["csi"] = 8
                        d["mcol"] = NOWN
                    elif kind == "oth":
                        d["csi"] = 9 + jt
                        d["kc"] = 128 * jt
                        jt += 1
                    info[i] = d

                def load(i):
                    T, src, kind, xc = blocks[i]
                    dma("sp", XT[i % NXT][:T], src, [], [("XT", i % NXT)])

                def stageF(i):
                    d = info[i]
                    T, kind, xc = d["T"], d["kind"], d["xc"]
                    p = i % 2
                    xt = XT[i % NXT]
                    xk = ("XT", i % NXT)
                    st = STAT[:T, i, :]
                    act(XS[p][:T], xt[:T], AF.Square, [xk], [("ST", i, 0), ("XS", p, 0), ("XS", p, 1)], accum_out=st[:, 0:1])
                    rstd_inplace(st[:, 0:1], T, D_MODEL, ("ST", i, 0))
                    for hf in range(2):
                        act(XS[p][:T, hf * 1024:(hf + 1) * 1024], xt[:T, hf * 1024:(hf + 1) * 1024], AF.Copy,
                            [xk, ("ST", i, 0)], [("XS", p, hf)], scale=st[:, 0:1])
                    for k in range(16):
                        tr(TP[:, k * 128:k * 128 + T], XS[p][:T, k * 128:(k + 1) * 128], IDB[:T, :T],
                           [("XS", p, k // 8)], [("TP", k // 8)])
                    tpv = TP[:, :].rearrange("p (k t) -> p k t", k=16)[:, :, 0:T]
                    if kind == "oth":
                        xn = XNO[d["csi"] % 2][:, :, 0:T]
                        xkey = ("XNO", d["csi"] % 2)
                    else:
                        xn = XN[:, :, xc:xc + T]
                        xkey = ("XN", i)
                    for hf in range(2):
                        tt("dve", xn[:, hf * 8:(hf + 1) * 8, :], tpv[:, hf * 8:(hf + 1) * 8, :],
                           bc(GFM[:, hf * 8:(hf + 1) * 8], [128, 8, T], 2), ALU.mult, [("TP", hf)], [(xkey, hf)])
                    if kind == "halo":
                        return
                    pa = d["pa"]
                    if kind != "oth":
                        for k in range(16):
                            mm(PA[pa][:T, 0:512], xn[:, k, :], WA[:, k, 0:512], k == 0, k == 15,
                               [(xkey, k // 8), ("WA", k)], [("PA0", pa)])
                    for k in range(16):
                        mm(PA[pa][:T, 512:832], xn[:, k, :], WA[:, k, 512:832], k == 0, k == 15,
                           [(xkey, k // 8), ("WA", k)], [("PA1", pa)])

                def stageG1(i):
                    d = info[i]
                    T, kind = d["T"], d["kind"]
                    if kind == "halo":
                        return
                    pa = d["pa"]
                    pav = PA[pa]
                    p = i % 2
                    st = STAT[:T, i, :]
                    csi = d["csi"]
                    bk1 = ("BK_PA1", pa)
                    act(JUNK[:T, 0:256], pav[:T, 512:768], AF.Square, [("PA1", pa)], [("ST", i, 4), bk1],
                        accum_out=st[:, 2:3])
                    act(JUNK[:T, 0:64], pav[:T, 768:832], AF.Square, [("PA1", pa)], [("ST", i, 5), bk1],
                        accum_out=st[:, 3:4])
                    rstd_inplace(st[:, 2:3], T, KV_RANK, ("ST", i, 4))
                    rstd_inplace(st[:, 3:4], T, D_ROPE, ("ST", i, 5))
                    if kind != "oth":
                        act(JUNK[:T, 0:512], pav[:T, 0:512], AF.Square, [("PA0", pa)], [("ST", i, 1)],
                            accum_out=st[:, 1:2])
                        rstd_inplace(st[:, 1:2], T, Q_RANK, ("ST", i, 1))
                        stt("dve", CQN[:T], pav[:T, 0:512], st[:, 1:2], GQ[:T], ALU.mult, ALU.mult,
                            [("PA0", pa), ("ST", i, 1)], ["CQN"])
                        for k in range(4):
                            tr(TQ[:, k * 128:k * 128 + T], CQN[:T, k * 128:(k + 1) * 128], IDB[:T, :T],
                               ["CQN"], ["TQ"])
                        cp("dve", CQT[:, :, 0:T], TQ[:, 0:512].rearrange("p (k t) -> p k t", k=4)[:, :, 0:T],
                           ["TQ"], ["CQT"])

                def stageG2(i):
                    d = info[i]
                    T, kind = d["T"], d["kind"]
                    if kind == "halo":
                        return
                    pa = d["pa"]
                    pav = PA[pa]
                    p = i % 2
                    st = STAT[:T, i, :]
                    csi = d["csi"]
                    bk1 = ("BK_PA1", pa)
                    if kind != "oth":
                        for n in range(3):
                            for k in range(4):
                                mm(PQ[:T, :], CQT[:, k, 0:T], WUQ[:, k, n * 512:(n + 1) * 512],
                                   k == 0, k == 3, ["CQT", "WUQ"], ["PQ"])
                            act(QRAW[:T, n * 512:(n + 1) * 512], PQ[:T, :], AF.Copy, ["PQ"], [("QRAW", n)])
                        act(QSQ[:T], QRAW[:T], AF.Square, [("QRAW", 0), ("QRAW", 1), ("QRAW", 2)], ["QSQ"])
                    ckf = CKVF[p]
                    stt("dve", ckf[:T], pav[:T, 512:768], st[:, 2:3], GKV[:T], ALU.mult, ALU.mult,
                        [("PA1", pa), ("ST", i, 4)], [("CKVF", p), bk1])
                    stt("dve", KRF[:T], pav[:T, 768:832], st[:, 3:4], GKR[:T], ALU.mult, ALU.mult,
                        [("PA1", pa), ("ST", i, 5)], ["KRF", bk1])
                    if kind == "own":
                        dma("sp", ckv_o[d["mcol"]:d["mcol"] + T, :], ckf[:T], [("CKVF", p)], [])
                    elif kind == "smp":
                        dma("sp", ckv_s[:, :], ckf[:T], [("CKVF", p)], [])
                    cp("pool", CKVB[:T], ckf[:T], [("CKVF", p)], ["CKVB"])
                    for k in range(2):
                        tr(TQ[:, k * 128:k * 128 + T], CKVB[:T, k * 128:(k + 1) * 128], IDB[:T, :T],
                           ["CKVB"], ["TQ"])
                    tqv = TQ[:, 0:256].rearrange("p (k t) -> p k t", k=2)[:, :, 0:T]
                    if kind == "smp":
                        cp("dve", SCKT[:, :, 0:T], tqv, ["TQ"], ["SCKT"])
                    else:
                        kc = d["kc"]
                        cp("dve", CKT[:, :, kc:kc + T], tqv, ["TQ"], [("CKT", i)])
                    kro = KRO[p]
                    c1 = CS[:T, csi, 0:32]
                    s1_ = CS[:T, csi, 32:64]
                    r0, r1, r2, r3 = (RK_[q][:T, :] for q in range(4))
                    tt("pool", r0, KRF[:T, 0:32], c1, ALU.mult, ["KRF"], ["RK0"])
                    tt("pool", r1, KRF[:T, 32:64], s1_, ALU.mult, ["KRF"], ["RK1"])
                    tt("pool", r2, KRF[:T, 32:64], c1, ALU.mult, ["KRF"], ["RK2"])
                    tt("pool", r3, KRF[:T, 0:32], s1_, ALU.mult, ["KRF"], ["RK3"])
                    tt("pool", kro[:T, 0:32], r0, r1, ALU.subtract, ["RK0", "RK1"], [("KRO0", p)])
                    tt("pool", kro[:T, 32:64], r2, r3, ALU.add, ["RK2", "RK3"], [("KRO1", p)])
                    if kind == "own":
                        dma("sp", kr_o[d["mcol"]:d["mcol"] + T, :], kro[:T], [("KRO0", p), ("KRO1", p)], [])
                    elif kind == "smp":
                        dma("sp", kr_s[:, :], kro[:T], [("KRO0", p), ("KRO1", p)], [])
                    cp("pool", KRB[:T, 0:64], kro[:T], [("KRO0", p), ("KRO1", p)], ["KRBa"])
                    cp("pool", KRB[:T, 64:128], kro[:T], [("KRO0", p), ("KRO1", p)], ["KRBb"])
                    tr(TQ[:, 0:T], KRB[:T, :], IDB[:T, :T], ["KRBa", "KRBb"], ["TQ"])
                    if kind == "smp":
                        cp("dve", SKRT[:, 0:T], TQ[:, 0:T], ["TQ"], ["SKRT"])
                    else:
                        cp("dve", KRT[:, kc:kc + T], TQ[:, 0:T], ["TQ"], [("KRT", i)])

                def stageH(i):
                    d = info[i]
                    T, kind = d["T"], d["kind"]
                    if kind in ("halo", "oth"):
                        return
                    st = STAT[:T, i, :]
                    csi, mcol = d["csi"], d["mcol"]
                    qsv = QSQ[:T].rearrange("p (h d) -> p h d", h=8)
                    qv = QRAW[:T].rearrange("p (h d) -> p h d", h=8)
                    qk_ = [("QRAW", 0), ("QRAW", 1), ("QRAW", 2)]
                    red("dve", st[:, 8:16], qsv[:, :, 0:128], ["QSQ"], [("ST", i, 2)])
                    red("dve", st[:, 16:24], qsv[:, :, 128:192], ["QSQ"], [("ST", i, 3)])
                    rstd_inplace(st[:, 8:16], T, D_NOPE, ("ST", i, 2))
                    rstd_inplace(st[:, 16:24], T, D_ROPE, ("ST", i, 3))
                    tt("dve", QF[:T], qv[:, :, 0:128], bc(st[:, 8:16], [T, 8, 128], 2), ALU.mult,
                       qk_ + [("ST", i, 2)], ["QF", "QSQ"])
                    tt("dve", QNB[:T], QF[:T], bc(GQN[:T], [T, 8, 128], 1), ALU.mult, ["QF", "QSQ"], ["QNB"])
                    tt("dve", QRF[:T], qv[:, :, 128:192], bc(st[:, 16:24], [T, 8, 64], 2), ALU.mult,
                       qk_ + [("ST", i, 3)], ["QRF", "QSQ"])
                    tt("dve", QRF[:T], QRF[:T], bc(GQR[:T], [T, 8, 64], 1), ALU.mult, ["QRF"], ["QRF", "QSQ"])
                    cosb = bc(CS[:T, csi, 0:32], [T, 8, 32], 1)
                    sinb = bc(CS[:T, csi, 32:64], [T, 8, 32], 1)
                    x1 = QRF[:T, :, 0:32]
                    x2 = QRF[:T, :, 32:64]
                    tt("dve", RT[0][:T], x1, cosb, ALU.mult, ["QRF", "QSQ"], ["RT0"])
                    tt("dve", RT[1][:T], x2, sinb, ALU.mult, ["QRF", "QSQ"], ["RT1"])
                    tt("dve", RT[2][:T], x2, cosb, ALU.mult, ["QRF", "QSQ"], ["RT2"])
                    tt("dve", RT[3][:T], x1, sinb, ALU.mult, ["QRF", "QSQ"], ["RT3"])
                    tt("dve", QRB[:T, :, 0:32], RT[0][:T], RT[1][:T], ALU.subtract, ["RT0", "RT1"], ["QRB0"])
                    tt("dve", QRB[:T, :, 32:64], RT[2][:T], RT[3][:T], ALU.add, ["RT2", "RT3"], ["QRB1"])
                    for h in range(8):
                        tr(TQ[:, h * 128:h * 128 + T], QNB[:T, h, :], IDB[:T, :T], ["QNB"], ["TQ"])
                    tqh = TQ[:, :].rearrange("p (h t) -> p h t", h=8)[:, :, 0:T]
                    if kind == "own":
                        cp("dve", QTN[:, :, mcol:mcol + T], tqh, ["TQ"], [("QTN", i)])
                    else:
                        cp("dve", SQN[:, :, :, :].rearrange("p b h q -> p h b q"),
                           tqh.rearrange("p h (b q) -> p h b q", b=4), ["TQ"], [("QTN", i)])
                    for h in range(8):
                        tr(TQ[0:64, h * 128:h * 128 + T], QRB[:T, h, :], IDB[:T, :T], ["QRB0", "QRB1"], ["TQ"])
                    tqr = TQ[0:64, :].rearrange("p (h t) -> p h t", h=8)[:, :, 0:T]
                    if kind == "own":
                        cp("dve", QTR[0:64, :, mcol:mcol + T], tqr, ["TQ"], [("QTR", i)])
                    else:
                        cp("dve", SQR[0:64, :, :, :].rearrange("p b h q -> p h b q"),
                           tqr.rearrange("p h (b q) -> p h b q", b=4), ["TQ"], [("QTR", i)])

                npa = 0
                for i in range(NB):
                    if info[i]["kind"] != "halo":
                        info[i]["pa"] = npa % 2
                        npa += 1
                for i in range(min(NXT - 1, NB)):
                    load(i)
                stageF(0)
                for i in range(NB):
                    if i + NXT - 1 < NB:
                        load(i + NXT - 1)
                    if i + 1 < NB:
                        stageF(i + 1)
                    stageG1(i)
                    stageG2(i)
                    stageH(i)
            S.barrier()

        def sweepB():
            with ExitStack() as es:
                WS = [sb(es, "WSB%d" % i, [128, 16, 256], BF16) for i in range(3)]
                U = [sb(es, "U%d" % i, [128, UCOLS], F32) for i in range(2)]
                T1 = sb(es, "T1", [128, UCOLS], F32)
                T2 = sb(es, "T2", [128, UCOLS], F32)
                D = [sb(es, "D%d" % i, [128, 2, NTOK], BF16) for i in range(2)]
                GP = [sb(es, "GP%d" % i, [128, 2, NTOK], BF16) for i in range(2)]
                WP = sb(es, "WP", [128, 4, 2, 256], BF16)
                UT = sb(es, "UT", [128, 8, 80], F32)
                UTT = sb(es, "UTT", [128, 1024], F32)
                SPT = sb(es, "SPT", [128, 8, 64], F32)
                SPL = sb(es, "SPL", [64, 1024], F32)
                TM16 = sb(es, "TM16", [128, 16], F32)
                PU = [ps(es, "PU%d" % i, [128, 1536], F32) for i in range(2)]
                PP = ps(es, "PP", [128, 1024], F32)

                dma("sp", SPL[:], spl[:, :], [], ["SPL"])
                dma("pool", WP[:], w_pool.rearrange("g (k p) n -> p g k n", p=128), [], ["WP"])
                for m in range(8):
                    tr(PP[:, m * 64:(m + 1) * 64], SPL[:64, m * 128:(m + 1) * 128], IDF[:64, :64], ["SPL"], ["PP"])
                cp("dve", SPT[:, :, :], PP[:, 0:512].rearrange("p (m t) -> p m t", m=8), ["PP"], ["SPT"])

                chunks = []
                for g in range(4):
                    chunks.append(("u", g, 256 * g))
                    chunks.append(("gp", g, 1024 + 256 * g))
                for k in range(4):
                    chunks.append(("gm", k, 2880 + 256 * k))

                def wload(ci):
                    kind, g, c0 = chunks[ci]
                    if ci == 0:
                        for k in range(16):
                            dma("pool", WS[0][:, k, :], w_in[128 * k:128 * (k + 1), c0:c0 + 256], [],
                                [("WSB", 0, k)])
                        return
                    dma("pool", WS[ci % 3][:], w_in[:, c0:c0 + 256].rearrange("(k p) n -> p k n", p=128),
                        [], [("WSB", ci % 3, k) for k in range(16)])

                wload(0)
                wload(1)
                nt = 0
                for ci, (kind, g, c0) in enumerate(chunks):
                    if ci + 2 < len(chunks):
                        wload(ci + 2)
                    ws = WS[ci % 3]
                    for mt in range(2):
                        pu = PU[nt % 2]
                        pkey = ("PU", nt % 2)
                        nt += 1
                        if kind == "u":
                            nch = [(0, 512), (512, 1024), (1024, XCOLS)]
                        else:
                            nch = [(16, 528), (528, 1040), (1040, XCOLS)]
                        for c, (a, b) in enumerate(nch):
                            for k in range(16):
                                mm(pu[:, c * 512:c * 512 + (b - a)], ws[:, k, mt * 128:(mt + 1) * 128],
                                   XN[:, k, a:b], k == 0, k == 15, [("WSB", ci % 3, k), "XNALL"], [pkey])
                        if kind == "u":
                            m = 2 * g + mt
                            u = U[m % 2]
                            ukey = ("U", m % 2)
                            uv = u[:, 1040:1168].rearrange("p (b t) -> p b t", b=4)
                            act(u[:, 0:1040], pu[:, 0:1040], AF.Copy, [pkey], [ukey])
                            act(uv[:, :, 16:32], pu[:, 1040:1104].rearrange("p (b t) -> p b t", b=4), AF.Copy,
                                [pkey], [ukey])
                            cp("dve", uv[:, :, 0:16], SPT[:, m, :].rearrange("p (b t) -> p b t", b=4),
                               ["SPT"], [ukey])
                            w = (2, 4, 8, 16)[g]
                            L = UCOLS
                            tt("dve", T1[:, 1:L], u[:, 1:L], u[:, 0:L - 1], ALU.add, [ukey], ["T1"])
                            sw = T1
                            swk = "T1"
                            if w >= 4:
                                tt("dve", T2[:, 3:L], T1[:, 3:L], T1[:, 1:L - 2], ALU.add, ["T1"], ["T2"])
                                sw, swk = T2, "T2"
                            if w >= 8:
                                tt("dve", T1[:, 7:L], T2[:, 7:L], T2[:, 3:L - 4], ALU.add, ["T2"], ["T1"])
                                sw, swk = T1, "T1"
                            if w >= 16:
                                tt("dve", T2[:, 15:L], T1[:, 15:L], T1[:, 7:L - 8], ALU.add, ["T1"], ["T2"])
                                sw, swk = T2, "T2"
                            d = D[g % 2]
                            dkey = ("D", g % 2, mt)
                            stt("dve", d[:, mt, 0:1024], sw[:, 16:1040], 1.0 / w, u[:, 16:1040], ALU.mult,
                                ALU.subtract, [swk, ukey], [dkey])
                            tt("dve", TM16[:, :], sw[:, 16:32], RC16[:, 16 * g:16 * g + 16], ALU.mult,
                               [swk], ["TM16"])
                            tt("dve", d[:, mt, 0:16], TM16[:, :], u[:, 16:32], ALU.subtract, ["TM16", ukey], [dkey])
                            swv = sw[:, 1040:1168].rearrange("p (b t) -> p b t", b=4)
                            stt("dve", d[:, mt, 1024:1088].rearrange("p (b t) -> p b t", b=4), swv[:, :, 16:32],
                                1.0 / w, uv[:, :, 16:32], ALU.mult, ALU.subtract, [swk, ukey], [dkey])
                            cp("dve", UT[:, m, 0:16], u[:, 1024:1040], [ukey], [("UT", m)])
                            cp("dve", UT[:, m, 16:80].rearrange("p (b t) -> p b t", b=4), uv[:, :, 16:32],
                               [ukey], [("UT", m)])
                        elif kind == "gp":
                            act(GP[g % 2][:, mt, :], pu[:, 0:NTOK], AF.Silu, [pkey], [("GP", g % 2, mt)])
                        else:
                            act(MIX[:, 8 + 2 * g + mt, :], pu[:, 0:NTOK], AF.Silu, [pkey], [("MIX", 8 + 2 * g + mt)])
                    if kind == "gp":
                        for j in range(2):
                            for (a, b, passes) in ((0, 1024, ((0, 512), (512, 1024))), (1024, NTOK, ((1024, NTOK),))):
                                for (aa, bb) in passes:
                                    for k in range(2):
                                        mm(PP[:, aa - a:bb - a], WP[:, g, k, j * 128:(j + 1) * 128],
                                           D[g % 2][:, k, aa:bb], k == 0, k == 1,
                                           ["WP", ("D", g % 2, 0), ("D", g % 2, 1)], ["PP"])
                                stt("dve", MIX[:, 2 * g + j, a:b], PP[:, 0:b - a], PSFM[:, 2 * g + j:2 * g + j + 1],
                                    GP[g % 2][:, j, a:b], ALU.mult, ALU.mult,
                                    ["PP", ("GP", g % 2, j)], [("MIX", 2 * g + j, a)])
                        if g == 3:
                            for m in range(8):
                                tr(PP[0:80, m * 128:(m + 1) * 128], UT[:, m, :], IDF[:, :], [("UT", m)], ["PP"])
                            cp("dve", UTT[0:80, :], PP[0:80, :], ["PP"], ["UTT"])
                            dma("sp", pool_o[:, :], UTT[0:16, :], ["UTT"], [])
                            dma("sp", pool_s[:, :], UTT[16:80, :], ["UTT"], [])
            S.barrier()

        def sweepC():
            AT_es = ExitStack()
            ATS = sb(AT_es, "ATS", [128, 4, 1024], BF16)
            WUKV = sb(AT_es, "WUKV", [128, 2, 2048], BF16)
            dma("pool", WUKV[:], w_ukv.rearrange("(k p) n -> p k n", p=128), [], ["WUKV"])
            with ExitStack() as es:
                AT = sb(es, "AT", [128, 8, 1024], BF16)
                KT = [sb(es, "KT%d" % i, [128, 16 * 128], BF16) for i in range(2)]
                V1 = [sb(es, "V1%d" % i, [128, 16, 132], BF16) for i in range(2)]
                KN = [sb(es, "KN%d" % i, [128, 2, 128], BF16) for i in range(2)]
                PT = [sb(es, "PT%d" % i, [128, 4, 128], BF16) for i in range(3)]
                PTD = [sb(es, "PTD%d" % i, [128, 128], BF16) for i in range(2)]
                RD = sb(es, "RD", [128, 8], F32)
                SSK = sb(es, "SSK", [128, 8], F32)
                JC = sb(es, "JC", [128, 128], BF16)
                KVP = [ps(es, "KVP%d" % i, [128, 2, 256], F32) for i in range(2)]
                TPK = ps(es, "TPK", [128, 1024], BF16)
                STP = [ps(es, "STP%d" % i, [128, 512], F32) for i in range(3)]
                OP = [ps(es, "OP%d" % i, [128, 512], F32) for i in range(2)]
                TPX = TPK

                for i in range(2):
                    memset("dve", V1[i][:, :, 128:129], 1.0, [("V1ones", i)])
                    memset("dve", PTD[i][64:128, 0:64], 0.0, [("PTDz", i)])
                cnt = {"pair": 0, "grp": 0, "pt": 0, "ptd": 0, "o": 0}

                def expand(h, kb):
                    for t in range(0, 16, 2):
                        pr = cnt["pair"] % 2
                        cnt["pair"] += 1
                        kvp = KVP[pr]
                        bk = ("BK_KVP", pr)
                        for ti in range(2):
                            for k in range(2):
                                mm(kvp[:, ti, :], CKT[:, k, (t + ti) * 128:(t + ti + 1) * 128],
                                   WUKV[:, k, h * 256:(h + 1) * 256], k == 0, k == 1, ["WUKV"], [("KVP", pr)])
                        ssk = SSK[:, 2 * pr:2 * pr + 2]
                        for ti in range(2):
                            act(JC[:, :], kvp[:, ti, 0:128], AF.Square, [("KVP", pr)], [("SSK", pr), bk],
                                accum_out=SSK[:, 2 * pr + ti:2 * pr + ti + 1])
                        rstd_inplace(ssk, 128, D_NOPE, ("SSK", pr))
                        tt("dve", KN[pr][:, :, :], kvp[:, :, 0:128], bc(ssk, [128, 2, 128], 2), ALU.mult,
                           [("KVP", pr), ("SSK", pr)], [("KN", pr), bk])
                        cp("dve", V1[kb][:, t:t + 2, 0:128], kvp[:, :, 128:256], [("KVP", pr)], [("V1", kb), bk])
                        for ti in range(2):
                            tr(TPK[:, ti * 128:(ti + 1) * 128], KN[pr][:, ti, :], IDB[:, :], [("KN", pr)], ["TPK"])
                        tsc("dve", KT[kb][:, t * 128:(t + 2) * 128], TPK[:, 0:256], GKFM[:, 0:1], None, ALU.mult, None,
                            ["TPK"], [("KT", kb)])

                def make_groups(j):
                    tiles = [(t, "oth") for t in range(8)] + [(8 + t, "full") for t in range(j)] + [(8 + j, "diag")]
                    groups = []
                    cur = []
                    for tl in tiles:
                        t, kind = tl
                        if kind == "diag":
                            if cur:
                                groups.append(cur)
                                cur = []
                            groups.append([tl])
                        else:
                            if cur and (cur[0][1] != kind or len(cur) == 4):
                                groups.append(cur)
                                cur = []
                            cur.append(tl)
                    if cur:
                        groups.append(cur)
                    return groups

                def finalize_block(j):
                    for m in range(8):
                        tr(TPX[:, m * 128:(m + 1) * 128], AT[:, j, m * 128:(m + 1) * 128], IDB[:, :],
                           [("AT", j, m)], ["TPK"])
                    mv = MIX[:, 8:16, 128 * j:128 * (j + 1)]
                    tt("dve", mv, TPX[:, :].rearrange("p (m t) -> p m t", m=8), mv, ALU.mult,
                       ["TPK"], [("MIXF", j)])

                def run_head(h, kb, mid_hook):
                    items = []
                    for j in range(8):
                        gs = make_groups(j)
                        for gi_, g in enumerate(gs):
                            items.append((j, g, gi_ == 0, gi_ == len(gs) - 1))
                    stbuf = {}

                    def qk(idx):
                        j, grp, first, last = items[idx]
                        gi = cnt["grp"] % 3
                        cnt["grp"] += 1
                        stbuf[idx] = gi
                        st = STP[gi]
                        qn_ap = QTN[:, h, 128 * j:128 * (j + 1)]
                        qr_ap = QTR[:, h, 128 * j:128 * (j + 1)]
                        for i, (t, kind) in enumerate(grp):
                            mm(st[:, i * 128:(i + 1) * 128], KT[kb][:, t * 128:(t + 1) * 128], qn_ap, True, False,
                               [("KT", kb)], [("STP", gi)])
                            mm(st[:, i * 128:(i + 1) * 128], KRT[:, t * 128:(t + 1) * 128], qr_ap, False, True,
                               [], [("STP", gi)])

                    state = {"o": None, "okp": None}

                    def exp_pv(idx):
                        j, grp, first, last = items[idx]
                        gi = stbuf.pop(idx)
                        st = STP[gi]
                        if first:
                            state["o"] = OP[cnt["o"] % 2]
                            state["okp"] = ("OP", cnt["o"] % 2)
                            cnt["o"] += 1
                        o, okp = state["o"], state["okp"]
                        kind = grp[0][1]
                        if kind == "diag":
                            di = cnt["ptd"] % 2
                            cnt["ptd"] += 1
                            ptile = PTD[di]
                            pkey = ("PTD", di)
                            act(ptile[0:64, 0:128], st[0:64, 0:128], AF.Exp, [("STP", gi)], [pkey], scale=ATTN_SCALE)
                            act(ptile[64:128, 64:128], st[64:128, 64:128], AF.Exp, [("STP", gi)], [pkey],
                                scale=ATTN_SCALE)
                            lhs = [ptile[:, 0:128]]
                            extra = [("PTDz", di)]
                        else:
                            pi = cnt["pt"] % 3
                            cnt["pt"] += 1
                            ptile = PT[pi]
                            pkey = ("PT", pi)
                            ng = len(grp)
                            bias = OB[:, 0:1] if kind == "oth" else ZEROB[:, 0:1]
                            act(ptile[:, 0:ng, :], st[:, 0:ng * 128].rearrange("p (i t) -> p i t", i=ng), AF.Exp,
                                [("STP", gi)], [pkey], scale=ATTN_SCALE, bias=bias)
                            lhs = [ptile[:, i, :] for i in range(ng)]
                            extra = []
                        for i, (t, kind) in enumerate(grp):
                            mm(o[:, 0:129], lhs[i], V1[kb][:, t, 0:129], first and i == 0,
                               last and i == len(grp) - 1,
                               [pkey, ("V1", kb), ("V1ones", kb)] + extra, [okp])
                        if last:
                            rc = cnt["o"] % 8
                            recip(RD[:, rc:rc + 1], o[:, 128:129], [okp], [("RD", rc)])
                            tsc("dve", AT[:, j, h * 128:(h + 1) * 128], o[:, 0:128], RD[:, rc:rc + 1], None, ALU.mult,
                                None, [okp, ("RD", rc)], [("AT", j, h)])
                            if h == 7:
                                finalize_block(j)

                    N = len(items)
                    LA = 2
                    for idx in range(min(LA, N)):
                        qk(idx)
                    for idx in range(N):
                        if idx + LA < N:
                            qk(idx + LA)
                        exp_pv(idx)
                        if idx == N // 2:
                            mid_hook()

                expand(0, 0)
                for h in range(8):
                    run_head(h, h % 2, (lambda hn=h + 1: expand(hn, hn % 2)) if h + 1 < 8 else (lambda: None))
            S.barrier()

            with ExitStack() as es:
                WKT = sb(es, "WKT", [128, 8, 256], BF16)

                CKB = [sb(es, "CKB%d" % i, [128, 17, 264], BF16) for i in range(3)]
                SCT = [sb(es, "SCT%d" % i, [128, 2, 17 * 128], BF16) for i in range(2)]
                KRB2 = [sb(es, "KRB2%d" % i, [128, 16, 128], BF16) for i in range(3)]
                SKT = [sb(es, "SKT%d" % i, [128, 17 * 128], BF16) for i in range(2)]
                QA = [sb(es, "QA%d" % i, [128, 2, 128], BF16) for i in range(2)]
                SQ = [sb(es, "SQ%d" % i, [128, 1024], F32) for i in range(2)]
                SSK = sb(es, "SSKS", [128, 17, 8], F32)
                TMP = [sb(es, "TMPS%d" % i, [128, 128], F32) for i in range(3)]
                TMP2 = [sb(es, "TMPT%d" % i, [128, 128], F32) for i in range(3)]
                PTS = [sb(es, "PTS%d" % i, [128, 128], BF16) for i in range(4)]
                RD = sb(es, "RDS", [128, 4], F32)
                OLB = sb(es, "OLB", [128, 256], BF16)
                OLT = sb(es, "OLT", [128, 2, 128], BF16)
                RK = [ps(es, "RK%d" % i, [128, 1024], F32) for i in range(2)]
                SSP = [ps(es, "SSP%d" % i, [128, 512], F32) for i in range(2)]
                OL = ps(es, "OL", [128, 512], F32)
                TPX = ps(es, "TPXS", [128, 1024], BF16)
                RKK = [["RK0a", "RK0b"], ["RK1a", "RK1b"]]
                SCB = [(SSP[0], "SSP0"), (SSP[1], "SSP1"), (RK[0][:, 0:512], "RK0a"), (RK[0][:, 512:1024], "RK0b"),
                       (RK[1][:, 0:512], "RK1a")]

                for i in range(3):
                    memset("dve", CKB[i][:, :, 256:257], 1.0, [("CKBones", i)])
                for r in range(2):
                    for hh in range(4):
                        for k in range(2):
                            tr(TPX[:, (hh * 2 + k) * 128:(hh * 2 + k + 1) * 128],
                               WUKV[:, k, (4 * r + hh) * 256:(4 * r + hh) * 256 + 128], IDB[:, :], ["WUKV"], ["TPX"])
                    tsc("dve", WKT[:, 4 * r:4 * r + 4, :].rearrange("p h c -> p (h c)"), TPX[:, :], GKFM[:, 0:1], None,
                        ALU.mult, None, ["TPX"], ["WKT"])
                wk_all = WUKV[:, :, :].rearrange("p k (h x) -> p k h x", x=256)

                def load(b):
                    q = b % 2
                    dma("pool", CKB[b % 3][:, 0:16, 0:256], cck[b].rearrange("(t p) c -> p t c", p=128), [],
                        [("CKB", b % 3)])
                    dma("pool", KRB2[b % 3][:, :, 0:64], ckr[b].rearrange("(t p) c -> p t c", p=128), [],
                        [("KRB2a", b % 3)])
                    dma("pool", KRB2[b % 3][:, :, 64:128], ckr[b].rearrange("(t p) c -> p t c", p=128), [],
                        [("KRB2b", b % 3)])

                def prologue_steps(b):
                    q = b % 2
                    ckb, sct, skt = CKB[b % 3], SCT[q], SKT[q]
                    steps = []

                    def sct_group(t0):
                        for ti in range(4):
                            for k in range(2):
                                tr(TPX[:, (ti * 2 + k) * 128:(ti * 2 + k + 1) * 128],
                                   ckb[:, t0 + ti, k * 128:(k + 1) * 128], IDB[:, :], [("CKB", b % 3)], ["TPX"])
                        tv = TPX[:, :].rearrange("p (t k c) -> p t k c", t=4, k=2)
                        for k in range(2):
                            cp("dve", sct[:, k, t0 * 128:(t0 + 4) * 128].rearrange("p (t c) -> p t c", t=4),
                               tv[:, :, k, :], ["TPX"], [("SCT", q)])

                    for t0 in range(0, 16, 4):
                        steps.append(lambda t0=t0: sct_group(t0))

                    def new_keys():
                        cp("dve", sct[:, :, 2048:2064], SCKT[:, :, 16 * b:16 * b + 16], [], [("SCT", q)])
                        for k in range(2):
                            tr(TPX[0:16, k * 128:(k + 1) * 128], SCKT[:, k, 16 * b:16 * b + 16], IDB[:, :], [],
                               ["TPX"])
                        cp("dve", ckb[0:16, 16, 0:256], TPX[0:16, 0:256], ["TPX"], [("CKB", b % 3)])

                    steps.append(new_keys)

                    def skt_group(t0):
                        for ti in range(8):
                            tr(TPX[:, ti * 128:(ti + 1) * 128], KRB2[b % 3][:, t0 + ti, :], IDB[:, :],
                               [("KRB2a", b % 3), ("KRB2b", b % 3)], ["TPX"])
                        cp("dve", skt[:, t0 * 128:(t0 + 8) * 128], TPX[:, :], ["TPX"], [("SKT", q)])

                    for t0 in range(0, 16, 8):
                        steps.append(lambda t0=t0: skt_group(t0))

                    def qa_step():
                        cp("dve", skt[:, 2048:2064], SKRT[:, 16 * b:16 * b + 16], [], [("SKT", q)])
                        for h in range(8):
                            for k in range(2):
                                mm(SSP[0][:, k * 128 + h * 16:k * 128 + h * 16 + 16],
                                   WKT[:, h, k * 128:(k + 1) * 128], SQN[:, b, h, :], True, True, ["WKT"], ["SSP0"])
                        cp("dve", QA[q][:, :, :], SSP[0][:, 0:256].rearrange("p (k x) -> p k x", k=2), ["SSP0"],
                           [("QA", q)])

                    steps.append(qa_step)
                    return steps

                def prologue(b):
                    for st_ in prologue_steps(b):
                        st_()

                def phase1(b, steps=()):
                    q = b % 2
                    sct = SCT[q]
                    steps = list(steps)

                    def rawk(t):
                        n = 128 if t < 16 else 16
                        rb = t % 2
                        for half in range(2):
                            for k in range(2):
                                mm(RK[rb][:n, half * 512:(half + 1) * 512], sct[:, k, t * 128:t * 128 + n],
                                   wk_all[:, k, 4 * half:4 * half + 4, 0:128], k == 0, k == 1,
                                   [("SCT", q), "WUKV"], RKK[rb])

                    rawk(0)
                    for t in range(17):
                        n = 128 if t < 16 else 16
                        rb = t % 2
                        if t + 1 < 17:
                            rawk(t + 1)
                        act(SQ[rb][:n], RK[rb][:n, :], AF.Square, RKK[rb], [("SQ", rb)])
                        red("dve", SSK[:n, t, :], SQ[rb][:n].rearrange("p (h d) -> p h d", h=8), [("SQ", rb)],
                            ["SSKS"])
                        if steps and t % 2 == 0:
                            steps.pop(0)()
                    while steps:
                        steps.pop(0)()
                    rstd_inplace(SSK[:, 0:16, :], 128, D_NOPE, "SSKS")
                    rstd_inplace(SSK[:16, 16, :], 16, D_NOPE, "SSKS")

                def phase2(b):
                    q = b % 2
                    sct, skt, ckb = SCT[q], SKT[q], CKB[b % 3]
                    qr_all = SQR[:, b, :, :].rearrange("p h q -> p (h q)")
                    NB_ = len(SCB)

                    def score(t):
                        n = 128 if t < 16 else 16
                        buf, key = SCB[t % NB_]
                        for k in range(2):
                            mm(buf[:n, 0:128], sct[:, k, t * 128:t * 128 + n], QA[q][:, k, :], k == 0, k == 1,
                               [("SCT", q), ("QA", q)], [key])
                        mm(buf[:n, 128:256], skt[:, t * 128:t * 128 + n], qr_all, True, True, [("SKT", q)], [key])

                    def soft(t):
                        n = 128 if t < 16 else 16
                        buf, key = SCB[t % NB_]
                        r3 = t % 3
                        pi = t % 4
                        tt("dve", TMP[r3][:n].rearrange("p (h q) -> p h q", h=8),
                           buf[:n, 0:128].rearrange("p (h q) -> p h q", h=8),
                           bc(SSK[:n, t, :], [n, 8, 16], 2), ALU.mult, [key, "SSKS"], [("TMPS", r3)])
                        tt("dve", TMP2[r3][:n], buf[:n, 128:256], TMP[r3][:n], ALU.add, [key, ("TMPS", r3)],
                           [("TMPT", r3)])
                        act(PTS[pi][:n], TMP2[r3][:n], AF.Exp, [("TMPT", r3)], [("PTS", pi)], scale=ATTN_SCALE)

                    def pv(t):
                        n = 128 if t < 16 else 16
                        pi = t % 4
                        mm(OL[:, 0:257], PTS[pi][:n, :], ckb[:n, t, 0:257], t == 0, t == 16,
                           [("PTS", pi), ("CKB", b % 3), ("CKBones", b % 3)], ["OL"])

                    LA = 3
                    for t in range(LA):
                        score(t)
                    for t in range(17):
                        soft(t)
                        if t + LA < 17:
                            score(t + LA)
                        pv(t)

                def epilogue(b):
                    recip(RD[:, b:b + 1], OL[:, 256:257], ["OL"], [("RDS", b)])
                    tsc("dve", OLB[:, :], OL[:, 0:256], RD[:, b:b + 1], None, ALU.mult, None, ["OL", ("RDS", b)], ["OLB"])
                    for k in range(2):
                        tr(TPX[:, k * 128:(k + 1) * 128], OLB[:, k * 128:(k + 1) * 128], IDB[:, :], ["OLB"], ["TPX"])
                    cp("dve", OLT[:, :, :], TPX[:, 0:256].rearrange("p (k x) -> p k x", k=2), ["TPX"], ["OLT"])
                    for h in range(8):
                        for k in range(2):
                            mm(RK[0][:16, h * 128:(h + 1) * 128], OLT[:, k, h * 16:(h + 1) * 16],
                               wk_all[:, k, h, 128:256], k == 0, k == 1, ["OLT", "WUKV"], RKK[0])
                    cp("dve", ATS[:16, b, :], RK[0][:16, :], RKK[0], [("ATS", b)])
                    for m in range(8):
                        tr(TPX[:, m * 128:m * 128 + 16], ATS[:16, b, m * 128:(m + 1) * 128], IDB[:16, :16],
                           [("ATS", b)], ["TPX"])
                    mvb = MIX[:, 8:16, NOWN + 16 * b:NOWN + 16 * b + 16]
                    tt("dve", mvb, TPX[:, :].rearrange("p (m t) -> p m t", m=8)[:, :, 0:16], mvb, ALU.mult,
                       ["TPX"], [("MIXF", 8, b)])

                load(0)
                prologue(0)
                load(1)
                for b in range(4):
                    if b + 2 < 4:
                        load(b + 2)
                    S.tag = ("phase1", b)
                    phase1(b, prologue_steps(b + 1) if b + 1 < 4 else ())
                    S.tag = ("phase2", b)
                    phase2(b)
                    S.tag = ("epilogue", b)
                    epilogue(b)
                    S.tag = None
            AT_es.close()
            S.barrier()

        def sweepDE():
            with ExitStack() as es:
                H = sb(es, "H", [128, 9, 2048], F32)
                WS = [sb(es, "WSD%d" % i, [128, 16, 512], BF16) for i in range(2)]
                blocks = [(128, 128 * j) for j in range(8)] + [(64, NOWN)]
                HN = MIX
                XR = [sb(es, "XR%d" % i, [128, 512], F32) for i in range(3)]
                HS = [sb(es, "HS%d" % i, [128, 2048], BF16) for i in range(2)]
                WPLE = sb(es, "WPLE", [128, 2, 2048], BF16)
                PB32 = [sb(es, "PB32%d" % i, [128, 256], F32) for i in range(2)]
                PBB = [sb(es, "PBB%d" % i, [128, 256], BF16) for i in range(2)]
                PTT = sb(es, "PTT", [128, 2, NTOK], BF16)
                BB = sb(es, "BB", [128, 2048], F32)
                GT = [sb(es, "GT%d" % i, [128, 512], F32) for i in range(2)]
                YT = [sb(es, "YT%d" % i, [128, 512], F32) for i in range(2)]
                YO = [sb(es, "YO%d" % i, [128, 512], F32) for i in range(2)]
                SE = sb(es, "SE", [128, 16], F32)
                PH = [ps(es, "PH%d" % i, [128, 512], F32) for i in range(4)]
                TPE = ps(es, "TPE", [128, 2048], BF16)
                TP2 = ps(es, "TP2", [128, 1024], BF16)

                def xsrc(bi, c):
                    if bi < 8:
                        return xo[16 + 128 * bi:16 + 128 * (bi + 1), c * 512:(c + 1) * 512]
                    return xsm[:, c * 512:(c + 1) * 512]

                def prep_gate_inputs(bi):
                    T, col = blocks[bi]
                    p = bi % 2
                    hv = H[:T, bi, :]
                    hk = [("H", bi, c) for c in range(4)]
                    act(HS[p][:T], hv, AF.Square, hk, [("SE", bi), ("HS", p)], accum_out=SE[:T, bi:bi + 1])
                    rstd_inplace(SE[:T, bi:bi + 1], T, D_MODEL, ("SE", bi))
                    act(HS[p][:T], hv, AF.Copy, hk + [("SE", bi)], [("HS", p)], scale=SE[:T, bi:bi + 1])
                    for k in range(16):
                        tr(TPE[:, k * 128:k * 128 + T], HS[p][:T, k * 128:(k + 1) * 128], IDB[:T, :T],
                           [("HS", p)], ["TPE"])
                    tt("dve", HN[:, :, col:col + T], TPE[:, :].rearrange("p (k t) -> p k t", k=16)[:, :, 0:T],
                       bc(PGFM[:, :], [128, 16, T], 2), ALU.mult, ["TPE"], [("MIXB", bi)])
                    src = pp[128 * bi:128 * (bi + 1), :] if bi < 8 else psm[:, :]
                    dma("sp", PB32[p][:T], src, [], [("PB32", p)])
                    cp("pool", PBB[p][:T], PB32[p][:T], [("PB32", p)], [("PBB", p)])
                    for k in range(2):
                        tr(TP2[:, k * 128:k * 128 + T], PBB[p][:T, k * 128:(k + 1) * 128], IDB[:T, :T],
                           [("PBB", p)], ["TP2"])
                    cp("dve", PTT[:, :, col:col + T], TP2[:, 0:256].rearrange("p (k t) -> p k t", k=2)[:, :, 0:T],
                       ["TP2"], [("PTT", bi)])

                dma("sp", BB[:], b_gate.partition_broadcast(128), [], ["BB"])
                for k in range(16):
                    dma("pool", WS[0][:, k, :], w_out[128 * k:128 * (k + 1), 0:512], [], [("WSD", 0, k)])
                it = 0
                for c in range(4):
                    if c + 1 < 4:
                        dma("pool", WS[(c + 1) % 2][:],
                            w_out[:, (c + 1) * 512:(c + 2) * 512].rearrange("(k p) n -> p k n", p=128),
                            [], [("WSD", (c + 1) % 2, k) for k in range(16)])
                    else:
                        dma("pool", WPLE[:], w_ple.rearrange("(k p) n -> p k n", p=128), [], ["WPLE"])
                    for bi, (T, col) in enumerate(blocks):
                        xr = XR[it % 3]
                        ph = PH[it % 4]
                        dma("sp", xr[:T], xsrc(bi, c), [], [("XR", it % 3)])
                        for k in range(16):
                            mm(ph[:T, :], MIX[:, k, col:col + T], WS[c % 2][:, k, :], k == 0, k == 15,
                               [("MIXB", bi), ("WSD", c % 2, k)], [("PH", it % 4)])
                        tt("dve", H[:T, bi, c * 512:(c + 1) * 512], ph[:T, :], xr[:T], ALU.add,
                           [("PH", it % 4), ("XR", it % 3)], [("H", bi, c)])
                        it += 1
                        if c == 3 and bi >= 1:
                            prep_gate_inputs(bi - 1)
                prep_gate_inputs(8)

                PG = [PH[0], PH[1]]
                PW = [PH[2], PH[3]]
                it = 0

                def gload(c):
                    dma("pool", WS[c % 2][:],
                        w_gate[:, c * 512:(c + 1) * 512].rearrange("(k p) n -> p k n", p=128),
                        [], [("WSD", c % 2, k) for k in range(16)])

                gload(0)
                for c in range(4):
                    if c + 1 < 4:
                        gload(c + 1)
                    cs_ = slice(c * 512, (c + 1) * 512)
                    for bi, (T, col) in enumerate(blocks):
                        q = it % 2
                        for k in range(16):
                            mm(PG[q][:T, :], HN[:, k, col:col + T], WS[c % 2][:, k, :], k == 0, k == 15,
                               [("MIXB", bi), ("WSD", c % 2, k)], [("PH", q)])
                        for k in range(2):
                            mm(PW[q][:T, :], PTT[:, k, col:col + T], WPLE[:, k, cs_], k == 0, k == 1,
                               [("PTT", bi), "WPLE"], [("PH", 2 + q)])
                        tt("dve", GT[q][:T], PG[q][:T, :], BB[:T, cs_], ALU.add, [("PH", q), "BB"], [("GT", q)])
                        act(GT[q][:T], GT[q][:T], AF.Sigmoid, [("GT", q)], [("GT", q)])
                        tt("dve", YT[q][:T], PW[q][:T, :], GT[q][:T], ALU.mult, [("PH", 2 + q), ("GT", q)], [("YT", q)])
                        tt("pool", YO[q][:T], YT[q][:T], H[:T, bi, cs_], ALU.add, [("YT", q), ("H", bi, c)],
                           [("YO", q)])
                        dst = y_o[128 * bi:128 * (bi + 1), cs_] if bi < 8 else y_s[:, cs_]
                        dma("sp", dst, YO[q][:T], [("YO", q)], [])
                        it += 1
            S.barrier()

        sweepA()
        if stage >= 2:
            sweepB()
        s2.__exit__(None, None, None)
        if stage >= 3:
            sweepC()
        s1.__exit__(None, None, None)
        if stage >= 4:
            sweepDE()
        S.emit(block)
    return nc


_PROGRAM = {}


def _get_program(stage=99):
    if stage not in _PROGRAM:
        _PROGRAM[stage] = build_program(stage)
    return _PROGRAM[stage]


def _rope_tab(pos):
    inv = (10000.0 ** (-(np.arange(0, D_ROPE, 2, dtype=np.float64) / D_ROPE))).astype(np.float32)
    ang = (pos.astype(np.float32)[:, None] * inv[None, :]).astype(np.float32).astype(np.float64)
    return np.concatenate([np.cos(ang), np.sin(ang)], axis=1).astype(np.float32)


def make_in_maps(inp):
    f = lambda a: np.ascontiguousarray(np.asarray(a, dtype=np.float32))
    xp = f(inp["x_prompt"])
    xs = f(inp["x_sample"])
    shared = {
        "w_in": f(inp["w_in"][0]), "w_uq": f(inp["w_uq"][0]), "w_ukv": f(inp["w_ukv"][0]),
        "w_pool": f(inp["w_pool"][0]), "w_out": f(inp["w_out"][0]), "w_gate": f(inp["w_ple_gate"][0]),
        "w_ple": f(inp["w_ple"][0]), "norm_g": f(inp["norm_g"][0]), "q_norm_g": f(inp["q_norm_g"][0]),
        "kv_norm_g": f(inp["kv_norm_g"][0]), "q_nope_g": f(inp["q_nope_g"][0]),
        "q_rope_g": f(inp["q_rope_g"][0]), "k_nope_g": f(inp["k_nope_g"][0]),
        "k_rope_g": f(inp["k_rope_g"][0]), "pool_scale": f(inp["pool_scale"][0]),
        "ple_norm_g": f(inp["ple_norm_g"][0]), "b_gate": f(inp["b_ple_gate"][0]),
    }
    maps = []
    for c in range(8):
        b, h = c // 2, c % 2
        m = dict(shared)
        x_own = np.zeros((16 + NOWN, D_MODEL), np.float32)
        x_own[16:] = xp[b, h * NOWN:(h + 1) * NOWN]
        if h == 1:
            x_own[:16] = xp[b, NOWN - 16:NOWN]
        m["xo"] = x_own
        m["xoth"] = f(xp[b, 0:NOWN])
        m["xsm"] = f(xs[4 * c:4 * c + 4].reshape(NSMP, D_MODEL))
        m["cck"] = f(inp["cache_ckv"][0, 4 * c:4 * c + 4])
        m["ckr"] = f(inp["cache_krope"][0, 4 * c:4 * c + 4])
        sp_ = np.zeros((4, 16, D_POOL), np.float32)
        sp_[:, 1:] = inp["state_pool"][0, 4 * c:4 * c + 4]
        m["spl"] = sp_.reshape(64, D_POOL)
        m["pp"] = f(inp["p_prompt"][0, b, h * NOWN:(h + 1) * NOWN])
        m["psm"] = f(inp["p_sample"][0, 4 * c:4 * c + 4].reshape(NSMP, D_PLE))
        cs = np.zeros((17, 128, 64), np.float32)
        own_pos = h * NOWN + np.arange(NOWN)
        cs[0:8] = _rope_tab(own_pos).reshape(8, 128, 64)
        cs[8, 0:64] = _rope_tab(2048 + np.tile(np.arange(16), 4))
        cs[9:17] = _rope_tab(np.arange(NOWN)).reshape(8, 128, 64)
        m["cs"] = cs
        m["ob"] = np.full((128, 1), 0.0 if h == 1 else NEG, np.float32)
        rc = np.zeros((4, 16), np.float32)
        for g, w in enumerate((2, 4, 8, 16)):
            if h == 0:
                rc[g] = 1.0 / np.minimum(np.arange(16) + 1, w)
            else:
                rc[g] = 1.0 / w
        m["rc16"] = np.ascontiguousarray(np.broadcast_to(rc.reshape(1, 64), (128, 64)))
        maps.append(m)
    return maps


def assemble(results):
    yp = np.zeros((4, 2048, D_MODEL), np.float32)
    ys = np.zeros((32, 16, D_MODEL), np.float32)
    ckvp = np.zeros((1, 4, 2048, KV_RANK), np.float32)
    krp = np.zeros((1, 4, 2048, D_ROPE), np.float32)
    poolp = np.zeros((1, 4, 15, D_POOL), np.float32)
    ckvs = np.zeros((1, 32, 16, KV_RANK), np.float32)
    krs = np.zeros((1, 32, 16, D_ROPE), np.float32)
    pools = np.zeros((1, 32, 15, D_POOL), np.float32)
    for c in range(8):
        r = results[c]
        b, h = c // 2, c % 2
        sl = slice(h * NOWN, (h + 1) * NOWN)
        yp[b, sl] = r["y_o"]
        ys[4 * c:4 * c + 4] = r["y_s"].reshape(4, 16, D_MODEL)
        ckvp[0, b, sl] = r["ckv_o"]
        krp[0, b, sl] = r["kr_o"]
        if h == 1:
            poolp[0, b] = r["pool_o"][1:16]
        ckvs[0, 4 * c:4 * c + 4] = r["ckv_s"].reshape(4, 16, KV_RANK)
        krs[0, 4 * c:4 * c + 4] = r["kr_s"].reshape(4, 16, D_ROPE)
        pools[0, 4 * c:4 * c + 4] = r["pool_s"].reshape(4, 16, D_POOL)[:, 1:16]
    return (yp, ys, ckvp, krp, poolp, ckvs, krs, pools)


STAGE = 99
import os
DBG = int(os.environ.get('KDBG', '9'))


def kernel(**inputs):
    nc = _get_program(STAGE)
    maps = make_in_maps(inputs)
    res = run_bass_kernel_spmd(nc, maps, core_ids=list(range(8)))
    return assemble(res.results)
```

```python
import numpy as np
from contextlib import ExitStack
import concourse.bass as bass
import concourse.mybir as mybir
from concourse.bass_utils import run_bass_kernel_spmd

F32 = mybir.dt.float32
BF16 = mybir.dt.bfloat16
AF = mybir.ActivationFunctionType
ALU = mybir.AluOpType
AX = mybir.AxisListType

D_MODEL = 2048
D_POOL = 1024
D_IN = 3904
Q_RANK = 512
KV_RANK = 256
D_ROPE = 64
D_NOPE = 128
N_HEADS = 8
D_PLE = 256
EPS = 1e-6
ATTN_SCALE = 192.0 ** -0.5
NEG = -30000.0
NOWN = 1024
NSMP = 64
NTOK = NOWN + NSMP
XCOLS = 16 + NTOK
UCOLS = 16 + NOWN + 128


class Instr:
    __slots__ = ("eng", "fn", "is_dma", "deps", "milestone", "val", "dsem", "dval", "uid", "cost", "attach", "tag", "waits")


class Sched:
    COMPUTE = ("pe", "act", "dve", "pool")
    QUEUES = ("sp", "act", "pool")

    def __init__(self, nc, es, ndma=8):
        self.nc = nc
        self.streams = {e: [] for e in ("pe", "act", "dve", "pool", "sp")}
        self.lastw = {}
        self.readers = {}
        self.sem = {e: es.enter_context(nc.semaphore("c_" + e)) for e in self.COMPUTE}
        self.dsems = {q: [es.enter_context(nc.semaphore("d_%s%d" % (q, i))) for i in range(ndma)]
                      for q in self.QUEUES}
        self.ndma = ndma
        self.dhist = {q: [] for q in self.QUEUES}
        self.uid = 0
        self.last = {e: None for e in self.streams}
        self.all_dma = []

    def add(self, eng, fn, reads=(), writes=(), dma=False, cost=100.0, attach=False):
        I = Instr()
        I.cost = cost
        I.attach = attach
        I.tag = getattr(self, "tag", None)
        I.eng = eng
        I.fn = fn
        I.is_dma = dma
        I.milestone = False
        I.val = 0
        I.uid = self.uid
        self.uid += 1
        deps = {}
        for b in reads:
            lw = self.lastw.get(b)
            if lw is not None:
                deps[lw.uid] = lw
        for b in writes:
            lw = self.lastw.get(b)
            if lw is not None:
                deps[lw.uid] = lw
            for r in self.readers.get(b, {}).values():
                deps[r.uid] = r
        if dma:
            h = self.dhist[eng]
            n = len(h)
            if n >= self.ndma:
                d = h[n - self.ndma]
                deps[d.uid] = d
            I.dsem = self.dsems[eng][n % self.ndma]
            I.dval = 16 * (n // self.ndma + 1)
            h.append(I)
            self.all_dma.append(I)
        if eng == "pe" and not dma:
            deps = {k: d for k, d in deps.items() if d.is_dma or d.eng != "pe"}
        I.deps = list(deps.values())
        for b in reads:
            rd = self.readers.setdefault(b, {})
            rd[("d", I.uid) if dma else eng] = I
        for b in writes:
            self.lastw[b] = I
            self.readers[b] = {}
        self.streams[eng].append(I)
        if fn is not None:
            self.last[eng] = I
        return I

    def barrier(self):
        prev = [self.last[e] for e in self.streams if self.last[e] is not None]
        prev_dma = list(self.all_dma)
        self.all_dma = []
        for e in self.streams:
            I = self.add(e, None, (), ())
            deps = {d.uid: d for d in prev if not d.is_dma}
            for d in prev_dma:
                deps[d.uid] = d
            I.deps = [d for d in deps.values() if d is not I]
        self.lastw = {}
        self.readers = {}

    def emit(self, block):
        for st in self.streams.values():
            for I in st:
                for d in I.deps:
                    if not d.is_dma:
                        d.milestone = True
        for e in self.COMPUTE:
            c = 0
            for I in self.streams[e]:
                if I.is_dma:
                    continue
                if I.milestone:
                    c += 1
                I.val = c
        sched = self
        know = {e: {} for e in self.streams}
        kdone = {}
        for I in sorted((I for st in self.streams.values() for I in st), key=lambda I: I.uid):
            ke = know[I.eng]
            need = {}
            for d in I.deps:
                if d.is_dma:
                    sm, v = d.dsem, d.dval
                else:
                    sm, v = self.sem[d.eng], d.val
                if ke.get(sm.name, 0) < v:
                    need[sm.name] = (sm, v)
                    for k2, v2 in kdone[d.uid].items():
                        if ke.get(k2, 0) < v2:
                            ke[k2] = v2
                    ke[sm.name] = v
            I.waits = list(need.values())
            if I.is_dma:
                kd = dict(ke)
                kd[I.dsem.name] = I.dval
                kdone[I.uid] = kd
            elif I.milestone:
                kd = dict(ke)
                kd[self.sem[I.eng].name] = I.val
                kdone[I.uid] = kd

        ms_list = {e: [I for I in self.streams[e] if (not I.is_dma) and I.milestone] for e in self.COMPUTE}
        semeng = {self.sem[e].name: e for e in self.COMPUTE}
        used = set()
        for st in self.streams.values():
            for I in st:
                for sm, v in I.waits:
                    if sm.name in semeng:
                        used.add(ms_list[semeng[sm.name]][v - 1].uid)
        for e in self.COMPUTE:
            c = 0
            for I in self.streams[e]:
                if I.is_dma:
                    continue
                I.milestone = I.uid in used
                if I.milestone:
                    c += 1
                I.val = c
        for st in self.streams.values():
            for I in st:
                new = []
                for sm, v in I.waits:
                    if sm.name in semeng:
                        X = ms_list[semeng[sm.name]][v - 1]
                        assert X.milestone and X.val >= 1 and X.uid < I.uid
                        new.append((sm, X.val))
                    else:
                        new.append((sm, v))
                I.waits = new

        def run(e, h):
            for I in sched.streams[e]:
                need = list(I.waits)
                carried = need.pop() if (need and I.attach and I.fn is not None) else None
                for s, v in need:
                    h.wait_ge(s, v)
                if I.fn is None:
                    if I.milestone:
                        h.nop().then_inc(sched.sem[e], 1)
                    continue
                ins = I.fn(h)
                if carried is not None:
                    ins._wait_ge(carried[0], carried[1])
                if I.is_dma:
                    ins.then_inc(I.dsem, 16)
                elif I.milestone:
                    ins.then_inc(sched.sem[e], 1)
            if e == "sp":
                for q in sched.QUEUES:
                    hq = sched.dhist[q]
                    for i in range(sched.ndma):
                        cnt = len([1 for n in range(len(hq)) if n % sched.ndma == i])
                        if cnt:
                            h.wait_ge(sched.dsems[q][i], 16 * cnt)

        @block.sync
        def _(h):
            run("sp", h)

        @block.scalar
        def _(h):
            run("act", h)

        @block.vector
        def _(h):
            run("dve", h)

        @block.gpsimd
        def _(h):
            run("pool", h)

        @block.tensor
        def _(h):
            run("pe", h)


PE_GHZ = 1.6


def _fsz(ap):
    n = 1
    for d in ap.shape[1:]:
        n *= d
    return float(n)


def _ecost(eng, ap):
    n = _fsz(ap)
    if eng == "pool":
        return 130.0 + 2.2 * n
    return 70.0 + 1.05 * n


def simulate(S, hop=350.0, selfhop=120.0):
    order = sorted((I for st in S.streams.values() for I in st), key=lambda I: I.uid)
    free = {e: 0.0 for e in S.streams}
    end = {}
    marks = []
    tags = {}
    for I in order:
        t = free[I.eng]
        for d in I.deps:
            de = end[d.uid]
            lat = selfhop if (d.eng == I.eng and not d.is_dma) else hop
            t = max(t, de + lat)
        if I.fn is None:
            free[I.eng] = t
            end[I.uid] = t
            if I.eng == "pe":
                marks.append(t)
            continue
        if I.is_dma:
            free[I.eng] = t + 60.0
            end[I.uid] = t + I.cost
        else:
            free[I.eng] = t + I.cost
            end[I.uid] = t + I.cost
        if I.tag is not None:
            a, b = tags.get(I.tag, (1e18, 0.0))
            tags[I.tag] = (min(a, t), max(b, end[I.uid]))
    simulate.tags = tags
    return marks, max(end.values())


def build_program(stage=99):
    nc = bass.Bass("TRN2", target_bir_lowering=False)

    def din(name, shape):
        return nc.dram_tensor(name, list(shape), F32, kind="ExternalInput").ap()

    def dout(name, shape):
        return nc.dram_tensor(name, list(shape), F32, kind="ExternalOutput").ap()

    xo = din("xo", [16 + NOWN, D_MODEL])
    xoth = din("xoth", [NOWN, D_MODEL])
    xsm = din("xsm", [NSMP, D_MODEL])
    cck = din("cck", [4, 2048, KV_RANK])
    ckr = din("ckr", [4, 2048, D_ROPE])
    spl = din("spl", [64, D_POOL])
    pp = din("pp", [NOWN, D_PLE])
    psm = din("psm", [NSMP, D_PLE])
    w_in = din("w_in", [D_MODEL, D_IN])
    w_uq = din("w_uq", [Q_RANK, 1536])
    w_ukv = din("w_ukv", [KV_RANK, 2048])
    w_pool = din("w_pool", [4, 256, 256])
    w_out = din("w_out", [D_MODEL, D_MODEL])
    w_gate = din("w_gate", [D_MODEL, D_MODEL])
    w_ple = din("w_ple", [D_PLE, D_MODEL])
    norm_g = din("norm_g", [D_MODEL])
    q_norm_g = din("q_norm_g", [Q_RANK])
    kv_norm_g = din("kv_norm_g", [KV_RANK])
    q_nope_g = din("q_nope_g", [D_NOPE])
    q_rope_g = din("q_rope_g", [D_ROPE])
    k_nope_g = din("k_nope_g", [D_NOPE])
    k_rope_g = din("k_rope_g", [D_ROPE])
    pool_scale = din("pool_scale", [D_POOL])
    ple_norm_g = din("ple_norm_g", [D_MODEL])
    b_gate = din("b_gate", [D_MODEL])
    cs = din("cs", [17, 128, 64])
    ob = din("ob", [128, 1])
    rc16 = din("rc16", [128, 64])

    y_o = dout("y_o", [NOWN, D_MODEL])
    y_s = dout("y_s", [NSMP, D_MODEL])
    ckv_o = dout("ckv_o", [NOWN, KV_RANK])
    kr_o = dout("kr_o", [NOWN, D_ROPE])
    pool_o = dout("pool_o", [16, D_POOL])
    ckv_s = dout("ckv_s", [NSMP, KV_RANK])
    kr_s = dout("kr_s", [NSMP, D_ROPE])
    pool_s = dout("pool_s", [NSMP, D_POOL])

    top = ExitStack()
    with top:
        S = Sched(nc, top)
        block = top.enter_context(nc.Block())

        def sb(es, name, shape, dt):
            return es.enter_context(nc.sbuf_tensor(name, list(shape), dt))

        def ps(es, name, shape, dt):
            return es.enter_context(nc.psum_tensor(name, list(shape), dt))

        def dma(q, out, in_, r, w, **kw):
            return S.add(q, lambda h: h.dma_start(out=out, in_=in_, **kw), r, w, dma=True,
                         cost=2000.0 + _fsz(out) * out.shape[0] * 4 / 150.0)

        def act(out, in_, func, r, w, **kw):
            return S.add("act", lambda h: h.activation(out=out, in_=in_, func=func, **kw), r, w,
                         cost=220.0 + _fsz(in_) * 0.95 + (100.0 if "accum_out" in kw else 0.0),
                         attach=("accum_out" not in kw))

        def mm(out, lhsT, rhs, start, stop, r, w):
            return S.add("pe", lambda h: h.matmul(out, lhsT=lhsT, rhs=rhs, start=start, stop=stop,
                                                  skip_group_check=True), r, w,
                         cost=max(64.0, _fsz(rhs)) / PE_GHZ + 3.0, attach=True)

        def tr(out, in_, ident, r, w):
            return S.add("pe", lambda h: h.transpose(out=out, in_=in_, identity=ident), r, w,
                         cost=max(64.0, in_.shape[0]) / PE_GHZ + 3.0, attach=True)

        def tt(eng, out, in0, in1, op, r, w):
            return S.add(eng, lambda h: h.tensor_tensor(out=out, in0=in0, in1=in1, op=op), r, w,
                         cost=_ecost(eng, out), attach=True)

        def tsc(eng, out, in0, s1, s2, op0, op1, r, w):
            if s2 is None:
                return S.add(eng, lambda h: h.tensor_scalar(out=out, in0=in0, scalar1=s1, scalar2=None,
                                                            op0=op0), r, w, cost=_ecost(eng, out), attach=True)
            return S.add(eng, lambda h: h.tensor_scalar(out=out, in0=in0, scalar1=s1, scalar2=s2,
                                                        op0=op0, op1=op1), r, w, cost=_ecost(eng, out), attach=True)

        def stt(eng, out, in0, scalar, in1, op0, op1, r, w):
            return S.add(eng, lambda h: h.scalar_tensor_tensor(out=out, in0=in0, scalar=scalar, in1=in1,
                                                               op0=op0, op1=op1), r, w, cost=_ecost(eng, out), attach=True)

        def cp(eng, out, in_, r, w):
            return S.add(eng, lambda h: h.tensor_copy(out=out, in_=in_), r, w, cost=_ecost(eng, out), attach=True)

        def red(eng, out, in_, r, w):
            return S.add(eng, lambda h: h.tensor_reduce(out=out, in_=in_, axis=AX.X, op=ALU.add), r, w,
                         cost=_ecost(eng, in_), attach=True)

        def recip(out, in_, r, w):
            return S.add("dve", lambda h: h.reciprocal(out=out, in_=in_), r, w, attach=True)

        def memset(eng, ap, val, w):
            return S.add(eng, lambda h: h.memset(ap, val), (), w, cost=_ecost(eng, ap))

        def bc(ap, shape, axis):
            return ap.unsqueeze(axis).to_broadcast(list(shape))

        IDB = sb(top, "IDB", [128, 128], BF16)
        IDF = sb(top, "IDF", [128, 128], F32)
        EPSB = sb(top, "EPSB", [128, 1], F32)
        ZEROB = sb(top, "ZEROB", [128, 1], F32)
        OB = sb(top, "OB", [128, 1], F32)
        RC16 = sb(top, "RC16", [128, 64], F32)
        CS = sb(top, "CS", [128, 17, 64], F32)
        GFM = sb(top, "GFM", [128, 16], F32)
        PGFM = sb(top, "PGFM", [128, 16], F32)
        PSFM = sb(top, "PSFM", [128, 8], F32)
        GKFM = sb(top, "GKFM", [128, 1], F32)
        GQ = sb(top, "GQ", [128, Q_RANK], F32)
        GKV = sb(top, "GKV", [128, KV_RANK], F32)
        GQN = sb(top, "GQN", [128, D_NOPE], F32)
        GQR = sb(top, "GQR", [128, D_ROPE], F32)
        GKR = sb(top, "GKR", [128, D_ROPE], F32)
        MIXRAW = sb(top, "MIXRAW", [128, 8 * NTOK], F32)
        MIX = MIXRAW[:, :].bitcast(BF16).rearrange("p (k t) -> p k t", k=16)
        STAT = sb(top, "STAT", [128, 18, 40], F32)

        memset("dve", IDF[:], 0.0, ["IDF"])
        S.add("pool", lambda h: h.affine_select(out=IDF[:], in_=IDF[:], pattern=[[-1, 128]],
                                                compare_op=ALU.not_equal, fill=1.0, base=0,
                                                channel_multiplier=1), ["IDF"], ["IDF"])
        cp("dve", IDB[:], IDF[:], ["IDF"], ["IDB"])
        memset("dve", EPSB[:], EPS, ["EPSB"])
        memset("dve", ZEROB[:], 0.0, ["ZEROB"])
        dma("sp", OB[:], ob[:, :], [], ["OB"])
        dma("sp", RC16[:], rc16[:, :], [], ["RC16"])
        dma("sp", CS[:], cs.rearrange("b p c -> p b c"), [], ["CS"])
        dma("sp", GFM[:], norm_g.rearrange("(k p) -> p k", p=128), [], ["GFM"], allow_slow_non_contiguous=True)
        dma("sp", PGFM[:], ple_norm_g.rearrange("(k p) -> p k", p=128), [], ["PGFM"],
            allow_slow_non_contiguous=True)
        dma("sp", PSFM[:], pool_scale.rearrange("(k p) -> p k", p=128), [], ["PSFM"],
            allow_slow_non_contiguous=True)
        dma("sp", GKFM[:], k_nope_g.rearrange("(p o) -> p o", o=1), [], ["GKFM"])
        dma("sp", GQ[:], q_norm_g.partition_broadcast(128), [], ["GQ"])
        dma("sp", GKV[:], kv_norm_g.partition_broadcast(128), [], ["GKV"])
        dma("sp", GQN[:], q_nope_g.partition_broadcast(128), [], ["GQN"])
        dma("sp", GQR[:], q_rope_g.partition_broadcast(128), [], ["GQR"])
        dma("sp", GKR[:], k_rope_g.partition_broadcast(128), [], ["GKR"])
        CONSTS = ["IDB", "IDF", "EPSB", "ZEROB", "OB", "RC16", "CS", "GFM", "PGFM", "PSFM", "GKFM", "GQ",
                  "GKV", "GQN", "GQR", "GKR"]
        S.barrier()

        def rstd_inplace(ap, T, n, key):
            act(ap, ap, AF.Ln, [key], [key], scale=1.0 / n, bias=EPSB[:T])
            act(ap, ap, AF.Exp, [key], [key], scale=-0.5)

        s1 = ExitStack()
        s1.__enter__()
        QTN = sb(s1, "QTN", [128, 8, NOWN], BF16)
        QTR = sb(s1, "QTR", [128, 8, NOWN], BF16)
        SQN = sb(s1, "SQN", [128, 4, 8, 16], BF16)
        SQR = sb(s1, "SQR", [128, 4, 8, 16], BF16)
        CKT = sb(s1, "CKT", [128, 2, 2048], BF16)
        KRT = sb(s1, "KRT", [128, 2048], BF16)
        SCKT = sb(s1, "SCKT", [128, 2, 64], BF16)
        SKRT = sb(s1, "SKRT", [128, 64], BF16)
        s2 = ExitStack()
        s2.__enter__()
        XN = sb(s2, "XN", [128, 16, XCOLS], BF16)

        def sweepA():
            with ExitStack() as es:
                WA = sb(es, "WA", [128, 16, 832], BF16)
                WUQ = sb(es, "WUQ", [128, 4, 1536], BF16)
                NXT = 4
                XT = [MIXRAW[:, 2048 * i:2048 * (i + 1)] for i in range(NXT)]
                memset("pool", QTR[64:128, :, :], 0.0, ["QTRz"])
                memset("pool", SQR[64:128, :, :, :], 0.0, ["SQRz"])
                XS = [sb(es, "XS%d" % i, [128, 2048], BF16) for i in range(2)]
                XNO = [sb(es, "XNO%d" % i, [128, 16, 128], BF16) for i in range(2)]
                JUNK = sb(es, "JUNK", [128, 512], BF16)
                CQN = sb(es, "CQN", [128, 512], BF16)
                CQT = sb(es, "CQT", [128, 4, 128], BF16)
                QRAW = sb(es, "QRAW", [128, 1536], F32)
                QSQ = sb(es, "QSQ", [128, 1536], F32)
                QF = QSQ[:, 0:1024].rearrange("p (h d) -> p h d", h=8)
                QNB = sb(es, "QNB", [128, 8, 128], BF16)
                QRF = QSQ[:, 1024:1536].rearrange("p (h d) -> p h d", h=8)
                RT = [sb(es, "RT%d" % i, [128, 8, 32], F32) for i in range(4)]
                RK_ = [sb(es, "RK_%d" % i, [128, 32], F32) for i in range(4)]
                QRB = sb(es, "QRB", [128, 8, 64], BF16)
                CKVF = [sb(es, "CKVF%d" % i, [128, 256], F32) for i in range(2)]
                CKVB = sb(es, "CKVB", [128, 256], BF16)
                KRF = sb(es, "KRF", [128, 64], F32)
                KRO = [sb(es, "KRO%d" % i, [128, 64], F32) for i in range(2)]
                KRB = sb(es, "KRB", [128, 128], BF16)
                TP = ps(es, "TP", [128, 2048], BF16)
                PA = [ps(es, "PA%d" % i, [128, 1024], F32) for i in range(2)]
                PQ = ps(es, "PQ", [128, 512], F32)
                TQ = ps(es, "TQ", [128, 1024], BF16)

                for k in range(16):
                    dma("pool", WA[:, k, :], w_in[128 * k:128 * (k + 1), 2048:2880], [], [("WA", k)])
                dma("pool", WUQ[:], w_uq.rearrange("(k p) n -> p k n", p=128), [], ["WUQ"])

                blocks = []
                blocks.append((16, xo[0:16, :], "halo", 0))
                blocks.append((64, xsm[:, :], "smp", 16 + NOWN))
                for j in range(8):
                    blocks.append((128, xo[16 + 128 * j:16 + 128 * (j + 1), :], "own", 16 + 128 * j))
                    blocks.append((128, xoth[128 * j:128 * (j + 1), :], "oth", None))
                NB = len(blocks)
                info = {}
                jo = 0
                jt = 0
                for i, (T, src, kind, xc) in enumerate(blocks):
                    d = {"T": T, "kind": kind, "xc": xc}
                    if kind == "own":
                        d["csi"] = jo
                        d["mcol"] = 128 * jo
                        d["kc"] = 1024 + 128 * jo
                        jo += 1
                    elif kind == "smp":
                        d["csi"] = 8
                        d["mcol"] = NOWN
                    elif kind == "oth":
                        d["csi"] = 9 + jt
                        d["kc"] = 128 * jt
                        jt += 1
                    info[i] = d

                def load(i):
                    T, src, kind, xc = blocks[i]
                    dma("sp", XT[i % NXT][:T], src, [], [("XT", i % NXT)])

                def stageF(i):
                    d = info[i]
                    T, kind, xc = d["T"], d["kind"], d["xc"]
                    p = i % 2
                    xt = XT[i % NXT]
                    xk = ("XT", i % NXT)
                    st = STAT[:T, i, :]
                    act(XS[p][:T], xt[:T], AF.Square, [xk], [("ST", i, 0), ("XS", p, 0), ("XS", p, 1)], accum_out=st[:, 0:1])
                    rstd_inplace(st[:, 0:1], T, D_MODEL, ("ST", i, 0))
                    for hf in range(2):
                        act(XS[p][:T, hf * 1024:(hf + 1) * 1024], xt[:T, hf * 1024:(hf + 1) * 1024], AF.Copy,
                            [xk, ("ST", i, 0)], [("XS", p, hf)], scale=st[:, 0:1])
                    for k in range(16):
                        tr(TP[:, k * 128:k * 128 + T], XS[p][:T, k * 128:(k + 1) * 128], IDB[:T, :T],
                           [("XS", p, k // 8)], [("TP", k // 8)])
                    tpv = TP[:, :].rearrange("p (k t) -> p k t", k=16)[:, :, 0:T]
                    if kind == "oth":
                        xn = XNO[d["csi"] % 2][:, :, 0:T]
                        xkey = ("XNO", d["csi"] % 2)
                    else:
                        xn = XN[:, :, xc:xc + T]
                        xkey = ("XN", i)
                    for hf in range(2):
                        tt("dve", xn[:, hf * 8:(hf + 1) * 8, :], tpv[:, hf * 8:(hf + 1) * 8, :],
                           bc(GFM[:, hf * 8:(hf + 1) * 8], [128, 8, T], 2), ALU.mult, [("TP", hf)], [(xkey, hf)])
                    if kind == "halo":
                        return
                    pa = d["pa"]
                    if kind != "oth":
                        for k in range(16):
                            mm(PA[pa][:T, 0:512], xn[:, k, :], WA[:, k, 0:512], k == 0, k == 15,
                               [(xkey, k // 8), ("WA", k)], [("PA0", pa)])
                    for k in range(16):
                        mm(PA[pa][:T, 512:832], xn[:, k, :], WA[:, k, 512:832], k == 0, k == 15,
                           [(xkey, k // 8), ("WA", k)], [("PA1", pa)])

                def stageG1(i):
                    d = info[i]
                    T, kind = d["T"], d["kind"]
                    if kind == "halo":
                        return
                    pa = d["pa"]
                    pav = PA[pa]
                    p = i % 2
                    st = STAT[:T, i, :]
                    csi = d["csi"]
                    bk1 = ("BK_PA1", pa)
                    act(JUNK[:T, 0:256], pav[:T, 512:768], AF.Square, [("PA1", pa)], [("ST", i, 4), bk1],
                        accum_out=st[:, 2:3])
                    act(JUNK[:T, 0:64], pav[:T, 768:832], AF.Square, [("PA1", pa)], [("ST", i, 5), bk1],
                        accum_out=st[:, 3:4])
                    rstd_inplace(st[:, 2:3], T, KV_RANK, ("ST", i, 4))
                    rstd_inplace(st[:, 3:4], T, D_ROPE, ("ST", i, 5))
                    if kind != "oth":
                        act(JUNK[:T, 0:512], pav[:T, 0:512], AF.Square, [("PA0", pa)], [("ST", i, 1)],
                            accum_out=st[:, 1:2])
                        rstd_inplace(st[:, 1:2], T, Q_RANK, ("ST", i, 1))
                        stt("dve", CQN[:T], pav[:T, 0:512], st[:, 1:2], GQ[:T], ALU.mult, ALU.mult,
                            [("PA0", pa), ("ST", i, 1)], ["CQN"])
                        for k in range(4):
                            tr(TQ[:, k * 128:k * 128 + T], CQN[:T, k * 128:(k + 1) * 128], IDB[:T, :T],
                               ["CQN"], ["TQ"])
                        cp("dve", CQT[:, :, 0:T], TQ[:, 0:512].rearrange("p (k t) -> p k t", k=4)[:, :, 0:T],
                           ["TQ"], ["CQT"])

                def stageG2(i):
                    d = info[i]
                    T, kind = d["T"], d["kind"]
                    if kind == "halo":
                        return
                    pa = d["pa"]
                    pav = PA[pa]
                    p = i % 2
                    st = STAT[:T, i, :]
                    csi = d["csi"]
                    bk1 = ("BK_PA1", pa)
                    if kind != "oth":
                        for n in range(3):
                            for k in range(4):
                                mm(PQ[:T, :], CQT[:, k, 0:T], WUQ[:, k, n * 512:(n + 1) * 512],
                                   k == 0, k == 3, ["CQT", "WUQ"], ["PQ"])
                            act(QRAW[:T, n * 512:(n + 1) * 512], PQ[:T, :], AF.Copy, ["PQ"], [("QRAW", n)])
                        act(QSQ[:T], QRAW[:T], AF.Square, [("QRAW", 0), ("QRAW", 1), ("QRAW", 2)], ["QSQ"])
                    ckf = CKVF[p]
                    stt("dve", ckf[:T], pav[:T, 512:768], st[:, 2:3], GKV[:T], ALU.mult, ALU.mult,
                        [("PA1", pa), ("ST", i, 4)], [("CKVF", p), bk1])
                    stt("dve", KRF[:T], pav[:T, 768:832], st[:, 3:4], GKR[:T], ALU.mult, ALU.mult,
                        [("PA1", pa), ("ST", i, 5)], ["KRF", bk1])
                    if kind == "own":
                        dma("sp", ckv_o[d["mcol"]:d["mcol"] + T, :], ckf[:T], [("CKVF", p)], [])
                    elif kind == "smp":
                        dma("sp", ckv_s[:, :], ckf[:T], [("CKVF", p)], [])
                    cp("pool", CKVB[:T], ckf[:T], [("CKVF", p)], ["CKVB"])
                    for k in range(2):
                        tr(TQ[:, k * 128:k * 128 + T], CKVB[:T, k * 128:(k + 1) * 128], IDB[:T, :T],
                           ["CKVB"], ["TQ"])
                    tqv = TQ[:, 0:256].rearrange("p (k t) -> p k t", k=2)[:, :, 0:T]
                    if kind == "smp":
                        cp("dve", SCKT[:, :, 0:T], tqv, ["TQ"], ["SCKT"])
                    else:
                        kc = d["kc"]
                        cp("dve", CKT[:, :, kc:kc + T], tqv, ["TQ"], [("CKT", i)])
                    kro = KRO[p]
                    c1 = CS[:T, csi, 0:32]
                    s1_ = CS[:T, csi, 32:64]
                    r0, r1, r2, r3 = (RK_[q][:T, :] for q in range(4))
                    tt("pool", r0, KRF[:T, 0:32], c1, ALU.mult, ["KRF"], ["RK0"])
                    tt("pool", r1, KRF[:T, 32:64], s1_, ALU.mult, ["KRF"], ["RK1"])
                    tt("pool", r2, KRF[:T, 32:64], c1, ALU.mult, ["KRF"], ["RK2"])
                    tt("pool", r3, KRF[:T, 0:32], s1_, ALU.mult, ["KRF"], ["RK3"])
                    tt("pool", kro[:T, 0:32], r0, r1, ALU.subtract, ["RK0", "RK1"], [("KRO0", p)])
                    tt("pool", kro[:T, 32:64], r2, r3, ALU.add, ["RK2", "RK3"], [("KRO1", p)])
                    if kind == "own":
                        dma("sp", kr_o[d["mcol"]:d["mcol"] + T, :], kro[:T], [("KRO0", p), ("KRO1", p)], [])
                    elif kind == "smp":
                        dma("sp", kr_s[:, :], kro[:T], [("KRO0", p), ("KRO1", p)], [])
                    cp("pool", KRB[:T, 0:64], kro[:T], [("KRO0", p), ("KRO1", p)], ["KRBa"])
                    cp("pool", KRB[:T, 64:128], kro[:T], [("KRO0", p), ("KRO1", p)], ["KRBb"])
                    tr(TQ[:, 0:T], KRB[:T, :], IDB[:T, :T], ["KRBa", "KRBb"], ["TQ"])
                    if kind == "smp":
                        cp("dve", SKRT[:, 0:T], TQ[:, 0:T], ["TQ"], ["SKRT"])
                    else:
                        cp("dve", KRT[:, kc:kc + T], TQ[:, 0:T], ["TQ"], [("KRT", i)])

                def stageH(i):
                    d = info[i]
                    T, kind = d["T"], d["kind"]
                    if kind in ("halo", "oth"):
                        return
                    st = STAT[:T, i, :]
                    csi, mcol = d["csi"], d["mcol"]
                    qsv = QSQ[:T].rearrange("p (h d) -> p h d", h=8)
                    qv = QRAW[:T].rearrange("p (h d) -> p h d", h=8)
                    qk_ = [("QRAW", 0), ("QRAW", 1), ("QRAW", 2)]
                    red("dve", st[:, 8:16], qsv[:, :, 0:128], ["QSQ"], [("ST", i, 2)])
                    red("dve", st[:, 16:24], qsv[:, :, 128:192], ["QSQ"], [("ST", i, 3)])
                    rstd_inplace(st[:, 8:16], T, D_NOPE, ("ST", i, 2))
                    rstd_inplace(st[:, 16:24], T, D_ROPE, ("ST", i, 3))
                    tt("dve", QF[:T], qv[:, :, 0:128], bc(st[:, 8:16], [T, 8, 128], 2), ALU.mult,
                       qk_ + [("ST", i, 2)], ["QF", "QSQ"])
                    tt("dve", QNB[:T], QF[:T], bc(GQN[:T], [T, 8, 128], 1), ALU.mult, ["QF", "QSQ"], ["QNB"])
                    tt("dve", QRF[:T], qv[:, :, 128:192], bc(st[:, 16:24], [T, 8, 64], 2), ALU.mult,
                       qk_ + [("ST", i, 3)], ["QRF", "QSQ"])
                    tt("dve", QRF[:T], QRF[:T], bc(GQR[:T], [T, 8, 64], 1), ALU.mult, ["QRF"], ["QRF", "QSQ"])
                    cosb = bc(CS[:T, csi, 0:32], [T, 8, 32], 1)
                    sinb = bc(CS[:T, csi, 32:64], [T, 8, 32], 1)
                    x1 = QRF[:T, :, 0:32]
                    x2 = QRF[:T, :, 32:64]
                    tt("dve", RT[0][:T], x1, cosb, ALU.mult, ["QRF", "QSQ"], ["RT0"])
                    tt("dve", RT[1][:T], x2, sinb, ALU.mult, ["QRF", "QSQ"], ["RT1"])
                    tt("dve", RT[2][:T], x2, cosb, ALU.mult, ["QRF", "QSQ"], ["RT2"])
                    tt("dve", RT[3][:T], x1, sinb, ALU.mult, ["QRF", "QSQ"], ["RT3"])
                    tt("dve", QRB[:T, :, 0:32], RT[0][:T], RT[1][:T], ALU.subtract, ["RT0", "RT1"], ["QRB0"])
                    tt("dve", QRB[:T, :, 32:64], RT[2][:T], RT[3][:T], ALU.add, ["RT2", "RT3"], ["QRB1"])
                    for h in range(8):
                        tr(TQ[:, h * 128:h * 128 + T], QNB[:T, h, :], IDB[:T, :T], ["QNB"], ["TQ"])
                    tqh = TQ[:, :].rearrange("p (h t) -> p h t", h=8)[:, :, 0:T]
                    if kind == "own":
                        cp("dve", QTN[:, :, mcol:mcol + T], tqh, ["TQ"], [("QTN", i)])
                    else:
                        cp("dve", SQN[:, :, :, :].rearrange("p b h q -> p h b q"),
                           tqh.rearrange("p h (b q) -> p h b q", b=4), ["TQ"], [("QTN", i)])
                    for h in range(8):
                        tr(TQ[0:64, h * 128:h * 128 + T], QRB[:T, h, :], IDB[:T, :T], ["QRB0", "QRB1"], ["TQ"])
                    tqr = TQ[0:64, :].rearrange("p (h t) -> p h t", h=8)[:, :, 0:T]
                    if kind == "own":
                        cp("dve", QTR[0:64, :, mcol:mcol + T], tqr, ["TQ"], [("QTR", i)])
                    else:
                        cp("dve", SQR[0:64, :, :, :].rearrange("p b h q -> p h b q"),
                           tqr.rearrange("p h (b q) -> p h b q", b=4), ["TQ"], [("QTR", i)])

                npa = 0
                for i in range(NB):
                    if info[i]["kind"] != "halo":
                        info[i]["pa"] = npa % 2
                        npa += 1
                for i in range(min(NXT - 1, NB)):
                    load(i)
                stageF(0)
                for i in range(NB):
                    if i + NXT - 1 < NB:
                        load(i + NXT - 1)
                    if i + 1 < NB:
                        stageF(i + 1)
                    stageG1(i)
                    stageG2(i)
                    stageH(i)
            S.barrier()

        def sweepB():
            with ExitStack() as es:
                WS = [sb(es, "WSB%d" % i, [128, 16, 256], BF16) for i in range(3)]
                U = [sb(es, "U%d" % i, [128, UCOLS], F32) for i in range(2)]
                T1 = sb(es, "T1", [128, UCOLS], F32)
                T2 = sb(es, "T2", [128, UCOLS], F32)
                D = [sb(es, "D%d" % i, [128, 2, NTOK], BF16) for i in range(2)]
                GP = [sb(es, "GP%d" % i, [128, 2, NTOK], BF16) for i in range(2)]
                WP = sb(es, "WP", [128, 4, 2, 256], BF16)
                UT = sb(es, "UT", [128, 8, 80], F32)
                UTT = sb(es, "UTT", [128, 1024], F32)
                SPT = sb(es, "SPT", [128, 8, 64], F32)
                SPL = sb(es, "SPL", [64, 1024], F32)
                TM16 = sb(es, "TM16", [128, 16], F32)
                PU = [ps(es, "PU%d" % i, [128, 1536], F32) for i in range(2)]
                PP = ps(es, "PP", [128, 1024], F32)

                dma("sp", SPL[:], spl[:, :], [], ["SPL"])
                dma("pool", WP[:], w_pool.rearrange("g (k p) n -> p g k n", p=128), [], ["WP"])
                for m in range(8):
                    tr(PP[:, m * 64:(m + 1) * 64], SPL[:64, m * 128:(m + 1) * 128], IDF[:64, :64], ["SPL"], ["PP"])
                cp("dve", SPT[:, :, :], PP[:, 0:512].rearrange("p (m t) -> p m t", m=8), ["PP"], ["SPT"])

                chunks = []
                for g in range(4):
                    chunks.append(("u", g, 256 * g))
                    chunks.append(("gp", g, 1024 + 256 * g))
                for k in range(4):
                    chunks.append(("gm", k, 2880 + 256 * k))

                def wload(ci):
                    kind, g, c0 = chunks[ci]
                    if ci == 0:
                        for k in range(16):
                            dma("pool", WS[0][:, k, :], w_in[128 * k:128 * (k + 1), c0:c0 + 256], [],
                                [("WSB", 0, k)])
                        return
                    dma("pool", WS[ci % 3][:], w_in[:, c0:c0 + 256].rearrange("(k p) n -> p k n", p=128),
                        [], [("WSB", ci % 3, k) for k in range(16)])

                wload(0)
                wload(1)
                nt = 0
                for ci, (kind, g, c0) in enumerate(chunks):
                    if ci + 2 < len(chunks):
                        wload(ci + 2)
                    ws = WS[ci % 3]
                    for mt in range(2):
                        pu = PU[nt % 2]
                        pkey = ("PU", nt % 2)
                        nt += 1
                        if kind == "u":
                            nch = [(0, 512), (512, 1024), (1024, XCOLS)]
                        else:
                            nch = [(16, 528), (528, 1040), (1040, XCOLS)]
                        for c, (a, b) in enumerate(nch):
                            for k in range(16):
                                mm(pu[:, c * 512:c * 512 + (b - a)], ws[:, k, mt * 128:(mt + 1) * 128],
                                   XN[:, k, a:b], k == 0, k == 15, [("WSB", ci % 3, k), "XNALL"], [pkey])
                        if kind == "u":
                            m = 2 * g + mt
                            u = U[m % 2]
                            ukey = ("U", m % 2)
                            uv = u[:, 1040:1168].rearrange("p (b t) -> p b t", b=4)
                            act(u[:, 0:1040], pu[:, 0:1040], AF.Copy, [pkey], [ukey])
                            act(uv[:, :, 16:32], pu[:, 1040:1104].rearrange("p (b t) -> p b t", b=4), AF.Copy,
                                [pkey], [ukey])
                            cp("dve", uv[:, :, 0:16], SPT[:, m, :].rearrange("p (b t) -> p b t", b=4),
                               ["SPT"], [ukey])
                            w = (2, 4, 8, 16)[g]
                            L = UCOLS
                            tt("dve", T1[:, 1:L], u[:, 1:L], u[:, 0:L - 1], ALU.add, [ukey], ["T1"])
                            sw = T1
                            swk = "T1"
                            if w >= 4:
                                tt("dve", T2[:, 3:L], T1[:, 3:L], T1[:, 1:L - 2], ALU.add, ["T1"], ["T2"])
                                sw, swk = T2, "T2"
                            if w >= 8:
                                tt("dve", T1[:, 7:L], T2[:, 7:L], T2[:, 3:L - 4], ALU.add, ["T2"], ["T1"])
                                sw, swk = T1, "T1"
                            if w >= 16:
                                tt("dve", T2[:, 15:L], T1[:, 15:L], T1[:, 7:L - 8], ALU.add, ["T1"], ["T2"])
                                sw, swk = T2, "T2"
                            d = D[g % 2]
                            dkey = ("D", g % 2, mt)
                            stt("dve", d[:, mt, 0:1024], sw[:, 16:1040], 1.0 / w, u[:, 16:1040], ALU.mult,
                                ALU.subtract, [swk, ukey], [dkey])
                            tt("dve", TM16[:, :], sw[:, 16:32], RC16[:, 16 * g:16 * g + 16], ALU.mult,
                               [swk], ["TM16"])
                            tt("dve", d[:, mt, 0:16], TM16[:, :], u[:, 16:32], ALU.subtract, ["TM16", ukey], [dkey])
                            swv = sw[:, 1040:1168].rearrange("p (b t) -> p b t", b=4)
                            stt("dve", d[:, mt, 1024:1088].rearrange("p (b t) -> p b t", b=4), swv[:, :, 16:32],
                                1.0 / w, uv[:, :, 16:32], ALU.mult, ALU.subtract, [swk, ukey], [dkey])
                            cp("dve", UT[:, m, 0:16], u[:, 1024:1040], [ukey], [("UT", m)])
                            cp("dve", UT[:, m, 16:80].rearrange("p (b t) -> p b t", b=4), uv[:, :, 16:32],
                               [ukey], [("UT", m)])
                        elif kind == "gp":
                            act(GP[g % 2][:, mt, :], pu[:, 0:NTOK], AF.Silu, [pkey], [("GP", g % 2, mt)])
                        else:
                            act(MIX[:, 8 + 2 * g + mt, :], pu[:, 0:NTOK], AF.Silu, [pkey], [("MIX", 8 + 2 * g + mt)])
                    if kind == "gp":
                        for j in range(2):
                            for (a, b, passes) in ((0, 1024, ((0, 512), (512, 1024))), (1024, NTOK, ((1024, NTOK),))):
                                for (aa, bb) in passes:
                                    for k in range(2):
                                        mm(PP[:, aa - a:bb - a], WP[:, g, k, j * 128:(j + 1) * 128],
                                           D[g % 2][:, k, aa:bb], k == 0, k == 1,
                                           ["WP", ("D", g % 2, 0), ("D", g % 2, 1)], ["PP"])
                                stt("dve", MIX[:, 2 * g + j, a:b], PP[:, 0:b - a], PSFM[:, 2 * g + j:2 * g + j + 1],
                                    GP[g % 2][:, j, a:b], ALU.mult, ALU.mult,
                                    ["PP", ("GP", g % 2, j)], [("MIX", 2 * g + j, a)])
                        if g == 3:
                            for m in range(8):
                                tr(PP[0:80, m * 128:(m + 1) * 128], UT[:, m, :], IDF[:, :], [("UT", m)], ["PP"])
                            cp("dve", UTT[0:80, :], PP[0:80, :], ["PP"], ["UTT"])
                            dma("sp", pool_o[:, :], UTT[0:16, :], ["UTT"], [])
                            dma("sp", pool_s[:, :], UTT[16:80, :], ["UTT"], [])
            S.barrier()

        def sweepC():
            AT_es = ExitStack()
            ATS = sb(AT_es, "ATS", [128, 4, 1024], BF16)
            WUKV = sb(AT_es, "WUKV", [128, 2, 2048], BF16)
            dma("pool", WUKV[:], w_ukv.rearrange("(k p) n -> p k n", p=128), [], ["WUKV"])
            with ExitStack() as es:
                AT = sb(es, "AT", [128, 8, 1024], BF16)
                KT = [sb(es, "KT%d" % i, [128, 16 * 128], BF16) for i in range(2)]
                V1 = [sb(es, "V1%d" % i, [128, 16, 132], BF16) for i in range(2)]
                KN = [sb(es, "KN%d" % i, [128, 2, 128], BF16) for i in range(2)]
                PT = [sb(es, "PT%d" % i, [128, 4, 128], BF16) for i in range(3)]
                PTD = [sb(es, "PTD%d" % i, [128, 128], BF16) for i in range(2)]
                RD = sb(es, "RD", [128, 8], F32)
                SSK = sb(es, "SSK", [128, 8], F32)
                JC = sb(es, "JC", [128, 128], BF16)
                KVP = [ps(es, "KVP%d" % i, [128, 2, 256], F32) for i in range(2)]
                TPK = ps(es, "TPK", [128, 1024], BF16)
                STP = [ps(es, "STP%d" % i, [128, 512], F32) for i in range(3)]
                OP = [ps(es, "OP%d" % i, [128, 512], F32) for i in range(2)]
                TPX = TPK

                for i in range(2):
                    memset("dve", V1[i][:, :, 128:129], 1.0, [("V1ones", i)])
                    memset("dve", PTD[i][64:128, 0:64], 0.0, [("PTDz", i)])
                cnt = {"pair": 0, "grp": 0, "pt": 0, "ptd": 0, "o": 0}

                def expand(h, kb):
                    for t in range(0, 16, 2):
                        pr = cnt["pair"] % 2
                        cnt["pair"] += 1
                        kvp = KVP[pr]
                        bk = ("BK_KVP", pr)
                        for ti in range(2):
                            for k in range(2):
                                mm(kvp[:, ti, :], CKT[:, k, (t + ti) * 128:(t + ti + 1) * 128],
                                   WUKV[:, k, h * 256:(h + 1) * 256], k == 0, k == 1, ["WUKV"], [("KVP", pr)])
                        ssk = SSK[:, 2 * pr:2 * pr + 2]
                        for ti in range(2):
                            act(JC[:, :], kvp[:, ti, 0:128], AF.Square, [("KVP", pr)], [("SSK", pr), bk],
                                accum_out=SSK[:, 2 * pr + ti:2 * pr + ti + 1])
                        rstd_inplace(ssk, 128, D_NOPE, ("SSK", pr))
                        tt("dve", KN[pr][:, :, :], kvp[:, :, 0:128], bc(ssk, [128, 2, 128], 2), ALU.mult,
                           [("KVP", pr), ("SSK", pr)], [("KN", pr), bk])
                        cp("dve", V1[kb][:, t:t + 2, 0:128], kvp[:, :, 128:256], [("KVP", pr)], [("V1", kb), bk])
                        for ti in range(2):
                            tr(TPK[:, ti * 128:(ti + 1) * 128], KN[pr][:, ti, :], IDB[:, :], [("KN", pr)], ["TPK"])
                        tsc("dve", KT[kb][:, t * 128:(t + 2) * 128], TPK[:, 0:256], GKFM[:, 0:1], None, ALU.mult, None,
                            ["TPK"], [("KT", kb)])

                def make_groups(j):
                    tiles = [(t, "oth") for t in range(8)] + [(8 + t, "full") for t in range(j)] + [(8 + j, "diag")]
                    groups = []
                    cur = []
                    for tl in tiles:
                        t, kind = tl
                        if kind == "diag":
                            if cur:
                                groups.append(cur)
                                cur = []
                            groups.append([tl])
                        else:
                            if cur and (cur[0][1] != kind or len(cur) == 4):
                                groups.append(cur)
                                cur = []
                            cur.append(tl)
                    if cur:
                        groups.append(cur)
                    return groups

                def finalize_block(j):
                    for m in range(8):
                        tr(TPX[:, m * 128:(m + 1) * 128], AT[:, j, m * 128:(m + 1) * 128], IDB[:, :],
                           [("AT", j, m)], ["TPK"])
                    mv = MIX[:, 8:16, 128 * j:128 * (j + 1)]
                    tt("dve", mv, TPX[:, :].rearrange("p (m t) -> p m t", m=8), mv, ALU.mult,
                       ["TPK"], [("MIXF", j)])

                def run_head(h, kb, mid_hook):
                    items = []
                    for j in range(8):
                        gs = make_groups(j)
                        for gi_, g in enumerate(gs):
                            items.append((j, g, gi_ == 0, gi_ == len(gs) - 1))
                    stbuf = {}

                    def qk(idx):
                        j, grp, first, last = items[idx]
                        gi = cnt["grp"] % 3
                        cnt["grp"] += 1
                        stbuf[idx] = gi
                        st = STP[gi]
                        qn_ap = QTN[:, h, 128 * j:128 * (j + 1)]
                        qr_ap = QTR[:, h, 128 * j:128 * (j + 1)]
                        for i, (t, kind) in enumerate(grp):
                            mm(st[:, i * 128:(i + 1) * 128], KT[kb][:, t * 128:(t + 1) * 128], qn_ap, True, False,
                               [("KT", kb)], [("STP", gi)])
                            mm(st[:, i * 128:(i + 1) * 128], KRT[:, t * 128:(t + 1) * 128], qr_ap, False, True,
                               [], [("STP", gi)])

                    state = {"o": None, "okp": None}

                    def exp_pv(idx):
                        j, grp, first, last = items[idx]
                        gi = stbuf.pop(idx)
                        st = STP[gi]
                        if first:
                            state["o"] = OP[cnt["o"] % 2]
                            state["okp"] = ("OP", cnt["o"] % 2)
                            cnt["o"] += 1
                        o, okp = state["o"], state["okp"]
                        kind = grp[0][1]
                        if kind == "diag":
                            di = cnt["ptd"] % 2
                            cnt["ptd"] += 1
                            ptile = PTD[di]
                            pkey = ("PTD", di)
                            act(ptile[0:64, 0:128], st[0:64, 0:128], AF.Exp, [("STP", gi)], [pkey], scale=ATTN_SCALE)
                            act(ptile[64:128, 64:128], st[64:128, 64:128], AF.Exp, [("STP", gi)], [pkey],
                                scale=ATTN_SCALE)
                            lhs = [ptile[:, 0:128]]
                            extra = [("PTDz", di)]
                        else:
                            pi = cnt["pt"] % 3
                            cnt["pt"] += 1
                            ptile = PT[pi]
                            pkey = ("PT", pi)
                            ng = len(grp)
                            bias = OB[:, 0:1] if kind == "oth" else ZEROB[:, 0:1]
                            act(ptile[:, 0:ng, :], st[:, 0:ng * 128].rearrange("p (i t) -> p i t", i=ng), AF.Exp,
                                [("STP", gi)], [pkey], scale=ATTN_SCALE, bias=bias)
                            lhs = [ptile[:, i, :] for i in range(ng)]
                            extra = []
                        for i, (t, kind) in enumerate(grp):
                            mm(o[:, 0:129], lhs[i], V1[kb][:, t, 0:129], first and i == 0,
                               last and i == len(grp) - 1,
                               [pkey, ("V1", kb), ("V1ones", kb)] + extra, [okp])
                        if last:
                            rc = cnt["o"] % 8
                            recip(RD[:, rc:rc + 1], o[:, 128:129], [okp], [("RD", rc)])
                            tsc("dve", AT[:, j, h * 128:(h + 1) * 128], o[:, 0:128], RD[:, rc:rc + 1], None, ALU.mult,
                                None, [okp, ("RD", rc)], [("AT", j, h)])
                            if h == 7:
                                finalize_block(j)

                    N = len(items)
                    LA = 2
                    for idx in range(min(LA, N)):
                        qk(idx)
                    for idx in range(N):
                        if idx + LA < N:
                            qk(idx + LA)
                        exp_pv(idx)
                        if idx == N // 2:
                            mid_hook()

                expand(0, 0)
                for h in range(8):
                    run_head(h, h % 2, (lambda hn=h + 1: expand(hn, hn % 2)) if h + 1 < 8 else (lambda: None))
            S.barrier()

            with ExitStack() as es:
                WKT = sb(es, "WKT", [128, 8, 256], BF16)

                CKB = [sb(es, "CKB%d" % i, [128, 17, 264], BF16) for i in range(3)]
                SCT = [sb(es, "SCT%d" % i, [128, 2, 17 * 128], BF16) for i in range(2)]
                KRB2 = [sb(es, "KRB2%d" % i, [128, 16, 128], BF16) for i in range(3)]
                SKT = [sb(es, "SKT%d" % i, [128, 17 * 128], BF16) for i in range(2)]
                QA = [sb(es, "QA%d" % i, [128, 2, 128], BF16) for i in range(2)]
                SQ = [sb(es, "SQ%d" % i, [128, 1024], F32) for i in range(2)]
                SSK = sb(es, "SSKS", [128, 17, 8], F32)
                TMP = [sb(es, "TMPS%d" % i, [128, 128], F32) for i in range(3)]
                TMP2 = [sb(es, "TMPT%d" % i, [128, 128], F32) for i in range(3)]
                PTS = [sb(es, "PTS%d" % i, [128, 128], BF16) for i in range(4)]
                RD = sb(es, "RDS", [128, 4], F32)
                OLB = sb(es, "OLB", [128, 256], BF16)
                OLT = sb(es, "OLT", [128, 2, 128], BF16)
                RK = [ps(es, "RK%d" % i, [128, 1024], F32) for i in range(2)]
                SSP = [ps(es, "SSP%d" % i, [128, 512], F32) for i in range(2)]
                OL = ps(es, "OL", [128, 512], F32)
                TPX = ps(es, "TPXS", [128, 1024], BF16)
                RKK = [["RK0a", "RK0b"], ["RK1a", "RK1b"]]
                SCB = [(SSP[0], "SSP0"), (SSP[1], "SSP1"), (RK[0][:, 0:512], "RK0a"), (RK[0][:, 512:1024], "RK0b"),
                       (RK[1][:, 0:512], "RK1a")]

                for i in range(3):
                    memset("dve", CKB[i][:, :, 256:257], 1.0, [("CKBones", i)])
                for r in range(2):
                    for hh in range(4):
                        for k in range(2):
                            tr(TPX[:, (hh * 2 + k) * 128:(hh * 2 + k + 1) * 128],
                               WUKV[:, k, (4 * r + hh) * 256:(4 * r + hh) * 256 + 128], IDB[:, :], ["WUKV"], ["TPX"])
                    tsc("dve", WKT[:, 4 * r:4 * r + 4, :].rearrange("p h c -> p (h c)"), TPX[:, :], GKFM[:, 0:1], None,
                        ALU.mult, None, ["TPX"], ["WKT"])
                wk_all = WUKV[:, :, :].rearrange("p k (h x) -> p k h x", x=256)

                def load(b):
                    q = b % 2
                    dma("pool", CKB[b % 3][:, 0:16, 0:256], cck[b].rearrange("(t p) c -> p t c", p=128), [],
                        [("CKB", b % 3)])
                    dma("pool", KRB2[b % 3][:, :, 0:64], ckr[b].rearrange("(t p) c -> p t c", p=128), [],
                        [("KRB2a", b % 3)])
                    dma("pool", KRB2[b % 3][:, :, 64:128], ckr[b].rearrange("(t p) c -> p t c", p=128), [],
                        [("KRB2b", b % 3)])

                def prologue_steps(b):
                    q = b % 2
                    ckb, sct, skt = CKB[b % 3], SCT[q], SKT[q]
                    steps = []

                    def sct_group(t0):
                        for ti in range(4):
                            for k in range(2):
                                tr(TPX[:, (ti * 2 + k) * 128:(ti * 2 + k + 1) * 128],
                                   ckb[:, t0 + ti, k * 128:(k + 1) * 128], IDB[:, :], [("CKB", b % 3)], ["TPX"])
                        tv = TPX[:, :].rearrange("p (t k c) -> p t k c", t=4, k=2)
                        for k in range(2):
                            cp("dve", sct[:, k, t0 * 128:(t0 + 4) * 128].rearrange("p (t c) -> p t c", t=4),
                               tv[:, :, k, :], ["TPX"], [("SCT", q)])

                    for t0 in range(0, 16, 4):
                        steps.append(lambda t0=t0: sct_group(t0))

                    def new_keys():
                        cp("dve", sct[:, :, 2048:2064], SCKT[:, :, 16 * b:16 * b + 16], [], [("SCT", q)])
                        for k in range(2):
                            tr(TPX[0:16, k * 128:(k + 1) * 128], SCKT[:, k, 16 * b:16 * b + 16], IDB[:, :], [],
                               ["TPX"])
                        cp("dve", ckb[0:16, 16, 0:256], TPX[0:16, 0:256], ["TPX"], [("CKB", b % 3)])

                    steps.append(new_keys)

                    def skt_group(t0):
                        for ti in range(8):
                            tr(TPX[:, ti * 128:(ti + 1) * 128], KRB2[b % 3][:, t0 + ti, :], IDB[:, :],
                               [("KRB2a", b % 3), ("KRB2b", b % 3)], ["TPX"])
                        cp("dve", skt[:, t0 * 128:(t0 + 8) * 128], TPX[:, :], ["TPX"], [("SKT", q)])

                    for t0 in range(0, 16, 8):
                        steps.append(lambda t0=t0: skt_group(t0))

                    def qa_step():
                        cp("dve", skt[:, 2048:2064], SKRT[:, 16 * b:16 * b + 16], [], [("SKT", q)])
                        for h in range(8):
                            for k in range(2):
                                mm(SSP[0][:, k * 128 + h * 16:k * 128 + h * 16 + 16],
                                   WKT[:, h, k * 128:(k + 1) * 128], SQN[:, b, h, :], True, True, ["WKT"], ["SSP0"])
                        cp("dve", QA[q][:, :, :], SSP[0][:, 0:256].rearrange("p (k x) -> p k x", k=2), ["SSP0"],
                           [("QA", q)])

                    steps.append(qa_step)
                    return steps

                def prologue(b):
                    for st_ in prologue_steps(b):
                        st_()

                def phase1(b, steps=()):
                    q = b % 2
                    sct = SCT[q]
                    steps = list(steps)

                    def rawk(t):
                        n = 128 if t < 16 else 16
                        rb = t % 2
                        for half in range(2):
                            for k in range(2):
                                mm(RK[rb][:n, half * 512:(half + 1) * 512], sct[:, k, t * 128:t * 128 + n],
                                   wk_all[:, k, 4 * half:4 * half + 4, 0:128], k == 0, k == 1,
                                   [("SCT", q), "WUKV"], RKK[rb])

                    rawk(0)
                    for t in range(17):
                        n = 128 if t < 16 else 16
                        rb = t % 2
                        if t + 1 < 17:
                            rawk(t + 1)
                        act(SQ[rb][:n], RK[rb][:n, :], AF.Square, RKK[rb], [("SQ", rb)])
                        red("dve", SSK[:n, t, :], SQ[rb][:n].rearrange("p (h d) -> p h d", h=8), [("SQ", rb)],
                            ["SSKS"])
                        if steps and t % 2 == 0:
                            steps.pop(0)()
                    while steps:
                        steps.pop(0)()
                    rstd_inplace(SSK[:, 0:16, :], 128, D_NOPE, "SSKS")
                    rstd_inplace(SSK[:16, 16, :], 16, D_NOPE, "SSKS")

                def phase2(b):
                    q = b % 2
                    sct, skt, ckb = SCT[q], SKT[q], CKB[b % 3]
                    qr_all = SQR[:, b, :, :].rearrange("p h q -> p (h q)")
                    NB_ = len(SCB)

                    def score(t):
                        n = 128 if t < 16 else 16
                        buf, key = SCB[t % NB_]
                        for k in range(2):
                            mm(buf[:n, 0:128], sct[:, k, t * 128:t * 128 + n], QA[q][:, k, :], k == 0, k == 1,
                               [("SCT", q), ("QA", q)], [key])
                        mm(buf[:n, 128:256], skt[:, t * 128:t * 128 + n], qr_all, True, True, [("SKT", q)], [key])

                    def soft(t):
                        n = 128 if t < 16 else 16
                        buf, key = SCB[t % NB_]
                        r3 = t % 3
                        pi = t % 4
                        tt("dve", TMP[r3][:n].rearrange("p (h q) -> p h q", h=8),
                           buf[:n, 0:128].rearrange("p (h q) -> p h q", h=8),
                           bc(SSK[:n, t, :], [n, 8, 16], 2), ALU.mult, [key, "SSKS"], [("TMPS", r3)])
                        tt("dve", TMP2[r3][:n], buf[:n, 128:256], TMP[r3][:n], ALU.add, [key, ("TMPS", r3)],
                           [("TMPT", r3)])
                        act(PTS[pi][:n], TMP2[r3][:n], AF.Exp, [("TMPT", r3)], [("PTS", pi)], scale=ATTN_SCALE)

                    def pv(t):
                        n = 128 if t < 16 else 16
                        pi = t % 4
                        mm(OL[:, 0:257], PTS[pi][:n, :], ckb[:n, t, 0:257], t == 0, t == 16,
                           [("PTS", pi), ("CKB", b % 3), ("CKBones", b % 3)], ["OL"])

                    LA = 3
                    for t in range(LA):
                        score(t)
                    for t in range(17):
                        soft(t)
                        if t + LA < 17:
                            score(t + LA)
                        pv(t)

                def epilogue(b):
                    recip(RD[:, b:b + 1], OL[:, 256:257], ["OL"], [("RDS", b)])
                    tsc("dve", OLB[:, :], OL[:, 0:256], RD[:, b:b + 1], None, ALU.mult, None, ["OL", ("RDS", b)], ["OLB"])
                    for k in range(2):
                        tr(TPX[:, k * 128:(k + 1) * 128], OLB[:, k * 128:(k + 1) * 128], IDB[:, :], ["OLB"], ["TPX"])
                    cp("dve", OLT[:, :, :], TPX[:, 0:256].rearrange("p (k x) -> p k x", k=2), ["TPX"], ["OLT"])
                    for h in range(8):
                        for k in range(2):
                            mm(RK[0][:16, h * 128:(h + 1) * 128], OLT[:, k, h * 16:(h + 1) * 16],
                               wk_all[:, k, h, 128:256], k == 0, k == 1, ["OLT", "WUKV"], RKK[0])
                    cp("dve", ATS[:16, b, :], RK[0][:16, :], RKK[0], [("ATS", b)])

                load(0)
                prologue(0)
                load(1)
                for b in range(4):
                    if b + 2 < 4:
                        load(b + 2)
                    S.tag = ("phase1", b)
                    phase1(b, prologue_steps(b + 1) if b + 1 < 4 else ())
                    S.tag = ("phase2", b)
                    phase2(b)
                    S.tag = ("epilogue", b)
                    epilogue(b)
                    S.tag = None
                for m in range(8):
                    for b in range(4):
                        tr(TPX[:, m * 128 + 16 * b:m * 128 + 16 * b + 16], ATS[:16, b, m * 128:(m + 1) * 128],
                           IDB[:16, :16], [("ATS", b)], ["TPX"])
                mv = MIX[:, 8:16, NOWN:NTOK]
                tt("dve", mv, TPX[:, :].rearrange("p (m t) -> p m t", m=8)[:, :, 0:64], mv, ALU.mult,
                   ["TPX"], [("MIXF", 8)])
            AT_es.close()
            S.barrier()

        def sweepDE():
            with ExitStack() as es:
                H = sb(es, "H", [128, 9, 2048], F32)
                WS = [sb(es, "WSD%d" % i, [128, 16, 512], BF16) for i in range(2)]
                blocks = [(128, 128 * j) for j in range(8)] + [(64, NOWN)]
                HN = MIX
                XR = [sb(es, "XR%d" % i, [128, 512], F32) for i in range(3)]
                HS = [sb(es, "HS%d" % i, [128, 2048], BF16) for i in range(2)]
                WPLE = sb(es, "WPLE", [128, 2, 2048], BF16)
                PB32 = [sb(es, "PB32%d" % i, [128, 256], F32) for i in range(2)]
                PBB = [sb(es, "PBB%d" % i, [128, 256], BF16) for i in range(2)]
                PTT = sb(es, "PTT", [128, 2, NTOK], BF16)
                BB = sb(es, "BB", [128, 2048], F32)
                GT = [sb(es, "GT%d" % i, [128, 512], F32) for i in range(2)]
                YT = [sb(es, "YT%d" % i, [128, 512], F32) for i in range(2)]
                YO = [sb(es, "YO%d" % i, [128, 512], F32) for i in range(2)]
                SE = sb(es, "SE", [128, 16], F32)
                PH = [ps(es, "PH%d" % i, [128, 512], F32) for i in range(4)]
                TPE = ps(es, "TPE", [128, 2048], BF16)
                TP2 = ps(es, "TP2", [128, 1024], BF16)

                def xsrc(bi, c):
                    if bi < 8:
                        return xo[16 + 128 * bi:16 + 128 * (bi + 1), c * 512:(c + 1) * 512]
                    return xsm[:, c * 512:(c + 1) * 512]

                def prep_gate_inputs(bi):
                    T, col = blocks[bi]
                    p = bi % 2
                    hv = H[:T, bi, :]
                    hk = [("H", bi, c) for c in range(4)]
                    act(HS[p][:T], hv, AF.Square, hk, [("SE", bi), ("HS", p)], accum_out=SE[:T, bi:bi + 1])
                    rstd_inplace(SE[:T, bi:bi + 1], T, D_MODEL, ("SE", bi))
                    act(HS[p][:T], hv, AF.Copy, hk + [("SE", bi)], [("HS", p)], scale=SE[:T, bi:bi + 1])
                    for k in range(16):
                        tr(TPE[:, k * 128:k * 128 + T], HS[p][:T, k * 128:(k + 1) * 128], IDB[:T, :T],
                           [("HS", p)], ["TPE"])
                    tt("dve", HN[:, :, col:col + T], TPE[:, :].rearrange("p (k t) -> p k t", k=16)[:, :, 0:T],
                       bc(PGFM[:, :], [128, 16, T], 2), ALU.mult, ["TPE"], [("MIXB", bi)])
                    src = pp[128 * bi:128 * (bi + 1), :] if bi < 8 else psm[:, :]
                    dma("sp", PB32[p][:T], src, [], [("PB32", p)])
                    cp("pool", PBB[p][:T], PB32[p][:T], [("PB32", p)], [("PBB", p)])
                    for k in range(2):
                        tr(TP2[:, k * 128:k * 128 + T], PBB[p][:T, k * 128:(k + 1) * 128], IDB[:T, :T],
                           [("PBB", p)], ["TP2"])
                    cp("dve", PTT[:, :, col:col + T], TP2[:, 0:256].rearrange("p (k t) -> p k t", k=2)[:, :, 0:T],
                       ["TP2"], [("PTT", bi)])

                dma("sp", BB[:], b_gate.partition_broadcast(128), [], ["BB"])
                for k in range(16):
                    dma("pool", WS[0][:, k, :], w_out[128 * k:128 * (k + 1), 0:512], [], [("WSD", 0, k)])
                it = 0
                for c in range(4):
                    if c + 1 < 4:
                        dma("pool", WS[(c + 1) % 2][:],
                            w_out[:, (c + 1) * 512:(c + 2) * 512].rearrange("(k p) n -> p k n", p=128),
                            [], [("WSD", (c + 1) % 2, k) for k in range(16)])
                    else:
                        dma("pool", WPLE[:], w_ple.rearrange("(k p) n -> p k n", p=128), [], ["WPLE"])
                    for bi, (T, col) in enumerate(blocks):
                        xr = XR[it % 3]
                        ph = PH[it % 4]
                        dma("sp", xr[:T], xsrc(bi, c), [], [("XR", it % 3)])
                        for k in range(16):
                            mm(ph[:T, :], MIX[:, k, col:col + T], WS[c % 2][:, k, :], k == 0, k == 15,
                               [("MIXB", bi), ("WSD", c % 2, k)], [("PH", it % 4)])
                        tt("dve", H[:T, bi, c * 512:(c + 1) * 512], ph[:T, :], xr[:T], ALU.add,
                           [("PH", it % 4), ("XR", it % 3)], [("H", bi, c)])
                        it += 1
                        if c == 3 and bi >= 1:
                            prep_gate_inputs(bi - 1)
                prep_gate_inputs(8)

                PG = [PH[0], PH[1]]
                PW = [PH[2], PH[3]]
                it = 0

                def gload(c):
                    dma("pool", WS[c % 2][:],
                        w_gate[:, c * 512:(c + 1) * 512].rearrange("(k p) n -> p k n", p=128),
                        [], [("WSD", c % 2, k) for k in range(16)])

                gload(0)
                for c in range(4):
                    if c + 1 < 4:
                        gload(c + 1)
                    cs_ = slice(c * 512, (c + 1) * 512)
                    for bi, (T, col) in enumerate(blocks):
                        q = it % 2
                        for k in range(16):
                            mm(PG[q][:T, :], HN[:, k, col:col + T], WS[c % 2][:, k, :], k == 0, k == 15,
                               [("MIXB", bi), ("WSD", c % 2, k)], [("PH", q)])
                        for k in range(2):
                            mm(PW[q][:T, :], PTT[:, k, col:col + T], WPLE[:, k, cs_], k == 0, k == 1,
                               [("PTT", bi), "WPLE"], [("PH", 2 + q)])
                        tt("dve", GT[q][:T], PG[q][:T, :], BB[:T, cs_], ALU.add, [("PH", q), "BB"], [("GT", q)])
                        act(GT[q][:T], GT[q][:T], AF.Sigmoid, [("GT", q)], [("GT", q)])
                        tt("dve", YT[q][:T], PW[q][:T, :], GT[q][:T], ALU.mult, [("PH", 2 + q), ("GT", q)], [("YT", q)])
                        tt("pool", YO[q][:T], YT[q][:T], H[:T, bi, cs_], ALU.add, [("YT", q), ("H", bi, c)],
                           [("YO", q)])
                        dst = y_o[128 * bi:128 * (bi + 1), cs_] if bi < 8 else y_s[:, cs_]
                        dma("sp", dst, YO[q][:T], [("YO", q)], [])
                        it += 1
            S.barrier()

        sweepA()
        if stage >= 2:
            sweepB()
        s2.__exit__(None, None, None)
        if stage >= 3:
            sweepC()
        s1.__exit__(None, None, None)
        if stage >= 4:
            sweepDE()
        S.emit(block)
    return nc


_PROGRAM = {}


def _get_program(stage=99):
    if stage not in _PROGRAM:
        _PROGRAM[stage] = build_program(stage)
    return _PROGRAM[stage]


def _rope_tab(pos):
    inv = (10000.0 ** (-(np.arange(0, D_ROPE, 2, dtype=np.float64) / D_ROPE))).astype(np.float32)
    ang = (pos.astype(np.float32)[:, None] * inv[None, :]).astype(np.float32).astype(np.float64)
    return np.concatenate([np.cos(ang), np.sin(ang)], axis=1).astype(np.float32)


def make_in_maps(inp):
    f = lambda a: np.ascontiguousarray(np.asarray(a, dtype=np.float32))
    xp = f(inp["x_prompt"])
    xs = f(inp["x_sample"])
    shared = {
        "w_in": f(inp["w_in"][0]), "w_uq": f(inp["w_uq"][0]), "w_ukv": f(inp["w_ukv"][0]),
        "w_pool": f(inp["w_pool"][0]), "w_out": f(inp["w_out"][0]), "w_gate": f(inp["w_ple_gate"][0]),
        "w_ple": f(inp["w_ple"][0]), "norm_g": f(inp["norm_g"][0]), "q_norm_g": f(inp["q_norm_g"][0]),
        "kv_norm_g": f(inp["kv_norm_g"][0]), "q_nope_g": f(inp["q_nope_g"][0]),
        "q_rope_g": f(inp["q_rope_g"][0]), "k_nope_g": f(inp["k_nope_g"][0]),
        "k_rope_g": f(inp["k_rope_g"][0]), "pool_scale": f(inp["pool_scale"][0]),
        "ple_norm_g": f(inp["ple_norm_g"][0]), "b_gate": f(inp["b_ple_gate"][0]),
    }
    maps = []
    for c in range(8):
        b, h = c // 2, c % 2
        m = dict(shared)
        x_own = np.zeros((16 + NOWN, D_MODEL), np.float32)
        x_own[16:] = xp[b, h * NOWN:(h + 1) * NOWN]
        if h == 1:
            x_own[:16] = xp[b, NOWN - 16:NOWN]
        m["xo"] = x_own
        m["xoth"] = f(xp[b, 0:NOWN])
        m["xsm"] = f(xs[4 * c:4 * c + 4].reshape(NSMP, D_MODEL))
        m["cck"] = f(inp["cache_ckv"][0, 4 * c:4 * c + 4])
        m["ckr"] = f(inp["cache_krope"][0, 4 * c:4 * c + 4])
        sp_ = np.zeros((4, 16, D_POOL), np.float32)
        sp_[:, 1:] = inp["state_pool"][0, 4 * c:4 * c + 4]
        m["spl"] = sp_.reshape(64, D_POOL)
        m["pp"] = f(inp["p_prompt"][0, b, h * NOWN:(h + 1) * NOWN])
        m["psm"] = f(inp["p_sample"][0, 4 * c:4 * c + 4].reshape(NSMP, D_PLE))
        cs = np.zeros((17, 128, 64), np.float32)
        own_pos = h * NOWN + np.arange(NOWN)
        cs[0:8] = _rope_tab(own_pos).reshape(8, 128, 64)
        cs[8, 0:64] = _rope_tab(2048 + np.tile(np.arange(16), 4))
        cs[9:17] = _rope_tab(np.arange(NOWN)).reshape(8, 128, 64)
        m["cs"] = cs
        m["ob"] = np.full((128, 1), 0.0 if h == 1 else NEG, np.float32)
        rc = np.zeros((4, 16), np.float32)
        for g, w in enumerate((2, 4, 8, 16)):
            if h == 0:
                rc[g] = 1.0 / np.minimum(np.arange(16) + 1, w)
            else:
                rc[g] = 1.0 / w
        m["rc16"] = np.ascontiguousarray(np.broadcast_to(rc.reshape(1, 64), (128, 64)))
        maps.append(m)
    return maps


def assemble(results):
    yp = np.zeros((4, 2048, D_MODEL), np.float32)
    ys = np.zeros((32, 16, D_MODEL), np.float32)
    ckvp = np.zeros((1, 4, 2048, KV_RANK), np.float32)
    krp = np.zeros((1, 4, 2048, D_ROPE), np.float32)
    poolp = np.zeros((1, 4, 15, D_POOL), np.float32)
    ckvs = np.zeros((1, 32, 16, KV_RANK), np.float32)
    krs = np.zeros((1, 32, 16, D_ROPE), np.float32)
    pools = np.zeros((1, 32, 15, D_POOL), np.float32)
    for c in range(8):
        r = results[c]
        b, h = c // 2, c % 2
        sl = slice(h * NOWN, (h + 1) * NOWN)
        yp[b, sl] = r["y_o"]
        ys[4 * c:4 * c + 4] = r["y_s"].reshape(4, 16, D_MODEL)
        ckvp[0, b, sl] = r["ckv_o"]
        krp[0, b, sl] = r["kr_o"]
        if h == 1:
            poolp[0, b] = r["pool_o"][1:16]
        ckvs[0, 4 * c:4 * c + 4] = r["ckv_s"].reshape(4, 16, KV_RANK)
        krs[0, 4 * c:4 * c + 4] = r["kr_s"].reshape(4, 16, D_ROPE)
        pools[0, 4 * c:4 * c + 4] = r["pool_s"].reshape(4, 16, D_POOL)[:, 1:16]
    return (yp, ys, ckvp, krp, poolp, ckvs, krs, pools)


STAGE = 99
import os
DBG = int(os.environ.get('KDBG', '9'))


def kernel(**inputs):
    nc = _get_program(STAGE)
    maps = make_in_maps(inputs)
    res = run_bass_kernel_spmd(nc, maps, core_ids=list(range(8)))
    return assemble(res.results)
```

```python
import numpy as np
from contextlib import ExitStack
import concourse.bass as bass
import concourse.mybir as mybir
from concourse.bass_utils import run_bass_kernel_spmd

F32 = mybir.dt.float32
BF16 = mybir.dt.bfloat16
AF = mybir.ActivationFunctionType
ALU = mybir.AluOpType
AX = mybir.AxisListType

D_MODEL = 2048
D_POOL = 1024
D_IN = 3904
Q_RANK = 512
KV_RANK = 256
D_ROPE = 64
D_NOPE = 128
N_HEADS = 8
D_PLE = 256
EPS = 1e-6
ATTN_SCALE = 192.0 ** -0.5
NEG = -30000.0
NOWN = 1024
NSMP = 64
NTOK = NOWN + NSMP
XCOLS = 16 + NTOK
UCOLS = 16 + NOWN + 128


class Instr:
    __slots__ = ("eng", "fn", "is_dma", "deps", "milestone", "val", "dsem", "dval", "uid", "cost", "attach", "tag", "waits")


class Sched:
    COMPUTE = ("pe", "act", "dve", "pool")
    QUEUES = ("sp", "act", "pool")

    def __init__(self, nc, es, ndma=8):
        self.nc = nc
        self.streams = {e: [] for e in ("pe", "act", "dve", "pool", "sp")}
        self.lastw = {}
        self.readers = {}
        self.sem = {e: es.enter_context(nc.semaphore("c_" + e)) for e in self.COMPUTE}
        self.dsems = {q: [es.enter_context(nc.semaphore("d_%s%d" % (q, i))) for i in range(ndma)]
                      for q in self.QUEUES}
        self.ndma = ndma
        self.dhist = {q: [] for q in self.QUEUES}
        self.uid = 0
        self.last = {e: None for e in self.streams}
        self.all_dma = []

    def add(self, eng, fn, reads=(), writes=(), dma=False, cost=100.0, attach=False):
        I = Instr()
        I.cost = cost
        I.attach = attach
        I.tag = getattr(self, "tag", None)
        I.eng = eng
        I.fn = fn
        I.is_dma = dma
        I.milestone = False
        I.val = 0
        I.uid = self.uid
        self.uid += 1
        deps = {}
        for b in reads:
            lw = self.lastw.get(b)
            if lw is not None:
                deps[lw.uid] = lw
        for b in writes:
            lw = self.lastw.get(b)
            if lw is not None:
                deps[lw.uid] = lw
            for r in self.readers.get(b, {}).values():
                deps[r.uid] = r
        if dma:
            h = self.dhist[eng]
            n = len(h)
            if n >= self.ndma:
                d = h[n - self.ndma]
                deps[d.uid] = d
            I.dsem = self.dsems[eng][n % self.ndma]
            I.dval = 16 * (n // self.ndma + 1)
            h.append(I)
            self.all_dma.append(I)
        if eng == "pe" and not dma:
            deps = {k: d for k, d in deps.items() if d.is_dma or d.eng != "pe"}
        I.deps = list(deps.values())
        for b in reads:
            rd = self.readers.setdefault(b, {})
            rd[("d", I.uid) if dma else eng] = I
        for b in writes:
            self.lastw[b] = I
            self.readers[b] = {}
        self.streams[eng].append(I)
        if fn is not None:
            self.last[eng] = I
        return I

    def barrier(self):
        prev = [self.last[e] for e in self.streams if self.last[e] is not None]
        prev_dma = list(self.all_dma)
        self.all_dma = []
        for e in self.streams:
            I = self.add(e, None, (), ())
            deps = {d.uid: d for d in prev if not d.is_dma}
            for d in prev_dma:
                deps[d.uid] = d
            I.deps = [d for d in deps.values() if d is not I]
        self.lastw = {}
        self.readers = {}

    def emit(self, block):
        for st in self.streams.values():
            for I in st:
                for d in I.deps:
                    if not d.is_dma:
                        d.milestone = True
        for e in self.COMPUTE:
            c = 0
            for I in self.streams[e]:
                if I.is_dma:
                    continue
                if I.milestone:
                    c += 1
                I.val = c
        sched = self
        know = {e: {} for e in self.streams}
        kdone = {}
        for I in sorted((I for st in self.streams.values() for I in st), key=lambda I: I.uid):
            ke = know[I.eng]
            need = {}
            for d in I.deps:
                if d.is_dma:
                    sm, v = d.dsem, d.dval
                else:
                    sm, v = self.sem[d.eng], d.val
                if ke.get(sm.name, 0) < v:
                    need[sm.name] = (sm, v)
                    for k2, v2 in kdone[d.uid].items():
                        if ke.get(k2, 0) < v2:
                            ke[k2] = v2
                    ke[sm.name] = v
            I.waits = list(need.values())
            if I.is_dma:
                kd = dict(ke)
                kd[I.dsem.name] = I.dval
                kdone[I.uid] = kd
            elif I.milestone:
                kd = dict(ke)
                kd[self.sem[I.eng].name] = I.val
                kdone[I.uid] = kd

        ms_list = {e: [I for I in self.streams[e] if (not I.is_dma) and I.milestone] for e in self.COMPUTE}
        semeng = {self.sem[e].name: e for e in self.COMPUTE}
        used = set()
        for st in self.streams.values():
            for I in st:
                for sm, v in I.waits:
                    if sm.name in semeng:
                        used.add(ms_list[semeng[sm.name]][v - 1].uid)
        for e in self.COMPUTE:
            c = 0
            for I in self.streams[e]:
                if I.is_dma:
                    continue
                I.milestone = I.uid in used
                if I.milestone:
                    c += 1
                I.val = c
        for st in self.streams.values():
            for I in st:
                new = []
                for sm, v in I.waits:
                    if sm.name in semeng:
                        X = ms_list[semeng[sm.name]][v - 1]
                        assert X.milestone and X.val >= 1 and X.uid < I.uid
                        new.append((sm, X.val))
                    else:
                        new.append((sm, v))
                I.waits = new

        def run(e, h):
            for I in sched.streams[e]:
                need = list(I.waits)
                carried = need.pop() if (need and I.attach and I.fn is not None) else None
                for s, v in need:
                    h.wait_ge(s, v)
                if I.fn is None:
                    if I.milestone:
                        h.nop().then_inc(sched.sem[e], 1)
                    continue
                ins = I.fn(h)
                if carried is not None:
                    ins._wait_ge(carried[0], carried[1])
                if I.is_dma:
                    ins.then_inc(I.dsem, 16)
                elif I.milestone:
                    ins.then_inc(sched.sem[e], 1)
            if e == "sp":
                for q in sched.QUEUES:
                    hq = sched.dhist[q]
                    for i in range(sched.ndma):
                        cnt = len([1 for n in range(len(hq)) if n % sched.ndma == i])
                        if cnt:
                            h.wait_ge(sched.dsems[q][i], 16 * cnt)

        @block.sync
        def _(h):
            run("sp", h)

        @block.scalar
        def _(h):
            run("act", h)

        @block.vector
        def _(h):
            run("dve", h)

        @block.gpsimd
        def _(h):
            run("pool", h)

        @block.tensor
        def _(h):
            run("pe", h)


PE_GHZ = 1.6


def _fsz(ap):
    n = 1
    for d in ap.shape[1:]:
        n *= d
    return float(n)


def _ecost(eng, ap):
    n = _fsz(ap)
    if eng == "pool":
        return 130.0 + 2.2 * n
    return 70.0 + 1.05 * n


def simulate(S, hop=350.0, selfhop=120.0):
    order = sorted((I for st in S.streams.values() for I in st), key=lambda I: I.uid)
    free = {e: 0.0 for e in S.streams}
    end = {}
    marks = []
    tags = {}
    for I in order:
        t = free[I.eng]
        for d in I.deps:
            de = end[d.uid]
            lat = selfhop if (d.eng == I.eng and not d.is_dma) else hop
            t = max(t, de + lat)
        if I.fn is None:
            free[I.eng] = t
            end[I.uid] = t
            if I.eng == "pe":
                marks.append(t)
            continue
        if I.is_dma:
            free[I.eng] = t + 60.0
            end[I.uid] = t + I.cost
        else:
            free[I.eng] = t + I.cost
            end[I.uid] = t + I.cost
        if I.tag is not None:
            a, b = tags.get(I.tag, (1e18, 0.0))
            tags[I.tag] = (min(a, t), max(b, end[I.uid]))
    simulate.tags = tags
    return marks, max(end.values())


def build_program(stage=99):
    nc = bass.Bass("TRN2", target_bir_lowering=False)

    def din(name, shape):
        return nc.dram_tensor(name, list(shape), F32, kind="ExternalInput").ap()

    def dout(name, shape):
        return nc.dram_tensor(name, list(shape), F32, kind="ExternalOutput").ap()

    xo = din("xo", [16 + NOWN, D_MODEL])
    xoth = din("xoth", [NOWN, D_MODEL])
    xsm = din("xsm", [NSMP, D_MODEL])
    cck = din("cck", [4, 2048, KV_RANK])
    ckr = din("ckr", [4, 2048, D_ROPE])
    spl = din("spl", [64, D_POOL])
    pp = din("pp", [NOWN, D_PLE])
    psm = din("psm", [NSMP, D_PLE])
    w_in = din("w_in", [D_MODEL, D_IN])
    w_uq = din("w_uq", [Q_RANK, 1536])
    w_ukv = din("w_ukv", [KV_RANK, 2048])
    w_pool = din("w_pool", [4, 256, 256])
    w_out = din("w_out", [D_MODEL, D_MODEL])
    w_gate = din("w_gate", [D_MODEL, D_MODEL])
    w_ple = din("w_ple", [D_PLE, D_MODEL])
    norm_g = din("norm_g", [D_MODEL])
    q_norm_g = din("q_norm_g", [Q_RANK])
    kv_norm_g = din("kv_norm_g", [KV_RANK])
    q_nope_g = din("q_nope_g", [D_NOPE])
    q_rope_g = din("q_rope_g", [D_ROPE])
    k_nope_g = din("k_nope_g", [D_NOPE])
    k_rope_g = din("k_rope_g", [D_ROPE])
    pool_scale = din("pool_scale", [D_POOL])
    ple_norm_g = din("ple_norm_g", [D_MODEL])
    b_gate = din("b_gate", [D_MODEL])
    cs = din("cs", [17, 128, 64])
    ob = din("ob", [128, 1])
    rc16 = din("rc16", [128, 64])

    y_o = dout("y_o", [NOWN, D_MODEL])
    y_s = dout("y_s", [NSMP, D_MODEL])
    ckv_o = dout("ckv_o", [NOWN, KV_RANK])
    kr_o = dout("kr_o", [NOWN, D_ROPE])
    pool_o = dout("pool_o", [16, D_POOL])
    ckv_s = dout("ckv_s", [NSMP, KV_RANK])
    kr_s = dout("kr_s", [NSMP, D_ROPE])
    pool_s = dout("pool_s", [NSMP, D_POOL])

    top = ExitStack()
    with top:
        S = Sched(nc, top)
        block = top.enter_context(nc.Block())

        def sb(es, name, shape, dt):
            return es.enter_context(nc.sbuf_tensor(name, list(shape), dt))

        def ps(es, name, shape, dt):
            return es.enter_context(nc.psum_tensor(name, list(shape), dt))

        def dma(q, out, in_, r, w, **kw):
            return S.add(q, lambda h: h.dma_start(out=out, in_=in_, **kw), r, w, dma=True,
                         cost=2000.0 + _fsz(out) * out.shape[0] * 4 / 150.0)

        def act(out, in_, func, r, w, **kw):
            return S.add("act", lambda h: h.activation(out=out, in_=in_, func=func, **kw), r, w,
                         cost=220.0 + _fsz(in_) * 0.95 + (100.0 if "accum_out" in kw else 0.0),
                         attach=("accum_out" not in kw))

        def mm(out, lhsT, rhs, start, stop, r, w):
            return S.add("pe", lambda h: h.matmul(out, lhsT=lhsT, rhs=rhs, start=start, stop=stop,
                                                  skip_group_check=True), r, w,
                         cost=max(64.0, _fsz(rhs)) / PE_GHZ + 3.0, attach=True)

        def tr(out, in_, ident, r, w):
            return S.add("pe", lambda h: h.transpose(out=out, in_=in_, identity=ident), r, w,
                         cost=max(64.0, in_.shape[0]) / PE_GHZ + 3.0, attach=True)

        def tt(eng, out, in0, in1, op, r, w):
            return S.add(eng, lambda h: h.tensor_tensor(out=out, in0=in0, in1=in1, op=op), r, w,
                         cost=_ecost(eng, out), attach=True)

        def tsc(eng, out, in0, s1, s2, op0, op1, r, w):
            if s2 is None:
                return S.add(eng, lambda h: h.tensor_scalar(out=out, in0=in0, scalar1=s1, scalar2=None,
                                                            op0=op0), r, w, cost=_ecost(eng, out), attach=True)
            return S.add(eng, lambda h: h.tensor_scalar(out=out, in0=in0, scalar1=s1, scalar2=s2,
                                                        op0=op0, op1=op1), r, w, cost=_ecost(eng, out), attach=True)

        def stt(eng, out, in0, scalar, in1, op0, op1, r, w):
            return S.add(eng, lambda h: h.scalar_tensor_tensor(out=out, in0=in0, scalar=scalar, in1=in1,
                                                               op0=op0, op1=op1), r, w, cost=_ecost(eng, out), attach=True)

        def cp(eng, out, in_, r, w):
            return S.add(eng, lambda h: h.tensor_copy(out=out, in_=in_), r, w, cost=_ecost(eng, out), attach=True)

        def red(eng, out, in_, r, w):
            return S.add(eng, lambda h: h.tensor_reduce(out=out, in_=in_, axis=AX.X, op=ALU.add), r, w,
                         cost=_ecost(eng, in_), attach=True)

        def recip(out, in_, r, w):
            return S.add("dve", lambda h: h.reciprocal(out=out, in_=in_), r, w, attach=True)

        def memset(eng, ap, val, w):
            return S.add(eng, lambda h: h.memset(ap, val), (), w, cost=_ecost(eng, ap))

        def bc(ap, shape, axis):
            return ap.unsqueeze(axis).to_broadcast(list(shape))

        IDB = sb(top, "IDB", [128, 128], BF16)
        IDF = sb(top, "IDF", [128, 128], F32)
        EPSB = sb(top, "EPSB", [128, 1], F32)
        ZEROB = sb(top, "ZEROB", [128, 1], F32)
        OB = sb(top, "OB", [128, 1], F32)
        RC16 = sb(top, "RC16", [128, 64], F32)
        CS = sb(top, "CS", [128, 17, 64], F32)
        GFM = sb(top, "GFM", [128, 16], F32)
        PGFM = sb(top, "PGFM", [128, 16], F32)
        PSFM = sb(top, "PSFM", [128, 8], F32)
        GKFM = sb(top, "GKFM", [128, 1], F32)
        GQ = sb(top, "GQ", [128, Q_RANK], F32)
        GKV = sb(top, "GKV", [128, KV_RANK], F32)
        GQN = sb(top, "GQN", [128, D_NOPE], F32)
        GQR = sb(top, "GQR", [128, D_ROPE], F32)
        GKR = sb(top, "GKR", [128, D_ROPE], F32)
        MIXRAW = sb(top, "MIXRAW", [128, 8 * NTOK], F32)
        MIX = MIXRAW[:, :].bitcast(BF16).rearrange("p (k t) -> p k t", k=16)
        STAT = sb(top, "STAT", [128, 18, 40], F32)

        memset("dve", IDF[:], 0.0, ["IDF"])
        S.add("pool", lambda h: h.affine_select(out=IDF[:], in_=IDF[:], pattern=[[-1, 128]],
                                                compare_op=ALU.not_equal, fill=1.0, base=0,
                                                channel_multiplier=1), ["IDF"], ["IDF"])
        cp("dve", IDB[:], IDF[:], ["IDF"], ["IDB"])
        memset("dve", EPSB[:], EPS, ["EPSB"])
        memset("dve", ZEROB[:], 0.0, ["ZEROB"])
        dma("sp", OB[:], ob[:, :], [], ["OB"])
        dma("sp", RC16[:], rc16[:, :], [], ["RC16"])
        dma("sp", CS[:], cs.rearrange("b p c -> p b c"), [], ["CS"])
        dma("sp", GFM[:], norm_g.rearrange("(k p) -> p k", p=128), [], ["GFM"], allow_slow_non_contiguous=True)
        dma("sp", PGFM[:], ple_norm_g.rearrange("(k p) -> p k", p=128), [], ["PGFM"],
            allow_slow_non_contiguous=True)
        dma("sp", PSFM[:], pool_scale.rearrange("(k p) -> p k", p=128), [], ["PSFM"],
            allow_slow_non_contiguous=True)
        dma("sp", GKFM[:], k_nope_g.rearrange("(p o) -> p o", o=1), [], ["GKFM"])
        dma("sp", GQ[:], q_norm_g.partition_broadcast(128), [], ["GQ"])
        dma("sp", GKV[:], kv_norm_g.partition_broadcast(128), [], ["GKV"])
        dma("sp", GQN[:], q_nope_g.partition_broadcast(128), [], ["GQN"])
        dma("sp", GQR[:], q_rope_g.partition_broadcast(128), [], ["GQR"])
        dma("sp", GKR[:], k_rope_g.partition_broadcast(128), [], ["GKR"])
        CONSTS = ["IDB", "IDF", "EPSB", "ZEROB", "OB", "RC16", "CS", "GFM", "PGFM", "PSFM", "GKFM", "GQ",
                  "GKV", "GQN", "GQR", "GKR"]
        S.barrier()

        def rstd_inplace(ap, T, n, key):
            act(ap, ap, AF.Ln, [key], [key], scale=1.0 / n, bias=EPSB[:T])
            act(ap, ap, AF.Exp, [key], [key], scale=-0.5)

        s1 = ExitStack()
        s1.__enter__()
        QTN = sb(s1, "QTN", [128, 8, NOWN], BF16)
        QTR = sb(s1, "QTR", [128, 8, NOWN], BF16)
        SQN = sb(s1, "SQN", [128, 4, 8, 16], BF16)
        SQR = sb(s1, "SQR", [128, 4, 8, 16], BF16)
        CKT = sb(s1, "CKT", [128, 2, 2048], BF16)
        KRT = sb(s1, "KRT", [128, 2048], BF16)
        SCKT = sb(s1, "SCKT", [128, 2, 64], BF16)
        SKRT = sb(s1, "SKRT", [128, 64], BF16)
        s2 = ExitStack()
        s2.__enter__()
        XN = sb(s2, "XN", [128, 16, XCOLS], BF16)

        def sweepA():
            with ExitStack() as es:
                WA = sb(es, "WA", [128, 16, 832], BF16)
                WUQ = sb(es, "WUQ", [128, 4, 1536], BF16)
                NXT = 4
                XT = [MIXRAW[:, 2048 * i:2048 * (i + 1)] for i in range(NXT)]
                memset("pool", QTR[64:128, :, :], 0.0, ["QTRz"])
                memset("pool", SQR[64:128, :, :, :], 0.0, ["SQRz"])
                XS = [sb(es, "XS%d" % i, [128, 2048], BF16) for i in range(2)]
                XNO = [sb(es, "XNO%d" % i, [128, 16, 128], BF16) for i in range(2)]
                JUNK = sb(es, "JUNK", [128, 512], BF16)
                CQN = sb(es, "CQN", [128, 512], BF16)
                CQT = sb(es, "CQT", [128, 4, 128], BF16)
                QRAW = sb(es, "QRAW", [128, 1536], F32)
                QSQ = sb(es, "QSQ", [128, 1536], F32)
                QF = QSQ[:, 0:1024].rearrange("p (h d) -> p h d", h=8)
                QNB = sb(es, "QNB", [128, 8, 128], BF16)
                QRF = QSQ[:, 1024:1536].rearrange("p (h d) -> p h d", h=8)
                RT = [sb(es, "RT%d" % i, [128, 8, 32], F32) for i in range(4)]
                RK_ = [sb(es, "RK_%d" % i, [128, 32], F32) for i in range(4)]
                QRB = sb(es, "QRB", [128, 8, 64], BF16)
                CKVF = [sb(es, "CKVF%d" % i, [128, 256], F32) for i in range(2)]
                CKVB = sb(es, "CKVB", [128, 256], BF16)
                KRF = sb(es, "KRF", [128, 64], F32)
                KRO = [sb(es, "KRO%d" % i, [128, 64], F32) for i in range(2)]
                KRB = sb(es, "KRB", [128, 128], BF16)
                TP = ps(es, "TP", [128, 2048], BF16)
                PA = [ps(es, "PA%d" % i, [128, 1024], F32) for i in range(2)]
                PQ = ps(es, "PQ", [128, 512], F32)
                TQ = ps(es, "TQ", [128, 1024], BF16)

                for k in range(16):
                    dma("pool", WA[:, k, :], w_in[128 * k:128 * (k + 1), 2048:2880], [], [("WA", k)])
                dma("pool", WUQ[:], w_uq.rearrange("(k p) n -> p k n", p=128), [], ["WUQ"])

                blocks = []
                blocks.append((16, xo[0:16, :], "halo", 0))
                blocks.append((64, xsm[:, :], "smp", 16 + NOWN))
                for j in range(8):
                    blocks.append((128, xo[16 + 128 * j:16 + 128 * (j + 1), :], "own", 16 + 128 * j))
                    blocks.append((128, xoth[128 * j:128 * (j + 1), :], "oth", None))
                NB = len(blocks)
                info = {}
                jo = 0
                jt = 0
                for i, (T, src, kind, xc) in enumerate(blocks):
                    d = {"T": T, "kind": kind, "xc": xc}
                    if kind == "own":
                        d["csi"] = jo
                        d["mcol"] = 128 * jo
                        d["kc"] = 1024 + 128 * jo
                        jo += 1
                    elif kind == "smp":
                        d["csi"] = 8
                        d["mcol"] = NOWN
                    elif kind == "oth":
                        d["csi"] = 9 + jt
                        d["kc"] = 128 * jt
                        jt += 1
                    info[i] = d

                def load(i):
                    T, src, kind, xc = blocks[i]
                    dma("sp", XT[i % NXT][:T], src, [], [("XT", i % NXT)])

                def stageF(i):
                    d = info[i]
                    T, kind, xc = d["T"], d["kind"], d["xc"]
                    p = i % 2
                    xt = XT[i % NXT]
                    xk = ("XT", i % NXT)
                    st = STAT[:T, i, :]
                    act(XS[p][:T], xt[:T], AF.Square, [xk], [("ST", i, 0), ("XS", p, 0), ("XS", p, 1)], accum_out=st[:, 0:1])
                    rstd_inplace(st[:, 0:1], T, D_MODEL, ("ST", i, 0))
                    for hf in range(2):
                        act(XS[p][:T, hf * 1024:(hf + 1) * 1024], xt[:T, hf * 1024:(hf + 1) * 1024], AF.Copy,
                            [xk, ("ST", i, 0)], [("XS", p, hf)], scale=st[:, 0:1])
                    for k in range(16):
                        tr(TP[:, k * 128:k * 128 + T], XS[p][:T, k * 128:(k + 1) * 128], IDB[:T, :T],
                           [("XS", p, k // 8)], [("TP", k // 8)])
                    tpv = TP[:, :].rearrange("p (k t) -> p k t", k=16)[:, :, 0:T]
                    if kind == "oth":
                        xn = XNO[d["csi"] % 2][:, :, 0:T]
                        xkey = ("XNO", d["csi"] % 2)
                    else:
                        xn = XN[:, :, xc:xc + T]
                        xkey = ("XN", i)
                    for hf in range(2):
                        tt("dve", xn[:, hf * 8:(hf + 1) * 8, :], tpv[:, hf * 8:(hf + 1) * 8, :],
                           bc(GFM[:, hf * 8:(hf + 1) * 8], [128, 8, T], 2), ALU.mult, [("TP", hf)], [(xkey, hf)])
                    if kind == "halo":
                        return
                    pa = d["pa"]
                    if kind != "oth":
                        for k in range(16):
                            mm(PA[pa][:T, 0:512], xn[:, k, :], WA[:, k, 0:512], k == 0, k == 15,
                               [(xkey, k // 8), ("WA", k)], [("PA0", pa)])
                    for k in range(16):
                        mm(PA[pa][:T, 512:832], xn[:, k, :], WA[:, k, 512:832], k == 0, k == 15,
                           [(xkey, k // 8), ("WA", k)], [("PA1", pa)])

                def stageG1(i):
                    d = info[i]
                    T, kind = d["T"], d["kind"]
                    if kind == "halo":
                        return
                    pa = d["pa"]
                    pav = PA[pa]
                    p = i % 2
                    st = STAT[:T, i, :]
                    csi = d["csi"]
                    bk1 = ("BK_PA1", pa)
                    act(JUNK[:T, 0:256], pav[:T, 512:768], AF.Square, [("PA1", pa)], [("ST", i, 4), bk1],
                        accum_out=st[:, 2:3])
                    act(JUNK[:T, 0:64], pav[:T, 768:832], AF.Square, [("PA1", pa)], [("ST", i, 5), bk1],
                        accum_out=st[:, 3:4])
                    rstd_inplace(st[:, 2:3], T, KV_RANK, ("ST", i, 4))
                    rstd_inplace(st[:, 3:4], T, D_ROPE, ("ST", i, 5))
                    if kind != "oth":
                        act(JUNK[:T, 0:512], pav[:T, 0:512], AF.Square, [("PA0", pa)], [("ST", i, 1)],
                            accum_out=st[:, 1:2])
                        rstd_inplace(st[:, 1:2], T, Q_RANK, ("ST", i, 1))
                        stt("dve", CQN[:T], pav[:T, 0:512], st[:, 1:2], GQ[:T], ALU.mult, ALU.mult,
                            [("PA0", pa), ("ST", i, 1)], ["CQN"])
                        for k in range(4):
                            tr(TQ[:, k * 128:k * 128 + T], CQN[:T, k * 128:(k + 1) * 128], IDB[:T, :T],
                               ["CQN"], ["TQ"])
                        cp("dve", CQT[:, :, 0:T], TQ[:, 0:512].rearrange("p (k t) -> p k t", k=4)[:, :, 0:T],
                           ["TQ"], ["CQT"])

                def stageG2(i):
                    d = info[i]
                    T, kind = d["T"], d["kind"]
                    if kind == "halo":
                        return
                    pa = d["pa"]
                    pav = PA[pa]
                    p = i % 2
                    st = STAT[:T, i, :]
                    csi = d["csi"]
                    bk1 = ("BK_PA1", pa)
                    if kind != "oth":
                        for n in range(3):
                            for k in range(4):
                                mm(PQ[:T, :], CQT[:, k, 0:T], WUQ[:, k, n * 512:(n + 1) * 512],
                                   k == 0, k == 3, ["CQT", "WUQ"], ["PQ"])
                            act(QRAW[:T, n * 512:(n + 1) * 512], PQ[:T, :], AF.Copy, ["PQ"], [("QRAW", n)])
                        act(QSQ[:T], QRAW[:T], AF.Square, [("QRAW", 0), ("QRAW", 1), ("QRAW", 2)], ["QSQ"])
                    ckf = CKVF[p]
                    stt("dve", ckf[:T], pav[:T, 512:768], st[:, 2:3], GKV[:T], ALU.mult, ALU.mult,
                        [("PA1", pa), ("ST", i, 4)], [("CKVF", p), bk1])
                    stt("dve", KRF[:T], pav[:T, 768:832], st[:, 3:4], GKR[:T], ALU.mult, ALU.mult,
                        [("PA1", pa), ("ST", i, 5)], ["KRF", bk1])
                    if kind == "own":
                        dma("sp", ckv_o[d["mcol"]:d["mcol"] + T, :], ckf[:T], [("CKVF", p)], [])
                    elif kind == "smp":
                        dma("sp", ckv_s[:, :], ckf[:T], [("CKVF", p)], [])
                    cp("pool", CKVB[:T], ckf[:T], [("CKVF", p)], ["CKVB"])
                    for k in range(2):
                        tr(TQ[:, k * 128:k * 128 + T], CKVB[:T, k * 128:(k + 1) * 128], IDB[:T, :T],
                           ["CKVB"], ["TQ"])
                    tqv = TQ[:, 0:256].rearrange("p (k t) -> p k t", k=2)[:, :, 0:T]
                    if kind == "smp":
                        cp("dve", SCKT[:, :, 0:T], tqv, ["TQ"], ["SCKT"])
                    else:
                        kc = d["kc"]
                        cp("dve", CKT[:, :, kc:kc + T], tqv, ["TQ"], [("CKT", i)])
                    kro = KRO[p]
                    c1 = CS[:T, csi, 0:32]
                    s1_ = CS[:T, csi, 32:64]
                    r0, r1, r2, r3 = (RK_[q][:T, :] for q in range(4))
                    tt("pool", r0, KRF[:T, 0:32], c1, ALU.mult, ["KRF"], ["RK0"])
                    tt("pool", r1, KRF[:T, 32:64], s1_, ALU.mult, ["KRF"], ["RK1"])
                    tt("pool", r2, KRF[:T, 32:64], c1, ALU.mult, ["KRF"], ["RK2"])
                    tt("pool", r3, KRF[:T, 0:32], s1_, ALU.mult, ["KRF"], ["RK3"])
                    tt("pool", kro[:T, 0:32], r0, r1, ALU.subtract, ["RK0", "RK1"], [("KRO0", p)])
                    tt("pool", kro[:T, 32:64], r2, r3, ALU.add, ["RK2", "RK3"], [("KRO1", p)])
                    if kind == "own":
                        dma("sp", kr_o[d["mcol"]:d["mcol"] + T, :], kro[:T], [("KRO0", p), ("KRO1", p)], [])
                    elif kind == "smp":
                        dma("sp", kr_s[:, :], kro[:T], [("KRO0", p), ("KRO1", p)], [])
                    cp("pool", KRB[:T, 0:64], kro[:T], [("KRO0", p), ("KRO1", p)], ["KRBa"])
                    cp("pool", KRB[:T, 64:128], kro[:T], [("KRO0", p), ("KRO1", p)], ["KRBb"])
                    tr(TQ[:, 0:T], KRB[:T, :], IDB[:T, :T], ["KRBa", "KRBb"], ["TQ"])
                    if kind == "smp":
                        cp("dve", SKRT[:, 0:T], TQ[:, 0:T], ["TQ"], ["SKRT"])
                    else:
                        cp("dve", KRT[:, kc:kc + T], TQ[:, 0:T], ["TQ"], [("KRT", i)])

                def stageH(i):
                    d = info[i]
                    T, kind = d["T"], d["kind"]
                    if kind in ("halo", "oth"):
                        return
                    st = STAT[:T, i, :]
                    csi, mcol = d["csi"], d["mcol"]
                    qsv = QSQ[:T].rearrange("p (h d) -> p h d", h=8)
                    qv = QRAW[:T].rearrange("p (h d) -> p h d", h=8)
                    qk_ = [("QRAW", 0), ("QRAW", 1), ("QRAW", 2)]
                    red("dve", st[:, 8:16], qsv[:, :, 0:128], ["QSQ"], [("ST", i, 2)])
                    red("dve", st[:, 16:24], qsv[:, :, 128:192], ["QSQ"], [("ST", i, 3)])
                    rstd_inplace(st[:, 8:16], T, D_NOPE, ("ST", i, 2))
                    rstd_inplace(st[:, 16:24], T, D_ROPE, ("ST", i, 3))
                    tt("dve", QF[:T], qv[:, :, 0:128], bc(st[:, 8:16], [T, 8, 128], 2), ALU.mult,
                       qk_ + [("ST", i, 2)], ["QF", "QSQ"])
                    tt("dve", QNB[:T], QF[:T], bc(GQN[:T], [T, 8, 128], 1), ALU.mult, ["QF", "QSQ"], ["QNB"])
                    tt("dve", QRF[:T], qv[:, :, 128:192], bc(st[:, 16:24], [T, 8, 64], 2), ALU.mult,
                       qk_ + [("ST", i, 3)], ["QRF", "QSQ"])
                    tt("dve", QRF[:T], QRF[:T], bc(GQR[:T], [T, 8, 64], 1), ALU.mult, ["QRF"], ["QRF", "QSQ"])
                    cosb = bc(CS[:T, csi, 0:32], [T, 8, 32], 1)
                    sinb = bc(CS[:T, csi, 32:64], [T, 8, 32], 1)
                    x1 = QRF[:T, :, 0:32]
                    x2 = QRF[:T, :, 32:64]
                    tt("dve", RT[0][:T], x1, cosb, ALU.mult, ["QRF", "QSQ"], ["RT0"])
                    tt("dve", RT[1][:T], x2, sinb, ALU.mult, ["QRF", "QSQ"], ["RT1"])
                    tt("dve", RT[2][:T], x2, cosb, ALU.mult, ["QRF", "QSQ"], ["RT2"])
                    tt("dve", RT[3][:T], x1, sinb, ALU.mult, ["QRF", "QSQ"], ["RT3"])
                    tt("dve", QRB[:T, :, 0:32], RT[0][:T], RT[1][:T], ALU.subtract, ["RT0", "RT1"], ["QRB0"])
                    tt("dve", QRB[:T, :, 32:64], RT[2][:T], RT[3][:T], ALU.add, ["RT2", "RT3"], ["QRB1"])
                    for h in range(8):
                        tr(TQ[:, h * 128:h * 128 + T], QNB[:T, h, :], IDB[:T, :T], ["QNB"], ["TQ"])
                    tqh = TQ[:, :].rearrange("p (h t) -> p h t", h=8)[:, :, 0:T]
                    if kind == "own":
                        cp("dve", QTN[:, :, mcol:mcol + T], tqh, ["TQ"], [("QTN", i)])
                    else:
                        cp("dve", SQN[:, :, :, :].rearrange("p b h q -> p h b q"),
                           tqh.rearrange("p h (b q) -> p h b q", b=4), ["TQ"], [("QTN", i)])
                    for h in range(8):
                        tr(TQ[0:64, h * 128:h * 128 + T], QRB[:T, h, :], IDB[:T, :T], ["QRB0", "QRB1"], ["TQ"])
                    tqr = TQ[0:64, :].rearrange("p (h t) -> p h t", h=8)[:, :, 0:T]
                    if kind == "own":
                        cp("dve", QTR[0:64, :, mcol:mcol + T], tqr, ["TQ"], [("QTR", i)])
                    else:
                        cp("dve", SQR[0:64, :, :, :].rearrange("p b h q -> p h b q"),
                           tqr.rearrange("p h (b q) -> p h b q", b=4), ["TQ"], [("QTR", i)])

                npa = 0
                for i in range(NB):
                    if info[i]["kind"] != "halo":
                        info[i]["pa"] = npa % 2
                        npa += 1
                for i in range(min(NXT - 1, NB)):
                    load(i)
                stageF(0)
                for i in range(NB):
                    if i + NXT - 1 < NB:
                        load(i + NXT - 1)
                    if i + 1 < NB:
                        stageF(i + 1)
                    stageG1(i)
                    stageG2(i)
                    stageH(i)
            S.barrier()

        def sweepB():
            with ExitStack() as es:
                WS = [sb(es, "WSB%d" % i, [128, 16, 256], BF16) for i in range(3)]
                U = [sb(es, "U%d" % i, [128, UCOLS], F32) for i in range(2)]
                T1 = sb(es, "T1", [128, UCOLS], F32)
                T2 = sb(es, "T2", [128, UCOLS], F32)
                D = [sb(es, "D%d" % i, [128, 2, NTOK], BF16) for i in range(2)]
                GP = [sb(es, "GP%d" % i, [128, 2, NTOK], BF16) for i in range(2)]
                WP = sb(es, "WP", [128, 4, 2, 256], BF16)
                UT = sb(es, "UT", [128, 8, 80], F32)
                UTT = sb(es, "UTT", [128, 1024], F32)
                SPT = sb(es, "SPT", [128, 8, 64], F32)
                SPL = sb(es, "SPL", [64, 1024], F32)
                TM16 = sb(es, "TM16", [128, 16], F32)
                PU = [ps(es, "PU%d" % i, [128, 1536], F32) for i in range(2)]
                PP = ps(es, "PP", [128, 1024], F32)

                dma("sp", SPL[:], spl[:, :], [], ["SPL"])
                dma("pool", WP[:], w_pool.rearrange("g (k p) n -> p g k n", p=128), [], ["WP"])
                for m in range(8):
                    tr(PP[:, m * 64:(m + 1) * 64], SPL[:64, m * 128:(m + 1) * 128], IDF[:64, :64], ["SPL"], ["PP"])
                cp("dve", SPT[:, :, :], PP[:, 0:512].rearrange("p (m t) -> p m t", m=8), ["PP"], ["SPT"])

                chunks = []
                for g in range(4):
                    chunks.append(("u", g, 256 * g))
                    chunks.append(("gp", g, 1024 + 256 * g))
                for k in range(4):
                    chunks.append(("gm", k, 2880 + 256 * k))

                def wload(ci):
                    kind, g, c0 = chunks[ci]
                    if ci == 0:
                        for k in range(16):
                            dma("pool", WS[0][:, k, :], w_in[128 * k:128 * (k + 1), c0:c0 + 256], [],
                                [("WSB", 0, k)])
                        return
                    dma("pool", WS[ci % 3][:], w_in[:, c0:c0 + 256].rearrange("(k p) n -> p k n", p=128),
                        [], [("WSB", ci % 3, k) for k in range(16)])

                wload(0)
                wload(1)
                nt = 0
                for ci, (kind, g, c0) in enumerate(chunks):
                    if ci + 2 < len(chunks):
                        wload(ci + 2)
                    ws = WS[ci % 3]
                    for mt in range(2):
                        pu = PU[nt % 2]
                        pkey = ("PU", nt % 2)
                        nt += 1
                        if kind == "u":
                            nch = [(0, 512), (512, 1024), (1024, XCOLS)]
                        else:
                            nch = [(16, 528), (528, 1040), (1040, XCOLS)]
                        for c, (a, b) in enumerate(nch):
                            for k in range(16):
                                mm(pu[:, c * 512:c * 512 + (b - a)], ws[:, k, mt * 128:(mt + 1) * 128],
                                   XN[:, k, a:b], k == 0, k == 15, [("WSB", ci % 3, k), "XNALL"], [pkey])
                        if kind == "u":
                            m = 2 * g + mt
                            u = U[m % 2]
                            ukey = ("U", m % 2)
                            uv = u[:, 1040:1168].rearrange("p (b t) -> p b t", b=4)
                            act(u[:, 0:1040], pu[:, 0:1040], AF.Copy, [pkey], [ukey])
                            act(uv[:, :, 16:32], pu[:, 1040:1104].rearrange("p (b t) -> p b t", b=4), AF.Copy,
                                [pkey], [ukey])
                            cp("dve", uv[:, :, 0:16], SPT[:, m, :].rearrange("p (b t) -> p b t", b=4),
                               ["SPT"], [ukey])
                            w = (2, 4, 8, 16)[g]
                            L = UCOLS
                            tt("dve", T1[:, 1:L], u[:, 1:L], u[:, 0:L - 1], ALU.add, [ukey], ["T1"])
                            sw = T1
                            swk = "T1"
                            if w >= 4:
                                tt("dve", T2[:, 3:L], T1[:, 3:L], T1[:, 1:L - 2], ALU.add, ["T1"], ["T2"])
                                sw, swk = T2, "T2"
                            if w >= 8:
                                tt("dve", T1[:, 7:L], T2[:, 7:L], T2[:, 3:L - 4], ALU.add, ["T2"], ["T1"])
                                sw, swk = T1, "T1"
                            if w >= 16:
                                tt("dve", T2[:, 15:L], T1[:, 15:L], T1[:, 7:L - 8], ALU.add, ["T1"], ["T2"])
                                sw, swk = T2, "T2"
                            d = D[g % 2]
                            dkey = ("D", g % 2, mt)
                            stt("dve", d[:, mt, 0:1024], sw[:, 16:1040], 1.0 / w, u[:, 16:1040], ALU.mult,
                                ALU.subtract, [swk, ukey], [dkey])
                            tt("dve", TM16[:, :], sw[:, 16:32], RC16[:, 16 * g:16 * g + 16], ALU.mult,
                               [swk], ["TM16"])
                            tt("dve", d[:, mt, 0:16], TM16[:, :], u[:, 16:32], ALU.subtract, ["TM16", ukey], [dkey])
                            swv = sw[:, 1040:1168].rearrange("p (b t) -> p b t", b=4)
                            stt("dve", d[:, mt, 1024:1088].rearrange("p (b t) -> p b t", b=4), swv[:, :, 16:32],
                                1.0 / w, uv[:, :, 16:32], ALU.mult, ALU.subtract, [swk, ukey], [dkey])
                            cp("dve", UT[:, m, 0:16], u[:, 1024:1040], [ukey], [("UT", m)])
                            cp("dve", UT[:, m, 16:80].rearrange("p (b t) -> p b t", b=4), uv[:, :, 16:32],
                               [ukey], [("UT", m)])
                        elif kind == "gp":
                            act(GP[g % 2][:, mt, :], pu[:, 0:NTOK], AF.Silu, [pkey], [("GP", g % 2, mt)])
                        else:
                            act(MIX[:, 8 + 2 * g + mt, :], pu[:, 0:NTOK], AF.Silu, [pkey], [("MIX", 8 + 2 * g + mt)])
                    if kind == "gp":
                        for j in range(2):
                            for (a, b, passes) in ((0, 1024, ((0, 512), (512, 1024))), (1024, NTOK, ((1024, NTOK),))):
                                for (aa, bb) in passes:
                                    for k in range(2):
                                        mm(PP[:, aa - a:bb - a], WP[:, g, k, j * 128:(j + 1) * 128],
                                           D[g % 2][:, k, aa:bb], k == 0, k == 1,
                                           ["WP", ("D", g % 2, 0), ("D", g % 2, 1)], ["PP"])
                                stt("dve", MIX[:, 2 * g + j, a:b], PP[:, 0:b - a], PSFM[:, 2 * g + j:2 * g + j + 1],
                                    GP[g % 2][:, j, a:b], ALU.mult, ALU.mult,
                                    ["PP", ("GP", g % 2, j)], [("MIX", 2 * g + j, a)])
                        if g == 3:
                            for m in range(8):
                                tr(PP[0:80, m * 128:(m + 1) * 128], UT[:, m, :], IDF[:, :], [("UT", m)], ["PP"])
                            cp("dve", UTT[0:80, :], PP[0:80, :], ["PP"], ["UTT"])
                            dma("sp", pool_o[:, :], UTT[0:16, :], ["UTT"], [])
                            dma("sp", pool_s[:, :], UTT[16:80, :], ["UTT"], [])
            S.barrier()

        def sweepC():
            AT_es = ExitStack()
            ATS = sb(AT_es, "ATS", [128, 4, 1024], BF16)
            WUKV = sb(AT_es, "WUKV", [128, 2, 2048], BF16)
            dma("pool", WUKV[:], w_ukv.rearrange("(k p) n -> p k n", p=128), [], ["WUKV"])
            with ExitStack() as es:
                AT = sb(es, "AT", [128, 8, 1024], BF16)
                KT = [sb(es, "KT%d" % i, [128, 16 * 128], BF16) for i in range(2)]
                V1 = [sb(es, "V1%d" % i, [128, 16, 132], BF16) for i in range(2)]
                KN = [sb(es, "KN%d" % i, [128, 2, 128], BF16) for i in range(2)]
                PT = [sb(es, "PT%d" % i, [128, 4, 128], BF16) for i in range(3)]
                PTD = [sb(es, "PTD%d" % i, [128, 128], BF16) for i in range(2)]
                RD = sb(es, "RD", [128, 8], F32)
                SSK = sb(es, "SSK", [128, 8], F32)
                JC = sb(es, "JC", [128, 128], BF16)
                KVP = [ps(es, "KVP%d" % i, [128, 2, 256], F32) for i in range(2)]
                TPK = ps(es, "TPK", [128, 1024], BF16)
                STP = [ps(es, "STP%d" % i, [128, 512], F32) for i in range(3)]
                OP = [ps(es, "OP%d" % i, [128, 512], F32) for i in range(2)]
                TPX = TPK

                for i in range(2):
                    memset("dve", V1[i][:, :, 128:129], 1.0, [("V1ones", i)])
                    memset("dve", PTD[i][64:128, 0:64], 0.0, [("PTDz", i)])
                cnt = {"pair": 0, "grp": 0, "pt": 0, "ptd": 0, "o": 0}

                def expand(h, kb):
                    for t in range(0, 16, 2):
                        pr = cnt["pair"] % 2
                        cnt["pair"] += 1
                        kvp = KVP[pr]
                        bk = ("BK_KVP", pr)
                        for ti in range(2):
                            for k in range(2):
                                mm(kvp[:, ti, :], CKT[:, k, (t + ti) * 128:(t + ti + 1) * 128],
                                   WUKV[:, k, h * 256:(h + 1) * 256], k == 0, k == 1, ["WUKV"], [("KVP", pr)])
                        ssk = SSK[:, 2 * pr:2 * pr + 2]
                        for ti in range(2):
                            act(JC[:, :], kvp[:, ti, 0:128], AF.Square, [("KVP", pr)], [("SSK", pr), bk],
                                accum_out=SSK[:, 2 * pr + ti:2 * pr + ti + 1])
                        rstd_inplace(ssk, 128, D_NOPE, ("SSK", pr))
                        tt("dve", KN[pr][:, :, :], kvp[:, :, 0:128], bc(ssk, [128, 2, 128], 2), ALU.mult,
                           [("KVP", pr), ("SSK", pr)], [("KN", pr), bk])
                        cp("dve", V1[kb][:, t:t + 2, 0:128], kvp[:, :, 128:256], [("KVP", pr)], [("V1", kb), bk])
                        for ti in range(2):
                            tr(TPK[:, ti * 128:(ti + 1) * 128], KN[pr][:, ti, :], IDB[:, :], [("KN", pr)], ["TPK"])
                        tsc("dve", KT[kb][:, t * 128:(t + 2) * 128], TPK[:, 0:256], GKFM[:, 0:1], None, ALU.mult, None,
                            ["TPK"], [("KT", kb)])

                def make_groups(j):
                    tiles = [(t, "oth") for t in range(8)] + [(8 + t, "full") for t in range(j)] + [(8 + j, "diag")]
                    groups = []
                    cur = []
                    for tl in tiles:
                        t, kind = tl
                        if kind == "diag":
                            if cur:
                                groups.append(cur)
                                cur = []
                            groups.append([tl])
                        else:
                            if cur and (cur[0][1] != kind or len(cur) == 4):
                                groups.append(cur)
                                cur = []
                            cur.append(tl)
                    if cur:
                        groups.append(cur)
                    return groups

                def finalize_block(j):
                    for m in range(8):
                        tr(TPX[:, m * 128:(m + 1) * 128], AT[:, j, m * 128:(m + 1) * 128], IDB[:, :],
                           [("AT", j, m)], ["TPK"])
                    mv = MIX[:, 8:16, 128 * j:128 * (j + 1)]
                    tt("dve", mv, TPX[:, :].rearrange("p (m t) -> p m t", m=8), mv, ALU.mult,
                       ["TPK"], [("MIXF", j)])

                def run_head(h, kb, mid_hook):
                    items = []
                    for j in range(8):
                        gs = make_groups(j)
                        for gi_, g in enumerate(gs):
                            items.append((j, g, gi_ == 0, gi_ == len(gs) - 1))
                    stbuf = {}

                    def qk(idx):
                        j, grp, first, last = items[idx]
                        gi = cnt["grp"] % 3
                        cnt["grp"] += 1
                        stbuf[idx] = gi
                        st = STP[gi]
                        qn_ap = QTN[:, h, 128 * j:128 * (j + 1)]
                        qr_ap = QTR[:, h, 128 * j:128 * (j + 1)]
                        for i, (t, kind) in enumerate(grp):
                            mm(st[:, i * 128:(i + 1) * 128], KT[kb][:, t * 128:(t + 1) * 128], qn_ap, True, False,
                               [("KT", kb)], [("STP", gi)])
                            mm(st[:, i * 128:(i + 1) * 128], KRT[:, t * 128:(t + 1) * 128], qr_ap, False, True,
                               [], [("STP", gi)])

                    state = {"o": None, "okp": None}

                    def exp_pv(idx):
                        j, grp, first, last = items[idx]
                        gi = stbuf.pop(idx)
                        st = STP[gi]
                        if first:
                            state["o"] = OP[cnt["o"] % 2]
                            state["okp"] = ("OP", cnt["o"] % 2)
                            cnt["o"] += 1
                        o, okp = state["o"], state["okp"]
                        kind = grp[0][1]
                        if kind == "diag":
                            di = cnt["ptd"] % 2
                            cnt["ptd"] += 1
                            ptile = PTD[di]
                            pkey = ("PTD", di)
                            act(ptile[0:64, 0:128], st[0:64, 0:128], AF.Exp, [("STP", gi)], [pkey], scale=ATTN_SCALE)
                            act(ptile[64:128, 64:128], st[64:128, 64:128], AF.Exp, [("STP", gi)], [pkey],
                                scale=ATTN_SCALE)
                            lhs = [ptile[:, 0:128]]
                            extra = [("PTDz", di)]
                        else:
                            pi = cnt["pt"] % 3
                            cnt["pt"] += 1
                            ptile = PT[pi]
                            pkey = ("PT", pi)
                            ng = len(grp)
                            bias = OB[:, 0:1] if kind == "oth" else ZEROB[:, 0:1]
                            act(ptile[:, 0:ng, :], st[:, 0:ng * 128].rearrange("p (i t) -> p i t", i=ng), AF.Exp,
                                [("STP", gi)], [pkey], scale=ATTN_SCALE, bias=bias)
                            lhs = [ptile[:, i, :] for i in range(ng)]
                            extra = []
                        for i, (t, kind) in enumerate(grp):
                            mm(o[:, 0:129], lhs[i], V1[kb][:, t, 0:129], first and i == 0,
                               last and i == len(grp) - 1,
                               [pkey, ("V1", kb), ("V1ones", kb)] + extra, [okp])
                        if last:
                            rc = cnt["o"] % 8
                            recip(RD[:, rc:rc + 1], o[:, 128:129], [okp], [("RD", rc)])
                            tsc("dve", AT[:, j, h * 128:(h + 1) * 128], o[:, 0:128], RD[:, rc:rc + 1], None, ALU.mult,
                                None, [okp, ("RD", rc)], [("AT", j, h)])
                            if h == 7:
                                finalize_block(j)

                    N = len(items)
                    LA = 2
                    for idx in range(min(LA, N)):
                        qk(idx)
                    for idx in range(N):
                        if idx + LA < N:
                            qk(idx + LA)
                        exp_pv(idx)
                        if idx == N // 2:
                            mid_hook()

                expand(0, 0)
                for h in range(8):
                    run_head(h, h % 2, (lambda hn=h + 1: expand(hn, hn % 2)) if h + 1 < 8 else (lambda: None))
            S.barrier()

            with ExitStack() as es:
                WKT = sb(es, "WKT", [128, 8, 256], BF16)

                CKB = [sb(es, "CKB%d" % i, [128, 17, 264], BF16) for i in range(3)]
                SCT = [sb(es, "SCT%d" % i, [128, 2, 17 * 128], BF16) for i in range(2)]
                KRB2 = [sb(es, "KRB2%d" % i, [128, 16, 128], BF16) for i in range(3)]
                SKT = [sb(es, "SKT%d" % i, [128, 17 * 128], BF16) for i in range(2)]
                QA = [sb(es, "QA%d" % i, [128, 2, 128], BF16) for i in range(2)]
                SQ = [sb(es, "SQ%d" % i, [128, 1024], F32) for i in range(2)]
                SSK = sb(es, "SSKS", [128, 17, 8], F32)
                TMP = [sb(es, "TMPS%d" % i, [128, 128], F32) for i in range(3)]
                TMP2 = [sb(es, "TMPT%d" % i, [128, 128], F32) for i in range(3)]
                PTS = [sb(es, "PTS%d" % i, [128, 128], BF16) for i in range(4)]
                RD = sb(es, "RDS", [128, 4], F32)
                OLB = sb(es, "OLB", [128, 256], BF16)
                OLT = sb(es, "OLT", [128, 2, 128], BF16)
                RK = [ps(es, "RK%d" % i, [128, 1024], F32) for i in range(2)]
                SSP = [ps(es, "SSP%d" % i, [128, 512], F32) for i in range(2)]
                OL = ps(es, "OL", [128, 512], F32)
                TPX = ps(es, "TPXS", [128, 1024], BF16)
                RKK = [["RK0a", "RK0b"], ["RK1a", "RK1b"]]
                SCB = [(SSP[0], "SSP0"), (SSP[1], "SSP1"), (RK[0][:, 0:512], "RK0a"), (RK[0][:, 512:1024], "RK0b"),
                       (RK[1][:, 0:512], "RK1a")]

                for i in range(3):
                    memset("dve", CKB[i][:, :, 256:257], 1.0, [("CKBones", i)])
                for r in range(2):
                    for hh in range(4):
                        for k in range(2):
                            tr(TPX[:, (hh * 2 + k) * 128:(hh * 2 + k + 1) * 128],
                               WUKV[:, k, (4 * r + hh) * 256:(4 * r + hh) * 256 + 128], IDB[:, :], ["WUKV"], ["TPX"])
                    tsc("dve", WKT[:, 4 * r:4 * r + 4, :].rearrange("p h c -> p (h c)"), TPX[:, :], GKFM[:, 0:1], None,
                        ALU.mult, None, ["TPX"], ["WKT"])
                wk_all = WUKV[:, :, :].rearrange("p k (h x) -> p k h x", x=256)

                def load(b):
                    q = b % 2
                    dma("pool", CKB[b % 3][:, 0:16, 0:256], cck[b].rearrange("(t p) c -> p t c", p=128), [],
                        [("CKB", b % 3)])
                    dma("pool", KRB2[b % 3][:, :, 0:64], ckr[b].rearrange("(t p) c -> p t c", p=128), [],
                        [("KRB2a", b % 3)])
                    dma("pool", KRB2[b % 3][:, :, 64:128], ckr[b].rearrange("(t p) c -> p t c", p=128), [],
                        [("KRB2b", b % 3)])

                def prologue_steps(b):
                    q = b % 2
                    ckb, sct, skt = CKB[b % 3], SCT[q], SKT[q]
                    steps = []

                    def sct_group(t0):
                        for ti in range(4):
                            for k in range(2):
                                tr(TPX[:, (ti * 2 + k) * 128:(ti * 2 + k + 1) * 128],
                                   ckb[:, t0 + ti, k * 128:(k + 1) * 128], IDB[:, :], [("CKB", b % 3)], ["TPX"])
                        tv = TPX[:, :].rearrange("p (t k c) -> p t k c", t=4, k=2)
                        for k in range(2):
                            cp("dve", sct[:, k, t0 * 128:(t0 + 4) * 128].rearrange("p (t c) -> p t c", t=4),
                               tv[:, :, k, :], ["TPX"], [("SCT", q)])

                    for t0 in range(0, 16, 4):
                        steps.append(lambda t0=t0: sct_group(t0))

                    def new_keys():
                        cp("dve", sct[:, :, 2048:2064], SCKT[:, :, 16 * b:16 * b + 16], [], [("SCT", q)])
                        for k in range(2):
                            tr(TPX[0:16, k * 128:(k + 1) * 128], SCKT[:, k, 16 * b:16 * b + 16], IDB[:, :], [],
                               ["TPX"])
                        cp("dve", ckb[0:16, 16, 0:256], TPX[0:16, 0:256], ["TPX"], [("CKB", b % 3)])

                    steps.append(new_keys)

                    def skt_group(t0):
                        for ti in range(8):
                            tr(TPX[:, ti * 128:(ti + 1) * 128], KRB2[b % 3][:, t0 + ti, :], IDB[:, :],
                               [("KRB2a", b % 3), ("KRB2b", b % 3)], ["TPX"])
                        cp("dve", skt[:, t0 * 128:(t0 + 8) * 128], TPX[:, :], ["TPX"], [("SKT", q)])

                    for t0 in range(0, 16, 8):
                        steps.append(lambda t0=t0: skt_group(t0))

                    def qa_step():
                        cp("dve", skt[:, 2048:2064], SKRT[:, 16 * b:16 * b + 16], [], [("SKT", q)])
                        for h in range(8):
                            for k in range(2):
                                mm(SSP[0][:, k * 128 + h * 16:k * 128 + h * 16 + 16],
                                   WKT[:, h, k * 128:(k + 1) * 128], SQN[:, b, h, :], True, True, ["WKT"], ["SSP0"])
                        cp("dve", QA[q][:, :, :], SSP[0][:, 0:256].rearrange("p (k x) -> p k x", k=2), ["SSP0"],
                           [("QA", q)])

                    steps.append(qa_step)
                    return steps

                def prologue(b):
                    for st_ in prologue_steps(b):
                        st_()

                def phase1(b, steps=()):
                    q = b % 2
                    sct = SCT[q]
                    steps = list(steps)

                    def rawk(t):
                        n = 128 if t < 16 else 16
                        rb = t % 2
                        for half in range(2):
                            for k in range(2):
                                mm(RK[rb][:n, half * 512:(half + 1) * 512], sct[:, k, t * 128:t * 128 + n],
                                   wk_all[:, k, 4 * half:4 * half + 4, 0:128], k == 0, k == 1,
                                   [("SCT", q), "WUKV"], RKK[rb])

                    rawk(0)
                    for t in range(17):
                        n = 128 if t < 16 else 16
                        rb = t % 2
                        if t + 1 < 17:
                            rawk(t + 1)
                        act(SQ[rb][:n], RK[rb][:n, :], AF.Square, RKK[rb], [("SQ", rb)])
                        red("dve", SSK[:n, t, :], SQ[rb][:n].rearrange("p (h d) -> p h d", h=8), [("SQ", rb)],
                            ["SSKS"])
                        if steps and t % 2 == 0:
                            steps.pop(0)()
                    while steps:
                        steps.pop(0)()
                    rstd_inplace(SSK[:, 0:16, :], 128, D_NOPE, "SSKS")
                    rstd_inplace(SSK[:16, 16, :], 16, D_NOPE, "SSKS")

                def phase2(b):
                    q = b % 2
                    sct, skt, ckb = SCT[q], SKT[q], CKB[b % 3]
                    qr_all = SQR[:, b, :, :].rearrange("p h q -> p (h q)")
                    NB_ = len(SCB)

                    def score(t):
                        n = 128 if t < 16 else 16
                        buf, key = SCB[t % NB_]
                        for k in range(2):
                            mm(buf[:n, 0:128], sct[:, k, t * 128:t * 128 + n], QA[q][:, k, :], k == 0, k == 1,
                               [("SCT", q), ("QA", q)], [key])
                        mm(buf[:n, 128:256], skt[:, t * 128:t * 128 + n], qr_all, True, True, [("SKT", q)], [key])

                    def soft(t):
                        n = 128 if t < 16 else 16
                        buf, key = SCB[t % NB_]
                        r3 = t % 3
                        pi = t % 4
                        tt("dve", TMP[r3][:n].rearrange("p (h q) -> p h q", h=8),
                           buf[:n, 0:128].rearrange("p (h q) -> p h q", h=8),
                           bc(SSK[:n, t, :], [n, 8, 16], 2), ALU.mult, [key, "SSKS"], [("TMPS", r3)])
                        tt("dve", TMP2[r3][:n], buf[:n, 128:256], TMP[r3][:n], ALU.add, [key, ("TMPS", r3)],
                           [("TMPT", r3)])
                        act(PTS[pi][:n], TMP2[r3][:n], AF.Exp, [("TMPT", r3)], [("PTS", pi)], scale=ATTN_SCALE)

                    def pv(t):
                        n = 128 if t < 16 else 16
                        pi = t % 4
                        mm(OL[:, 0:257], PTS[pi][:n, :], ckb[:n, t, 0:257], t == 0, t == 16,
                           [("PTS", pi), ("CKB", b % 3), ("CKBones", b % 3)], ["OL"])

                    LA = 3
                    for t in range(LA):
                        score(t)
                    for t in range(17):
                        soft(t)
                        if t + LA < 17:
                            score(t + LA)
                        pv(t)

                def epilogue(b):
                    recip(RD[:, b:b + 1], OL[:, 256:257], ["OL"], [("RDS", b)])
                    tsc("dve", OLB[:, :], OL[:, 0:256], RD[:, b:b + 1], None, ALU.mult, None, ["OL", ("RDS", b)], ["OLB"])
                    for k in range(2):
                        tr(TPX[:, k * 128:(k + 1) * 128], OLB[:, k * 128:(k + 1) * 128], IDB[:, :], ["OLB"], ["TPX"])
                    cp("dve", OLT[:, :, :], TPX[:, 0:256].rearrange("p (k x) -> p k x", k=2), ["TPX"], ["OLT"])
                    for h in range(8):
                        for k in range(2):
                            mm(RK[0][:16, h * 128:(h + 1) * 128], OLT[:, k, h * 16:(h + 1) * 16],
                               wk_all[:, k, h, 128:256], k == 0, k == 1, ["OLT", "WUKV"], RKK[0])
                    cp("dve", ATS[:16, b, :], RK[0][:16, :], RKK[0], [("ATS", b)])
                    for m in range(8):
                        tr(TPX[:, m * 128:m * 128 + 16], ATS[:16, b, m * 128:(m + 1) * 128], IDB[:16, :16],
                           [("ATS", b)], ["TPX"])
                    mvb = MIX[:, 8:16, NOWN + 16 * b:NOWN + 16 * b + 16]
                    tt("dve", mvb, TPX[:, :].rearrange("p (m t) -> p m t", m=8)[:, :, 0:16], mvb, ALU.mult,
                       ["TPX"], [("MIXF", 8, b)])

                load(0)
                prologue(0)
                load(1)
                for b in range(4):
                    if b + 2 < 4:
                        load(b + 2)
                    S.tag = ("phase1", b)
                    phase1(b, prologue_steps(b + 1) if b + 1 < 4 else ())
                    S.tag = ("phase2", b)
                    phase2(b)
                    S.tag = ("epilogue", b)
                    epilogue(b)
                    S.tag = None
            AT_es.close()
            S.barrier()

        def sweepDE():
            with ExitStack() as es:
                H = sb(es, "H", [128, 9, 2048], F32)
                WS = [sb(es, "WSD%d" % i, [128, 16, 512], BF16) for i in range(2)]
                blocks = [(128, 128 * j) for j in range(8)] + [(64, NOWN)]
                HN = MIX
                XR = [sb(es, "XR%d" % i, [128, 512], F32) for i in range(3)]
                HS = [sb(es, "HS%d" % i, [128, 2048], BF16) for i in range(2)]
                WPLE = sb(es, "WPLE", [128, 2, 2048], BF16)
                PB32 = [sb(es, "PB32%d" % i, [128, 256], F32) for i in range(2)]
                PBB = [sb(es, "PBB%d" % i, [128, 256], BF16) for i in range(2)]
                PTT = sb(es, "PTT", [128, 2, NTOK], BF16)
                BB = sb(es, "BB", [128, 2048], F32)
                GT = [sb(es, "GT%d" % i, [128, 512], F32) for i in range(2)]
                YT = [sb(es, "YT%d" % i, [128, 512], F32) for i in range(2)]
                YO = [sb(es, "YO%d" % i, [128, 512], F32) for i in range(2)]
                SE = sb(es, "SE", [128, 16], F32)
                PH = [ps(es, "PH%d" % i, [128, 512], F32) for i in range(4)]
                TPE = ps(es, "TPE", [128, 2048], BF16)
                TP2 = ps(es, "TP2", [128, 1024], BF16)

                def xsrc(bi, c):
                    if bi < 8:
                        return xo[16 + 128 * bi:16 + 128 * (bi + 1), c * 512:(c + 1) * 512]
                    return xsm[:, c * 512:(c + 1) * 512]

                def prep_gate_inputs(bi):
                    T, col = blocks[bi]
                    p = bi % 2
                    hv = H[:T, bi, :]
                    hk = [("H", bi, c) for c in range(4)]
                    act(HS[p][:T], hv, AF.Square, hk, [("SE", bi), ("HS", p)], accum_out=SE[:T, bi:bi + 1])
                    rstd_inplace(SE[:T, bi:bi + 1], T, D_MODEL, ("SE", bi))
                    act(HS[p][:T], hv, AF.Copy, hk + [("SE", bi)], [("HS", p)], scale=SE[:T, bi:bi + 1])
                    for k in range(16):
                        tr(TPE[:, k * 128:k * 128 + T], HS[p][:T, k * 128:(k + 1) * 128], IDB[:T, :T],
                           [("HS", p)], ["TPE"])
                    tt("dve", HN[:, :, col:col + T], TPE[:, :].rearrange("p (k t) -> p k t", k=16)[:, :, 0:T],
                       bc(PGFM[:, :], [128, 16, T], 2), ALU.mult, ["TPE"], [("MIXB", bi)])
                    src = pp[128 * bi:128 * (bi + 1), :] if bi < 8 else psm[:, :]
                    dma("sp", PB32[p][:T], src, [], [("PB32", p)])
                    cp("pool", PBB[p][:T], PB32[p][:T], [("PB32", p)], [("PBB", p)])
                    for k in range(2):
                        tr(TP2[:, k * 128:k * 128 + T], PBB[p][:T, k * 128:(k + 1) * 128], IDB[:T, :T],
                           [("PBB", p)], ["TP2"])
                    cp("dve", PTT[:, :, col:col + T], TP2[:, 0:256].rearrange("p (k t) -> p k t", k=2)[:, :, 0:T],
                       ["TP2"], [("PTT", bi)])

                dma("sp", BB[:], b_gate.partition_broadcast(128), [], ["BB"])
                for k in range(16):
                    dma("pool", WS[0][:, k, :], w_out[128 * k:128 * (k + 1), 0:512], [], [("WSD", 0, k)])
                it = 0
                for c in range(4):
                    if c + 1 < 4:
                        dma("pool", WS[(c + 1) % 2][:],
                            w_out[:, (c + 1) * 512:(c + 2) * 512].rearrange("(k p) n -> p k n", p=128),
                            [], [("WSD", (c + 1) % 2, k) for k in range(16)])
                    else:
                        dma("pool", WPLE[:], w_ple.rearrange("(k p) n -> p k n", p=128), [], ["WPLE"])
                    for bi, (T, col) in enumerate(blocks):
                        xr = XR[it % 3]
                        ph = PH[it % 4]
                        dma("sp", xr[:T], xsrc(bi, c), [], [("XR", it % 3)])
                        for k in range(16):
                            mm(ph[:T, :], MIX[:, k, col:col + T], WS[c % 2][:, k, :], k == 0, k == 15,
                               [("MIXB", bi), ("WSD", c % 2, k)], [("PH", it % 4)])
                        tt("dve", H[:T, bi, c * 512:(c + 1) * 512], ph[:T, :], xr[:T], ALU.add,
                           [("PH", it % 4), ("XR", it % 3)], [("H", bi, c)])
                        it += 1
                        if c == 3 and bi >= 1:
                            prep_gate_inputs(bi - 1)
                prep_gate_inputs(8)

                PG = [PH[0], PH[1]]
                PW = [PH[2], PH[3]]
                it = 0

                def gload(c):
                    dma("pool", WS[c % 2][:],
                        w_gate[:, c * 512:(c + 1) * 512].rearrange("(k p) n -> p k n", p=128),
                        [], [("WSD", c % 2, k) for k in range(16)])

                gload(0)
                for c in range(4):
                    if c + 1 < 4:
                        gload(c + 1)
                    cs_ = slice(c * 512, (c + 1) * 512)
                    for bi, (T, col) in enumerate(blocks):
                        q = it % 2
                        for k in range(16):
                            mm(PG[q][:T, :], HN[:, k, col:col + T], WS[c % 2][:, k, :], k == 0, k == 15,
                               [("MIXB", bi), ("WSD", c % 2, k)], [("PH", q)])
                        for k in range(2):
                            mm(PW[q][:T, :], PTT[:, k, col:col + T], WPLE[:, k, cs_], k == 0, k == 1,
                               [("PTT", bi), "WPLE"], [("PH", 2 + q)])
                        tt("dve", GT[q][:T], PG[q][:T, :], BB[:T, cs_], ALU.add, [("PH", q), "BB"], [("GT", q)])
                        act(GT[q][:T], GT[q][:T], AF.Sigmoid, [("GT", q)], [("GT", q)])
                        tt("dve", YT[q][:T], PW[q][:T, :], GT[q][:T], ALU.mult, [("PH", 2 + q), ("GT", q)], [("YT", q)])
                        tt("pool", YO[q][:T], YT[q][:T], H[:T, bi, cs_], ALU.add, [("YT", q), ("H", bi, c)],
                           [("YO", q)])
                        dst = y_o[128 * bi:128 * (bi + 1), cs_] if bi < 8 else y_s[:, cs_]
                        dma("sp", dst, YO[q][:T], [("YO", q)], [])
                        it += 1
            S.barrier()

        sweepA()
        if stage >= 2:
            sweepB()
        s2.__exit__(None, None, None)
        if stage >= 3:
            sweepC()
        s1.__exit__(None, None, None)
        if stage >= 4:
            sweepDE()
        S.emit(block)
    return nc


_PROGRAM = {}


def _get_program(stage=99):
    if stage not in _PROGRAM:
        _PROGRAM[stage] = build_program(stage)
    return _PROGRAM[stage]


def _rope_tab(pos):
    inv = (10000.0 ** (-(np.arange(0, D_ROPE, 2, dtype=np.float64) / D_ROPE))).astype(np.float32)
    ang = (pos.astype(np.float32)[:, None] * inv[None, :]).astype(np.float32).astype(np.float64)
    return np.concatenate([np.cos(ang), np.sin(ang)], axis=1).astype(np.float32)


def make_in_maps(inp):
    f = lambda a: np.ascontiguousarray(np.asarray(a, dtype=np.float32))
    xp = f(inp["x_prompt"])
    xs = f(inp["x_sample"])
    shared = {
        "w_in": f(inp["w_in"][0]), "w_uq": f(inp["w_uq"][0]), "w_ukv": f(inp["w_ukv"][0]),
        "w_pool": f(inp["w_pool"][0]), "w_out": f(inp["w_out"][0]), "w_gate": f(inp["w_ple_gate"][0]),
        "w_ple": f(inp["w_ple"][0]), "norm_g": f(inp["norm_g"][0]), "q_norm_g": f(inp["q_norm_g"][0]),
        "kv_norm_g": f(inp["kv_norm_g"][0]), "q_nope_g": f(inp["q_nope_g"][0]),
        "q_rope_g": f(inp["q_rope_g"][0]), "k_nope_g": f(inp["k_nope_g"][0]),
        "k_rope_g": f(inp["k_rope_g"][0]), "pool_scale": f(inp["pool_scale"][0]),
        "ple_norm_g": f(inp["ple_norm_g"][0]), "b_gate": f(inp["b_ple_gate"][0]),
    }
    maps = []
    for c in range(8):
        b, h = c // 2, c % 2
        m = dict(shared)
        x_own = np.zeros((16 + NOWN, D_MODEL), np.float32)
        x_own[16:] = xp[b, h * NOWN:(h + 1) * NOWN]
        if h == 1:
            x_own[:16] = xp[b, NOWN - 16:NOWN]
        m["xo"] = x_own
        m["xoth"] = f(xp[b, 0:NOWN])
        m["xsm"] = f(xs[4 * c:4 * c + 4].reshape(NSMP, D_MODEL))
        m["cck"] = f(inp["cache_ckv"][0, 4 * c:4 * c + 4])
        m["ckr"] = f(inp["cache_krope"][0, 4 * c:4 * c + 4])
        sp_ = np.zeros((4, 16, D_POOL), np.float32)
        sp_[:, 1:] = inp["state_pool"][0, 4 * c:4 * c + 4]
        m["spl"] = sp_.reshape(64, D_POOL)
        m["pp"] = f(inp["p_prompt"][0, b, h * NOWN:(h + 1) * NOWN])
        m["psm"] = f(inp["p_sample"][0, 4 * c:4 * c + 4].reshape(NSMP, D_PLE))
        cs = np.zeros((17, 128, 64), np.float32)
        own_pos = h * NOWN + np.arange(NOWN)
        cs[0:8] = _rope_tab(own_pos).reshape(8, 128, 64)
        cs[8, 0:64] = _rope_tab(2048 + np.tile(np.arange(16), 4))
        cs[9:17] = _rope_tab(np.arange(NOWN)).reshape(8, 128, 64)
        m["cs"] = cs
        m["ob"] = np.full((128, 1), 0.0 if h == 1 else NEG, np.float32)
        rc = np.zeros((4, 16), np.float32)
        for g, w in enumerate((2, 4, 8, 16)):
            if h == 0:
                rc[g] = 1.0 / np.minimum(np.arange(16) + 1, w)
            else:
                rc[g] = 1.0 / w
        m["rc16"] = np.ascontiguousarray(np.broadcast_to(rc.reshape(1, 64), (128, 64)))
        maps.append(m)
    return maps


def assemble(results):
    yp = np.zeros((4, 2048, D_MODEL), np.float32)
    ys = np.zeros((32, 16, D_MODEL), np.float32)
    ckvp = np.zeros((1, 4, 2048, KV_RANK), np.float32)
    krp = np.zeros((1, 4, 2048, D_ROPE), np.float32)
    poolp = np.zeros((1, 4, 15, D_POOL), np.float32)
    ckvs = np.zeros((1, 32, 16, KV_RANK), np.float32)
    krs = np.zeros((1, 32, 16, D_ROPE), np.float32)
    pools = np.zeros((1, 32, 15, D_POOL), np.float32)
    for c in range(8):
        r = results[c]
        b, h = c // 2, c % 2
        sl = slice(h * NOWN, (h + 1) * NOWN)
        yp[b, sl] = r["y_o"]
        ys[4 * c:4 * c + 4] = r["y_s"].reshape(4, 16, D_MODEL)
        ckvp[0, b, sl] = r["ckv_o"]
        krp[0, b, sl] = r["kr_o"]
        if h == 1:
            poolp[0, b] = r["pool_o"][1:16]
        ckvs[0, 4 * c:4 * c + 4] = r["ckv_s"].reshape(4, 16, KV_RANK)
        krs[0, 4 * c:4 * c + 4] = r["kr_s"].reshape(4, 16, D_ROPE)
        pools[0, 4 * c:4 * c + 4] = r["pool_s"].reshape(4, 16, D_POOL)[:, 1:16]
    return (yp, ys, ckvp, krp, poolp, ckvs, krs, pools)


STAGE = 99
import os
DBG = int(os.environ.get('KDBG', '9'))


def kernel(**inputs):
    nc = _get_program(STAGE)
    maps = make_in_maps(inputs)
    res = run_bass_kernel_spmd(nc, maps, core_ids=list(range(8)))
    return assemble(res.results)
```
